# Optimizing a Trainium2 kernel written in Bass

```python
import math
import jax, jax.numpy as jnp
from jax import lax
import numpy as np

D_MODEL = 4096
BATCH = 8
SEQ = 2048
DEPTH = 2
DEC_BATCH = 8
DEC_SEQ = 16
PAST_LEN = 2048

CHUNK = 64
MIX_W = D_MODEL // 4
HEAD_DIM = 128
N_HEADS_B = MIX_W // HEAD_DIM
N_HEADS_C = MIX_W // HEAD_DIM
SSM_GROUP = 16
N_GROUPS = MIX_W // SSM_GROUP
SSM_STATE = 64
BAND_CHUNKS = 8
BAND_LEN = BAND_CHUNKS * CHUNK
MAX_REL = 128
SB_BLOCK = 128
N_BRANCH = 3
N_IN = 10 * MIX_W + N_BRANCH * D_MODEL
RMS_EPS = 1e-6
NEG_INF = -1e30

kernel_name = "hybrid_streaming_encoder_step"


def rmsnorm(x, g):
    x32 = x.astype(jnp.float32)
    r = lax.rsqrt(jnp.mean(x32 * x32, axis=-1, keepdims=True) + RMS_EPS)
    return (x32 * r * g.astype(jnp.float32)).astype(x.dtype)


def _cmul_combine(e1, e2):
    a1r, a1i, b1r, b1i = e1
    a2r, a2i, b2r, b2i = e2
    ar = a2r * a1r - a2i * a1i
    ai = a2r * a1i + a2i * a1r
    br = a2r * b1r - a2i * b1i + b2r
    bi = a2r * b1i + a2i * b1r + b2i
    return (ar, ai, br, bi)


def s5_scan(u, h0_re, h0_im, a_re, a_im, log_dt, b_re, b_im, c_re, c_im, d_skip):
    f32 = jnp.float32
    bsz, s, _ = u.shape
    u32 = u.astype(f32)
    ug = u32.reshape(bsz, s, N_GROUPS, SSM_GROUP)
    a_re = a_re.astype(f32)
    a_im = a_im.astype(f32)
    dt = jnp.exp(log_dt.astype(f32))[:, None]
    mag = jnp.exp(a_re * dt)
    ab_re = mag * jnp.cos(a_im * dt)
    ab_im = mag * jnp.sin(a_im * dt)
    den = a_re * a_re + a_im * a_im
    nr = ab_re - 1.0
    k_re = (nr * a_re + ab_im * a_im) / den
    k_im = (ab_im * a_re - nr * a_im) / den
    b_re = b_re.astype(f32)
    b_im = b_im.astype(f32)
    bb_re = k_re[..., None] * b_re - k_im[..., None] * b_im
    bb_im = k_re[..., None] * b_im + k_im[..., None] * b_re
    bu_re = jnp.einsum("bsgp,gnp->bsgn", ug, bb_re)
    bu_im = jnp.einsum("bsgp,gnp->bsgn", ug, bb_im)
    h0_re = h0_re.astype(f32)
    h0_im = h0_im.astype(f32)
    bu_re = bu_re.at[:, 0].add(ab_re * h0_re - ab_im * h0_im)
    bu_im = bu_im.at[:, 0].add(ab_re * h0_im + ab_im * h0_re)
    ar = jnp.broadcast_to(ab_re, bu_re.shape)
    ai = jnp.broadcast_to(ab_im, bu_re.shape)
    _, _, h_re, h_im = lax.associative_scan(_cmul_combine, (ar, ai, bu_re, bu_im), axis=1)
    y = (jnp.einsum("bsgn,gpn->bsgp", h_re, c_re.astype(f32))
         - jnp.einsum("bsgn,gpn->bsgp", h_im, c_im.astype(f32)))
    y = y.reshape(bsz, s, MIX_W) + d_skip.astype(f32) * u32
    return y, h_re[:, -1], h_im[:, -1]


def rel_bias_lookup(rel_bias, qpos, kpos):
    rel = jnp.clip(qpos[:, None] - kpos[None, :], -MAX_REL, MAX_REL) + MAX_REL
    return rel_bias[:, rel].astype(jnp.float32)


def band_attention_prompt(q, k, v, rel_bias):
    bsz, s, h, dh = q.shape
    nc = s // CHUNK
    band = (BAND_CHUNKS + 1) * CHUNK
    qc = q.reshape(bsz, nc, CHUNK, h, dh)
    kc = k.reshape(bsz, nc, CHUNK, h, dh)
    vc = v.reshape(bsz, nc, CHUNK, h, dh)
    cidx = jnp.arange(nc)[:, None] + jnp.arange(BAND_CHUNKS + 1)[None, :] - BAND_CHUNKS
    valid = jnp.repeat(cidx >= 0, CHUNK, axis=1)
    cidx = jnp.maximum(cidx, 0)
    kb = kc[:, cidx].reshape(bsz, nc, band, h, dh)
    vb = vc[:, cidx].reshape(bsz, nc, band, h, dh)
    bias = rel_bias_lookup(rel_bias, jnp.arange(CHUNK) + BAND_CHUNKS * CHUNK, jnp.arange(band))
    sc = jnp.einsum("bcqhd,bckhd->bchqk", qc, kb).astype(jnp.float32) * (1.0 / math.sqrt(dh)) + bias
    sc = jnp.where(valid[None, :, None, None, :], sc, NEG_INF)
    p = jax.nn.softmax(sc, axis=-1)
    o = jnp.einsum("bchqk,bckhd->bcqhd", p.astype(vb.dtype), vb)
    return o.reshape(bsz, s, h * dh)


def band_attention_sample(q, k, v, cache_k, cache_v, rel_bias):
    bsz, n, h, dh = q.shape
    lc = cache_k.shape[1]
    kk = jnp.concatenate([cache_k.astype(k.dtype), k], axis=1)
    vv = jnp.concatenate([cache_v.astype(v.dtype), v], axis=1)
    bias = rel_bias_lookup(rel_bias, jnp.arange(n) + lc, jnp.arange(lc + n))
    sc = jnp.einsum("bqhd,bkhd->bhqk", q, kk).astype(jnp.float32) * (1.0 / math.sqrt(dh)) + bias
    p = jax.nn.softmax(sc, axis=-1)
    o = jnp.einsum("bhqk,bkhd->bqhd", p.astype(vv.dtype), vv)
    return o.reshape(bsz, n, h * dh)


def stick_breaking_block(q, qpos, k, v, kpos):
    dh = q.shape[-1]
    z = jnp.einsum("bqhd,bkhd->bhqk", q, k).astype(jnp.float32) * (1.0 / math.sqrt(dh))
    causal = kpos[None, :] < qpos[:, None]
    log_beta = jax.nn.log_sigmoid(z)
    log_fail = jnp.where(causal, jax.nn.log_sigmoid(-z), 0.0)
    later = lax.cumsum(log_fail, axis=3, reverse=True) - log_fail
    w = jnp.where(causal, jnp.exp(log_beta + later), 0.0)
    return jnp.einsum("bhqk,bkhd->bqhd", w.astype(v.dtype), v)


def stick_breaking_prompt(q, k, v):
    bsz, s, h, dh = q.shape
    nb = s // SB_BLOCK
    qb = q.reshape(bsz, nb, SB_BLOCK, h, dh).transpose(1, 0, 2, 3, 4)
    qpos = jnp.arange(s).reshape(nb, SB_BLOCK)
    kpos = jnp.arange(s)
    o = lax.map(lambda a: stick_breaking_block(a[0], a[1], k, v, kpos), (qb, qpos))
    return o.transpose(1, 0, 2, 3, 4).reshape(bsz, s, h * dh)


def stick_breaking_sample(q, k, v, cache_k, cache_v):
    bsz, n, h, dh = q.shape
    pl = cache_k.shape[1]
    kk = jnp.concatenate([cache_k.astype(k.dtype), k], axis=1)
    vv = jnp.concatenate([cache_v.astype(v.dtype), v], axis=1)
    o = stick_breaking_block(q, pl + jnp.arange(n), kk, vv, jnp.arange(pl + n))
    return o.reshape(bsz, n, h * dh)


def hybrid_layer(x, w, cache):
    (norm_g, w_in, a_re, a_im, log_dt, b_re, b_im, c_re, c_im, d_skip, w_glu, b_glu,
     qn_g, kn_g, rel_bias, w_br_a, w_br_b, w_br_c, gate_b, w_out) = w
    f32 = jnp.float32
    bsz, s, _ = x.shape
    hx = rmsnorm(x, norm_g)
    proj = hx @ w_in
    (u_a, z_a, q_b, k_b, v_b, z_b, q_c, k_c, v_c, z_c, g_in) = jnp.split(
        proj, [MIX_W * i for i in range(1, 11)], axis=-1)
    split_heads = lambda t: t.reshape(bsz, s, -1, HEAD_DIM)
    if cache is None:
        h0_re = jnp.zeros((bsz, N_GROUPS, SSM_STATE), f32)
        h0_im = jnp.zeros((bsz, N_GROUPS, SSM_STATE), f32)
    else:
        sb_k_c, sb_v_c, band_k_c, band_v_c, h0_re, h0_im = cache
    y_a, hT_re, hT_im = s5_scan(u_a, h0_re, h0_im, a_re, a_im, log_dt, b_re, b_im, c_re, c_im, d_skip)
    y_a = jax.nn.gelu(y_a)
    y_a = (y_a * jax.nn.sigmoid(y_a @ w_glu.astype(f32) + b_glu.astype(f32))).astype(x.dtype)
    qb = rmsnorm(split_heads(q_b), qn_g)
    kb = rmsnorm(split_heads(k_b), kn_g)
    vb = split_heads(v_b)
    qc, kc, vc = split_heads(q_c), split_heads(k_c), split_heads(v_c)
    if cache is None:
        y_b = band_attention_prompt(qb, kb, vb, rel_bias)
        y_c = stick_breaking_prompt(qc, kc, vc)
        keep = min(BAND_LEN, s)
        band_k_new, band_v_new = kb[:, s - keep:], vb[:, s - keep:]
    else:
        y_b = band_attention_sample(qb, kb, vb, band_k_c, band_v_c, rel_bias)
        y_c = stick_breaking_sample(qc, kc, vc, sb_k_c, sb_v_c)
        band_k_new, band_v_new = kb, vb
    o_a = (y_a * jax.nn.silu(z_a)) @ w_br_a
    o_b = (y_b * jax.nn.silu(z_b)) @ w_br_b
    o_c = (y_c * jax.nn.silu(z_c)) @ w_br_c
    g = jax.nn.sigmoid((g_in + gate_b).astype(f32)).astype(x.dtype).reshape(bsz, s, N_BRANCH, D_MODEL)
    mixed = g[:, :, 0] * o_a + g[:, :, 1] * o_b + g[:, :, 2] * o_c
    y = x + mixed @ w_out
    return y, (kc, vc, band_k_new, band_v_new, hT_re, hT_im)


def setup_inputs(seed: int = 0) -> dict:
    key = jax.random.key(seed)
    ks = jax.random.split(key, 32)
    f32 = jnp.float32
    nrm = lambda k, shape, scale: jax.random.normal(k, shape, f32) * scale
    band_keep = min(BAND_LEN, PAST_LEN)
    n_idx = jnp.arange(SSM_STATE, dtype=f32)
    return {
        "x_prompt": nrm(ks[0], (BATCH, SEQ, D_MODEL), 1.0),
        "x_sample": nrm(ks[1], (DEC_BATCH, DEC_SEQ, D_MODEL), 1.0),
        "cache_sb_k": nrm(ks[2], (DEPTH, DEC_BATCH, PAST_LEN, N_HEADS_C, HEAD_DIM), 1.0),
        "cache_sb_v": nrm(ks[3], (DEPTH, DEC_BATCH, PAST_LEN, N_HEADS_C, HEAD_DIM), 1.0),
        "cache_band_k": nrm(ks[4], (DEPTH, DEC_BATCH, band_keep, N_HEADS_B, HEAD_DIM), 1.0),
        "cache_band_v": nrm(ks[5], (DEPTH, DEC_BATCH, band_keep, N_HEADS_B, HEAD_DIM), 1.0),
        "state_ssm_re": nrm(ks[6], (DEPTH, DEC_BATCH, N_GROUPS, SSM_STATE), 0.3),
        "state_ssm_im": nrm(ks[7], (DEPTH, DEC_BATCH, N_GROUPS, SSM_STATE), 0.3),
        "norm_g": 1.0 + nrm(ks[8], (DEPTH, D_MODEL), 0.05),
        "w_in": nrm(ks[9], (DEPTH, D_MODEL, N_IN), D_MODEL ** -0.5),
        "ssm_a_re": -0.5 + nrm(ks[10], (DEPTH, N_GROUPS, SSM_STATE), 0.01),
        "ssm_a_im": math.pi * n_idx + nrm(ks[11], (DEPTH, N_GROUPS, SSM_STATE), 0.01),
        "ssm_log_dt": jax.random.uniform(ks[12], (DEPTH, N_GROUPS), f32, math.log(1e-3), math.log(1e-1)),
        "ssm_b_re": nrm(ks[13], (DEPTH, N_GROUPS, SSM_STATE, SSM_GROUP), (2.0 * SSM_GROUP) ** -0.5),
        "ssm_b_im": nrm(ks[14], (DEPTH, N_GROUPS, SSM_STATE, SSM_GROUP), (2.0 * SSM_GROUP) ** -0.5),
        "ssm_c_re": nrm(ks[15], (DEPTH, N_GROUPS, SSM_GROUP, SSM_STATE), SSM_STATE ** -0.5),
        "ssm_c_im": nrm(ks[16], (DEPTH, N_GROUPS, SSM_GROUP, SSM_STATE), SSM_STATE ** -0.5),
        "ssm_d": nrm(ks[17], (DEPTH, MIX_W), 0.5),
        "w_glu": nrm(ks[18], (DEPTH, MIX_W, MIX_W), MIX_W ** -0.5),
        "b_glu": nrm(ks[19], (DEPTH, MIX_W), 0.02),
        "q_norm_g": 1.0 + nrm(ks[20], (DEPTH, HEAD_DIM), 0.05),
        "k_norm_g": 1.0 + nrm(ks[21], (DEPTH, HEAD_DIM), 0.05),
        "rel_bias": nrm(ks[22], (DEPTH, N_HEADS_B, 2 * MAX_REL + 1), 0.1),
        "w_br_a": nrm(ks[23], (DEPTH, MIX_W, D_MODEL), MIX_W ** -0.5),
        "w_br_b": nrm(ks[24], (DEPTH, MIX_W, D_MODEL), MIX_W ** -0.5),
        "w_br_c": nrm(ks[25], (DEPTH, MIX_W, D_MODEL), MIX_W ** -0.5),
        "gate_b": nrm(ks[26], (DEPTH, N_BRANCH * D_MODEL), 0.02),
        "w_out": nrm(ks[27], (DEPTH, D_MODEL, D_MODEL), D_MODEL ** -0.5),
    }


def reference(x_prompt, x_sample, cache_sb_k, cache_sb_v, cache_band_k, cache_band_v,
              state_ssm_re, state_ssm_im, norm_g, w_in, ssm_a_re, ssm_a_im, ssm_log_dt,
              ssm_b_re, ssm_b_im, ssm_c_re, ssm_c_im, ssm_d, w_glu, b_glu, q_norm_g, k_norm_g,
              rel_bias, w_br_a, w_br_b, w_br_c, gate_b, w_out):
    yp = x_prompt
    ys = x_sample
    p_states = []
    s_states = []
    for l in range(DEPTH):
        w = (norm_g[l], w_in[l], ssm_a_re[l], ssm_a_im[l], ssm_log_dt[l], ssm_b_re[l], ssm_b_im[l],
             ssm_c_re[l], ssm_c_im[l], ssm_d[l], w_glu[l], b_glu[l], q_norm_g[l], k_norm_g[l],
             rel_bias[l], w_br_a[l], w_br_b[l], w_br_c[l], gate_b[l], w_out[l])
        yp, sp = hybrid_layer(yp, w, None)
        ys, ss = hybrid_layer(ys, w, (cache_sb_k[l], cache_sb_v[l], cache_band_k[l], cache_band_v[l],
                                      state_ssm_re[l], state_ssm_im[l]))
        p_states.append(sp)
        s_states.append(ss)
    stk = lambda states, i: jnp.stack([st[i] for st in states], axis=0)
    return (yp, ys,
            stk(p_states, 0), stk(p_states, 1), stk(p_states, 2), stk(p_states, 3), stk(p_states, 4), stk(p_states, 5),
            stk(s_states, 0), stk(s_states, 1), stk(s_states, 2), stk(s_states, 3), stk(s_states, 4), stk(s_states, 5))
```

```python
import math
from contextlib import ExitStack
import numpy as np
import concourse.bass as bass
import concourse.mybir as mybir
from concourse.bass_utils import run_bass_kernel_spmd

F32 = mybir.dt.float32
BF16 = mybir.dt.bfloat16
AF = mybir.ActivationFunctionType
ALU = mybir.AluOpType
AX = mybir.AxisListType

T = 2048
TS = 16
NT = T + TS
D = 4096
MW = 1024
NIN = 22528
NH = 8
DH = 128
DEPTH = 2
KT = D // 128
CH = 344
NCH = NT // CH
G = 64
SN = 64
SP_ = 16
BAND = 512
EPS = 1e-6
SCALE = 1.0 / math.sqrt(DH)
NTT = 17


def trows(tt):
    return 128 if tt < 16 else TS


class Tile:
    __slots__ = ("w", "r")

    def __init__(self):
        self.w = None
        self.r = {}


class Eng:
    def __init__(self, name, sem):
        self.name = name
        self.sem = sem
        self.cnt = 0
        self.ops = []
        self.waited = {}

    def add(self, fn, deps, inc):
        waits = []
        for d in deps:
            if d is None:
                continue
            sem, v = d
            if self.waited.get(sem, 0) >= v:
                continue
            self.waited[sem] = v
            waits.append((sem, v))
        self.ops.append((waits, fn, inc))


class Rec:
    NDMA = 8

    def __init__(self, nc, st):
        self.nc = nc
        self.E = {}
        for n in ("sp", "act", "dve", "pool", "pe"):
            self.E[n] = Eng(n, st.enter_context(nc.semaphore("prog_" + n)))
        self.dq = {}
        for q in ("sp", "pool", "act"):
            sems = [st.enter_context(nc.semaphore("dma_%s_%d" % (q, i))) for i in range(self.NDMA)]
            self.dq[q] = {"sems": sems, "n": 0, "tok": [None] * self.NDMA}

    @staticmethod
    def _deps(ins, outs, extra):
        deps = list(extra)
        for t in ins:
            deps.append(t.w)
        for t in outs:
            deps.append(t.w)
            deps.extend(t.r.items())
        return deps

    @staticmethod
    def _mark(tok, ins, outs):
        for t in ins:
            s, v = tok
            if t.r.get(s, 0) < v:
                t.r[s] = v
        for t in outs:
            t.w = tok
            t.r = {}

    def op(self, eng, fn, ins=(), outs=(), extra=()):
        E = self.E[eng]
        deps = self._deps(ins, outs, extra)
        E.cnt += 1
        tok = (E.sem, E.cnt)
        E.add(fn, deps, (E.sem, 1))
        self._mark(tok, ins, outs)
        return tok

    def dma(self, q, out, in_, ins=(), outs=(), extra=(), slow=False):
        E = self.E[q]
        Q = self.dq[q]
        slot = Q["n"] % self.NDMA
        Q["n"] += 1
        deps = self._deps(ins, outs, extra)
        deps.append(Q["tok"][slot])
        sem = Q["sems"][slot]
        prev = Q["tok"][slot][1] if Q["tok"][slot] else 0
        tok = (sem, prev + 16)
        Q["tok"][slot] = tok
        E.add(lambda e: e.dma_start(out=out, in_=in_, allow_slow_non_contiguous=slow), deps, (sem, 16))
        self._mark(tok, ins, outs)
        return tok

    def all_tokens(self):
        toks = [(E.sem, E.cnt) for E in self.E.values() if E.cnt > 0]
        for Q in self.dq.values():
            toks.extend(t for t in Q["tok"] if t is not None)
        return toks

    def barrier(self):
        toks = self.all_tokens()
        for E in self.E.values():
            E.add(None, toks, None)

    def flush(self, name=None):
        nc = self.nc

        def replay(E):
            def f(e):
                for waits, fn, inc in E.ops:
                    for sem, v in waits:
                        e.wait_ge(sem, v)
                    if fn is not None:
                        ins = fn(e)
                        if inc is not None:
                            ins.then_inc(inc[0], inc[1])
                E.ops = []
            return f

        with nc.Block() as block:
            block.sync(replay(self.E["sp"]))
            block.scalar(replay(self.E["act"]))
            block.vector(replay(self.E["dve"]))
            block.gpsimd(replay(self.E["pool"]))
            block.tensor(replay(self.E["pe"]))


class Rot:
    def __init__(self, aps):
        self.items = [(a, Tile()) for a in aps]
        self.i = 0

    def next(self):
        it = self.items[self.i % len(self.items)]
        self.i += 1
        return it


class Prog:
    def __init__(self, dbg=None):
        self.dbg = dbg or {}
        self.nc = bass.Bass("TRN2", target_bir_lowering=False)
        self.st = ExitStack()
        self.R = Rec(self.nc, self.st)
        self.declare_io()

    def din(self, name, shape, dt=F32):
        if name in self.dbg.get("shrink", ()):
            return None
        return self.nc.dram_tensor(name, list(shape), dt, kind="ExternalInput").ap()

    def dout(self, name, shape, dt=F32):
        if name in self.dbg.get("shrink", ()):
            return self.nc.dram_tensor(name, list(shape), dt, kind="Internal").ap()
        return self.nc.dram_tensor(name, list(shape), dt, kind="ExternalOutput").ap()

    def dscr(self, name, shape, dt=F32):
        kind = "ExternalOutput" if name in self.dbg.get("dump", ()) else "Internal"
        return self.nc.dram_tensor(name, list(shape), dt, kind=kind).ap()

    def sb(self, st, name, shape, dt=F32):
        self.uid = getattr(self, "uid", 0) + 1
        return st.enter_context(self.nc.sbuf_tensor("%s_%d" % (name, self.uid), list(shape), dt))

    def declare_io(self):
        L = DEPTH
        self.x_p = self.din("x_prompt", [T, D])
        self.x_s = self.din("x_sample", [TS, D])
        self.c_sb_k = self.din("cache_sb_k", [L, T, MW])
        self.c_sb_v = self.din("cache_sb_v", [L, T, MW])
        self.c_bd_k = self.din("cache_band_k", [L, BAND, MW])
        self.c_bd_v = self.din("cache_band_v", [L, BAND, MW])
        self.st_re = self.din("state_ssm_re", [L, G, SN])
        self.st_im = self.din("state_ssm_im", [L, G, SN])
        self.norm_g = self.din("norm_g", [L, D])
        self.w_in = self.din("w_in", [L, D, NIN])
        self.a_re = self.din("ssm_a_re", [L, G, SN])
        self.a_im = self.din("ssm_a_im", [L, G, SN])
        self.log_dt = self.din("ssm_log_dt", [L, G])
        self.b_re = self.din("ssm_b_re", [L, G, SN, SP_])
        self.b_im = self.din("ssm_b_im", [L, G, SN, SP_])
        self.c_re = self.din("ssm_c_re", [L, G, SP_, SN])
        self.c_im = self.din("ssm_c_im", [L, G, SP_, SN])
        self.ssm_d = self.din("ssm_d", [L, MW])
        self.w_glu = self.din("w_glu", [L, MW, MW])
        self.b_glu = self.din("b_glu", [L, MW])
        self.qn_g = self.din("q_norm_g", [L, DH])
        self.kn_g = self.din("k_norm_g", [L, DH])
        self.rel_bias = self.din("rel_bias", [L, NH, 257])
        self.w_br = [self.din("w_br_a", [L, MW, D]), self.din("w_br_b", [L, MW, D]), self.din("w_br_c", [L, MW, D])]
        self.gate_b = self.din("gate_b", [L, 3 * D])
        self.w_out = self.din("w_out", [L, D, D])
        self.y_p = self.dout("y_prompt", [T, D])
        self.y_s = self.dout("y_sample", [TS, D])
        self.p_sb_k = self.dout("p_sb_k", [L, T, MW])
        self.p_sb_v = self.dout("p_sb_v", [L, T, MW])
        self.p_bd_k = self.dout("p_band_k", [L, BAND, MW])
        self.p_bd_v = self.dout("p_band_v", [L, BAND, MW])
        self.p_ss_re = self.dout("p_ssm_re", [L, G, SN])
        self.p_ss_im = self.dout("p_ssm_im", [L, G, SN])
        self.s_sb_k = self.dout("s_sb_k", [L, TS, MW])
        self.s_sb_v = self.dout("s_sb_v", [L, TS, MW])
        self.s_bd_k = self.dout("s_band_k", [L, TS, MW])
        self.s_bd_v = self.dout("s_band_v", [L, TS, MW])
        self.s_ss_re = self.dout("s_ssm_re", [L, G, SN])
        self.s_ss_im = self.dout("s_ssm_im", [L, G, SN])
        self.y1 = self.dscr("y1", [NT, D])
        self.uT = self.dscr("uT", [MW, NT])
        self.szT = [self.dscr("sz%dT" % i, [MW, NT]) for i in range(3)]
        self.qcT32 = self.dscr("qcT32", [MW, NT])
        self.gT = self.dscr("gT", [3 * D, NT])
        self.qb32 = self.dscr("qb32", [NT, MW])
        self.kb = self.dscr("kb", [NT, MW])
        self.vb = self.dscr("vb", [NT, MW])
        self.yT = [self.dscr("y%dT" % i, [MW, NT], BF16) for i in range(3)]
        self.mixT = self.dscr("mixT", [D, NT], BF16)
        self.yaT32 = self.dscr("yaT32", [MW, NT])

    def x_rows(self, l, tt):
        if l == 0:
            return self.x_p[tt * 128:(tt + 1) * 128, :] if tt < 16 else self.x_s[:, :]
        return self.y1[tt * 128:tt * 128 + trows(tt), :]

    def y_rows(self, l, tt):
        if l == DEPTH - 1:
            return self.y_p[tt * 128:(tt + 1) * 128, :] if tt < 16 else self.y_s[:, :]
        return self.y1[tt * 128:tt * 128 + trows(tt), :]

    def setup_consts(self):
        nc, R, st = self.nc, self.R, self.st
        self.ident = self.sb(st, "ident", [128, 128])
        self.identb = self.sb(st, "identb", [128, 128], BF16)
        self.t_ident = Tile()
        self.ps = [st.enter_context(nc.psum_tensor("ps%d" % i, [128, 512], F32)) for i in range(8)]
        self.psr = Rot([p for p in self.ps])
        ident, identb = self.ident, self.identb
        R.op("pool", lambda e: e.memset(ident[:], 0.0), outs=[self.t_ident])
        R.op("pool", lambda e: e.affine_select(out=ident[:], in_=ident[:], pattern=[[-1, 128]],
                                                compare_op=ALU.not_equal, fill=1.0, base=0,
                                                channel_multiplier=1), outs=[self.t_ident])
        R.op("pool", lambda e: e.tensor_copy(out=identb[:], in_=ident[:]), ins=[self.t_ident], outs=[self.t_ident])
        self.cst = self.sb(st, "cst", [128, 4])
        cst = self.cst
        self.eps_col = cst[:, 0:1]
        R.op("pool", lambda e: e.memset(cst[:, 0:1], EPS), outs=[self.t_ident])
        R.op("pool", lambda e: e.memset(cst[:, 1:2], 1.0), outs=[self.t_ident])
        R.op("pool", lambda e: e.memset(cst[:, 2:3], 0.0), outs=[self.t_ident])
        R.barrier()
        R.flush()

    def load_cols(self, st, name, src2d, n):
        R = self.R
        tmp = self.sb(st, name + "_rows", [128, 128])
        dst = self.sb(st, name, [128, n])
        t_tmp, t_dst = Tile(), Tile()
        R.dma("sp", tmp[:n, :], src2d, outs=[t_tmp])
        ps, t_ps = self.psr.next()
        ident = self.ident
        R.op("pe", lambda e: e.matmul(ps[:, :n], tmp[:n, :], ident[:n, :n], start=True, stop=True),
             ins=[t_tmp], outs=[t_ps])
        R.op("dve", lambda e: e.tensor_copy(out=dst[:, :n], in_=ps[:, :n]), ins=[t_ps], outs=[t_dst])
        return dst, t_dst

    def phase_norm(self, st, l, actT):
        R = self.R
        ready = []
        xts = Rot([self.sb(st, "n_xt%d" % i, [128, D]) for i in range(2)])
        junk = self.sb(st, "n_junk", [128, D], BF16)
        t_junk = Tile()
        gcol, t_g = self.load_cols(st, "n_gcol", self.norm_g[l].rearrange("(k p) -> k p", p=128), KT)
        small = Rot([self.sb(st, "n_sm%d" % i, [128, 4]) for i in range(2)])
        dg = Rot([self.sb(st, "n_dg%d" % i, [128, 128]) for i in range(2)])
        ident = self.ident
        for tt in range(NTT):
            r = trows(tt)
            xt, t_x = xts.next()
            R.dma("sp", xt[:r, :], self.x_rows(l, tt), outs=[t_x])
            sm, t_sm = small.next()
            R.op("act", lambda e, xt=xt, sm=sm, r=r: e.activation(out=junk[:r, :], in_=xt[:r, :], func=AF.Square,
                                                                 accum_out=sm[:r, 0:1]),
                 ins=[t_x], outs=[t_junk, t_sm])
            R.op("act", lambda e, sm=sm, r=r: e.activation(out=sm[:r, 1:2], in_=sm[:r, 0:1], func=AF.Sqrt,
                                                           scale=1.0 / D, bias=self.eps_col[:r, :]),
                 ins=[t_sm], outs=[t_sm])
            R.op("dve", lambda e, sm=sm, r=r: e.reciprocal(out=sm[:r, 2:3], in_=sm[:r, 1:2]), ins=[t_sm], outs=[t_sm])
            d, t_d = dg.next()
            R.op("dve", lambda e, d=d, sm=sm, r=r: e.tensor_scalar(out=d[:r, :r], in0=ident[:r, :r], scalar1=sm[:r, 2:3],
                                                                  scalar2=None, op0=ALU.mult),
                 ins=[t_sm, self.t_ident], outs=[t_d])
            for kg in range(KT // 4):
                ps, t_ps = self.psr.next()

                def mm(e, xt=xt, d=d, ps=ps, kg=kg, r=r):
                    for j in range(4):
                        k = kg * 4 + j
                        ins = e.matmul(ps[:, j * 128:j * 128 + r], xt[:r, k * 128:(k + 1) * 128], d[:r, :r],
                                       start=True, stop=True)
                    return ins
                R.op("pe", mm, ins=[t_x, t_d], outs=[t_ps])
                for j in range(4):
                    k = kg * 4 + j
                    dst = actT[:, k, tt * 128:tt * 128 + r]
                    if True:
                        ready.append(R.op("dve", lambda e, ps=ps, j=j, k=k, dst=dst, r=r: e.tensor_scalar(
                            out=dst, in0=ps[:, j * 128:j * 128 + r], scalar1=gcol[:, k:k + 1], scalar2=None,
                            op0=ALU.mult), ins=[t_ps, t_g]))
        return ready

    def load_slab(self, slabs, W, kt, c0, ncol):
        slab, t_slab = slabs.next()
        self.R.dma("pool", slab[:, :kt, :ncol], W[:, c0:c0 + ncol].rearrange("(k p) n -> p k n", p=128),
                   outs=[t_slab])
        return slab, t_slab

    def mm_F(self, slab, t_slab, cbl, actT, kt, ch, ready):
        ps, t_ps = self.psr.next()

        def mm(e):
            for k in range(kt):
                ins = e.matmul(ps[:, :CH], slab[:, k, cbl * 128:(cbl + 1) * 128], actT[:, k, ch * CH:(ch + 1) * CH],
                               start=(k == 0), stop=(k == kt - 1))
            return ins
        self.R.op("pe", mm, ins=[t_slab], outs=[t_ps], extra=ready)
        return ps, t_ps

    def mm_T(self, slab, t_slab, ncol, actT, kt, tt, ready):
        ps, t_ps = self.psr.next()
        r = trows(tt)

        def mm(e):
            for k in range(kt):
                ins = e.matmul(ps[:r, :ncol], actT[:, k, tt * 128:tt * 128 + r], slab[:, k, :ncol],
                               start=(k == 0), stop=(k == kt - 1))
            return ins
        self.R.op("pe", mm, ins=[t_slab], outs=[t_ps], extra=ready)
        return ps, t_ps

    def phase_gemm_in(self, st, l, actT, ready, slab_range=None):
        R = self.R
        W = self.w_in[l]
        slabs = Rot([self.sb(st, "g_slab%d" % i, [128, KT, 256], BF16) for i in range(2)])
        stF = Rot([self.sb(st, "g_stF%d" % i, [128, NT]) for i in range(2)])
        stT = Rot([self.sb(st, "g_stT%d" % i, [128, 256]) for i in range(4)])
        gb, t_c = self.load_cols(st, "g_gateb", self.gate_b[l].rearrange("(j p) -> j p", p=128), 96)
        qg = self.sb(st, "g_qg", [128, DH])
        kg = self.sb(st, "g_kg", [128, DH])
        R.dma("sp", qg[:], self.qn_g[l].partition_broadcast(128), outs=[t_c])
        R.dma("sp", kg[:], self.kn_g[l].partition_broadcast(128), outs=[t_c])
        R.op("dve", lambda e: e.tensor_scalar(out=qg[:], in0=qg[:], scalar1=SCALE, scalar2=None, op0=ALU.mult),
             ins=[t_c], outs=[t_c])
        small = Rot([self.sb(st, "g_sm%d" % i, [128, 8]) for i in range(4)])
        junk = self.sb(st, "g_junk", [128, 128], BF16)
        t_junk = Tile()

        regions = ["u", "z0", "qb", "kb", "vb", "z1", "qc", "kc", "vc", "z2"]
        nsl = NIN // 256
        for s in (slab_range if slab_range is not None else range(nsl)):
            c0 = s * 256
            reg = regions[c0 // MW] if c0 < 10 * MW else "g"
            rc = c0 % MW if reg != "g" else c0 - 10 * MW
            slab, t_slab = self.load_slab(slabs, W, KT, c0, 256)
            if reg in ("u", "z0", "z1", "z2", "qc", "g"):
                for cbl in range(2):
                    row0 = rc + cbl * 128
                    stg, t_stg = stF.next()
                    for ch in range(NCH):
                        ps, t_ps = self.mm_F(slab, t_slab, cbl, actT, KT, ch, ready)
                        dst = stg[:, ch * CH:(ch + 1) * CH]
                        if reg == "u":
                            R.op("dve", lambda e, dst=dst, ps=ps: e.tensor_copy(out=dst, in_=ps[:, :CH]),
                                 ins=[t_ps], outs=[t_stg])
                        elif reg == "qc":
                            R.op("dve", lambda e, dst=dst, ps=ps: e.tensor_copy(out=dst, in_=ps[:, :CH]),
                                 ins=[t_ps], outs=[t_stg])
                        elif reg == "g":
                            j = row0 // 128
                            R.op("act", lambda e, dst=dst, ps=ps, j=j: e.activation(
                                out=dst, in_=ps[:, :CH], func=AF.Sigmoid, bias=gb[:, j:j + 1]),
                                ins=[t_ps, t_c], outs=[t_stg])
                        else:
                            R.op("act", lambda e, dst=dst, ps=ps: e.activation(out=dst, in_=ps[:, :CH], func=AF.Silu),
                                 ins=[t_ps], outs=[t_stg])
                    if reg == "u":
                        dram = self.uT[row0:row0 + 128, :]
                    elif reg == "qc":
                        dram = self.qcT32[row0:row0 + 128, :]
                    elif reg == "g":
                        dram = self.gT[row0:row0 + 128, :]
                    else:
                        dram = self.szT[int(reg[1])][row0:row0 + 128, :]
                    R.dma("sp", dram, stg[:, :], ins=[t_stg])
            else:
                for tt in range(NTT):
                    r = trows(tt)
                    ps, t_ps = self.mm_T(slab, t_slab, 256, actT, KT, tt, ready)
                    stg, t_stg = stT.next()
                    if reg in ("qb", "kb"):
                        sm, t_sm = small.next()
                        for h in range(2):
                            R.op("act", lambda e, ps=ps, sm=sm, h=h, r=r: e.activation(
                                out=junk[:r, :], in_=ps[:r, h * 128:(h + 1) * 128], func=AF.Square,
                                accum_out=sm[:r, h:h + 1]), ins=[t_ps], outs=[t_junk, t_sm])
                        R.op("act", lambda e, sm=sm, r=r: e.activation(out=sm[:r, 2:4], in_=sm[:r, 0:2], func=AF.Sqrt,
                                                                       scale=1.0 / DH, bias=self.eps_col[:r, :]),
                             ins=[t_sm], outs=[t_sm])
                        R.op("dve", lambda e, sm=sm, r=r: e.reciprocal(out=sm[:r, 4:6], in_=sm[:r, 2:4]),
                             ins=[t_sm], outs=[t_sm])
                        gvec = qg if reg == "qb" else kg
                        for h in range(2):
                            R.op("dve", lambda e, ps=ps, sm=sm, h=h, r=r, stg=stg, gvec=gvec: e.scalar_tensor_tensor(
                                out=stg[:r, h * 128:(h + 1) * 128], in0=ps[:r, h * 128:(h + 1) * 128],
                                scalar=sm[:r, 4 + h:5 + h], in1=gvec[:r, :], op0=ALU.mult, op1=ALU.mult),
                                ins=[t_ps, t_sm, t_c], outs=[t_stg])
                    else:
                        R.op("dve", lambda e, ps=ps, stg=stg, r=r: e.tensor_copy(out=stg[:r, :], in_=ps[:r, :256]),
                             ins=[t_ps], outs=[t_stg])
                    cs = slice(rc, rc + 256)
                    tok0 = tt * 128
                    if reg == "qb":
                        R.dma("sp", self.qb32[tok0:tok0 + r, cs], stg[:r, :], ins=[t_stg])
                    elif reg in ("kb", "vb"):
                        scr = self.kb if reg == "kb" else self.vb
                        R.dma("sp", scr[tok0:tok0 + r, cs], stg[:r, :], ins=[t_stg])
                        pout = self.p_bd_k if reg == "kb" else self.p_bd_v
                        sout = self.s_bd_k if reg == "kb" else self.s_bd_v
                        if tt == 16:
                            R.dma("sp", sout[l, :, cs], stg[:r, :], ins=[t_stg])
                        elif tok0 >= T - BAND:
                            o0 = tok0 - (T - BAND)
                            R.dma("sp", pout[l, o0:o0 + 128, cs], stg[:r, :], ins=[t_stg])
                    else:
                        pout = self.p_sb_k if reg == "kc" else self.p_sb_v
                        sout = self.s_sb_k if reg == "kc" else self.s_sb_v
                        if tt == 16:
                            R.dma("sp", sout[l, :, cs], stg[:r, :], ins=[t_stg])
                        else:
                            R.dma("sp", pout[l, tok0:tok0 + 128, cs], stg[:r, :], ins=[t_stg])

    def run_phase(self, fn):
        with ExitStack() as st:
            fn(st)
            self.R.barrier()
            self.R.flush()

    def setup_masks(self):
        R, st = self.R, self.st
        self.m01 = self.sb(st, "m01", [128, 128])
        self.mneg = self.sb(st, "mneg", [128, 128])
        self.ones_col = self.cst[:, 1:2]
        m01, mneg = self.m01, self.mneg
        t = Tile()
        R.op("pool", lambda e: e.memset(m01[:], 1.0), outs=[t])
        R.op("pool", lambda e: e.affine_select(out=m01[:], in_=m01[:], pattern=[[-1, 128]], compare_op=ALU.is_gt,
                                                fill=0.0, base=0, channel_multiplier=1), outs=[t])
        R.op("pool", lambda e: e.memset(mneg[:], 0.0), outs=[t])
        R.op("pool", lambda e: e.affine_select(out=mneg[:], in_=mneg[:], pattern=[[-1, 128]], compare_op=ALU.is_gt,
                                                fill=-1e30, base=0, channel_multiplier=1), outs=[t])
        R.barrier()
        R.flush()

    def transpose_bf16(self, src_blocks, dst_fn, ins, outs_tile, evac_eng="act"):
        R = self.R
        identb = self.identb
        i = 0
        toks = []
        while i < len(src_blocks):
            grp = src_blocks[i:i + 4]
            ps, t_ps = self.psr.next()
            psb = ps[:].bitcast(BF16)

            def tr(e, grp=grp, psb=psb):
                for j, (ap, nk, nq) in enumerate(grp):
                    ins_ = e.transpose(psb[:nq, j * 128:j * 128 + nk], ap, identb[:nk, :nk])
                return ins_
            R.op("pe", tr, ins=ins, outs=[t_ps])
            for j, (ap, nk, nq) in enumerate(grp):
                dst = dst_fn(i + j)
                if evac_eng == "act":
                    toks.append(R.op("act", lambda e, dst=dst, psb=psb, j=j, nk=nk, nq=nq: e.activation(
                        out=dst, in_=psb[:nq, j * 128:j * 128 + nk], func=AF.Copy), ins=[t_ps], outs=[outs_tile]))
                else:
                    toks.append(R.op("dve", lambda e, dst=dst, psb=psb, j=j, nk=nk, nq=nq: e.tensor_copy(
                        out=dst, in_=psb[:nq, j * 128:j * 128 + nk]), ins=[t_ps], outs=[outs_tile]))
            i += 4
        return toks

    def phase_sb(self, st, l, heads=range(NH), qblocks=range(16), do_sample=True):
        R = self.R
        SM = T + TS
        qT = self.sb(st, "c_qT", [128, NT], BF16)
        kld = self.sb(st, "c_kld", [128, 16, 128], BF16)
        kT = self.sb(st, "c_kT", [128, T], BF16)
        kld2 = self.sb(st, "c_kld2", [128, 16, 128], BF16)
        t_kld2 = Tile()
        kTs = self.sb(st, "c_kTs", [128, SM], BF16)
        ksn = self.sb(st, "c_ksn", [TS, 128], BF16)
        V = self.sb(st, "c_V", [128, 16, 128], BF16)
        Vc = self.sb(st, "c_Vc", [128, 16, 128], BF16)
        Vn = self.sb(st, "c_Vn", [TS, 128], BF16)
        sz = self.sb(st, "c_sz", [128, NT])
        Y = self.sb(st, "c_Y", [128, NT], BF16)
        t_q, t_kld, t_kT, t_kTs, t_ksn, t_V, t_Vc, t_Vn, t_sz, t_Y = [Tile() for _ in range(10)]
        Eb = Rot([self.sb(st, "c_E%d" % i, [128, SM]) for i in range(3)])
        NLb = Rot([self.sb(st, "c_NL%d" % i, [128, SM]) for i in range(4)])
        Gb = Rot([self.sb(st, "c_G%d" % i, [128, SM]) for i in range(2)])
        Wb = Rot([self.sb(st, "c_W%d" % i, [128, SM], BF16) for i in range(2)])
        WTb = Rot([self.sb(st, "c_WT%d" % i, [128, 17, 128], BF16) for i in range(3)])
        ng = Rot([self.sb(st, "c_ng%d" % i, [128, 1]) for i in range(3)])
        identb = self.identb
        m01, mneg = self.m01, self.mneg
        ones_col = self.ones_col

        class Ctx:
            pass

        def st_z_el(desc):
            (h, qlo, r, kTa, S, vblocks, d0) = desc
            c = Ctx()
            c.qlo, c.r, c.S, c.vblocks, c.d0 = qlo, r, S, vblocks, d0
            c.E, c.t_E = Eb.next()
            c.pss = []
            nchunk = (S + 511) // 512
            for ci in range(nchunk):
                n = min(512, S - ci * 512)
                ps, t_ps = self.psr.next()
                R.op("pe", lambda e, ps=ps, ci=ci, n=n: e.matmul(ps[:r, :n], qT[:, qlo:qlo + r], kTa[:, ci * 512:ci * 512 + n],
                                                                 start=True, stop=True),
                     ins=[t_q, t_kT, t_kTs], outs=[t_ps])
                cs = slice(ci * 512, ci * 512 + n)
                E = c.E
                R.op("act", lambda e, ps=ps, n=n, cs=cs, E=E: e.activation(out=E[:r, cs], in_=ps[:r, :n], func=AF.Exp, scale=-SCALE),
                     ins=[t_ps], outs=[c.t_E])
                R.op("act", lambda e, cs=cs, E=E: e.activation(out=E[:r, cs], in_=E[:r, cs], func=AF.Ln, bias=ones_col[:r, :]),
                     ins=[c.t_E], outs=[c.t_E])
                c.pss.append((ps, t_ps, n, cs))
            return c

        def st_nlf(c):
            r, d0, E = c.r, c.d0, c.E
            c.NL, c.t_NL = NLb.next()
            NL = c.NL
            for (ps, t_ps, n, cs) in c.pss:
                R.op("dve", lambda e, ps=ps, n=n, cs=cs: e.scalar_tensor_tensor(
                    out=NL[:r, cs], in0=ps[:r, :n], scalar=SCALE, in1=E[:r, cs], op0=ALU.mult, op1=ALU.add),
                    ins=[t_ps, c.t_E], outs=[c.t_NL])
            R.op("dve", lambda e: e.tensor_tensor(out=NL[:r, d0:d0 + r], in0=NL[:r, d0:d0 + r], in1=m01[:r, :r], op=ALU.mult),
                 ins=[c.t_NL], outs=[c.t_NL])

        def st_scan_arg(c):
            r, S, d0, E, NL = c.r, c.S, c.d0, c.E, c.NL
            Gt, t_G = Gb.next()
            c.ngc, c.t_ng = ng.next()
            ngc = c.ngc
            R.op("dve", lambda e: e.tensor_tensor_scan(out=Gt[:r, :S], data0=ones_col[:r, :].to_broadcast([r, S]),
                                                       data1=NL[:r, :S], initial=0.0, op0=ALU.mult, op1=ALU.add),
                 ins=[c.t_NL], outs=[t_G])
            R.op("dve", lambda e: e.tensor_scalar(out=ngc[:r, :], in0=Gt[:r, S - 1:S], scalar1=-1.0, scalar2=None, op0=ALU.mult),
                 ins=[t_G], outs=[c.t_ng])
            R.op("pool", lambda e: e.tensor_tensor(out=NL[:r, :S], in0=Gt[:r, :S], in1=E[:r, :S], op=ALU.subtract),
                 ins=[t_G, c.t_E], outs=[c.t_NL])
            R.op("pool", lambda e: e.tensor_tensor(out=NL[:r, d0:d0 + r], in0=NL[:r, d0:d0 + r], in1=mneg[:r, :r], op=ALU.add),
                 ins=[c.t_NL], outs=[c.t_NL])

        def st_exp2(c):
            r, S, NL, ngc = c.r, c.S, c.NL, c.ngc
            c.W, c.t_W = Wb.next()
            W = c.W
            R.op("act", lambda e: e.activation(out=W[:r, :S], in_=NL[:r, :S], func=AF.Exp, bias=ngc[:r, :]),
                 ins=[c.t_NL, c.t_ng], outs=[c.t_W])

        def st_t_cp(c):
            r, S, W = c.r, c.S, c.W
            c.WT, c.t_WT = WTb.next()
            WT = c.WT
            nb = (S + 127) // 128
            c.nb = nb
            b0 = 0
            while b0 < nb:
                grp = list(range(b0, min(b0 + 4, nb)))
                ps, t_ps = self.psr.next()
                psb = ps[:].bitcast(BF16)

                def tr(e, grp=grp, psb=psb):
                    for j, b in enumerate(grp):
                        nk = min(128, S - b * 128)
                        ins_ = e.transpose(psb[:nk, j * 128:j * 128 + r], W[:r, b * 128:b * 128 + nk], identb[:r, :r])
                    return ins_
                R.op("pe", tr, ins=[c.t_W], outs=[t_ps])
                full = [b for b in grp if S - b * 128 >= 128]
                if full:
                    nf = len(full)
                    if r == 128:
                        R.op("act", lambda e, psb=psb, f0=full[0], nf=nf: e.activation(
                            out=WT[:, f0:f0 + nf, :], in_=psb[:, 0:nf * 128].rearrange("p (b q) -> p b q", q=128), func=AF.Copy),
                            ins=[t_ps], outs=[c.t_WT])
                    else:
                        for j, b in enumerate(full):
                            R.op("act", lambda e, psb=psb, j=j, b=b: e.activation(
                                out=WT[:, b, :r], in_=psb[:, j * 128:j * 128 + r], func=AF.Copy), ins=[t_ps], outs=[c.t_WT])
                for j, b in enumerate(grp):
                    nk = min(128, S - b * 128)
                    if nk < 128:
                        R.op("act", lambda e, psb=psb, j=j, b=b, nk=nk: e.activation(
                            out=WT[:nk, b, :r], in_=psb[:nk, j * 128:j * 128 + r], func=AF.Copy), ins=[t_ps], outs=[c.t_WT])
                b0 += 4

        def st_pv_y(c):
            r, qlo, vblocks, WT, nb = c.r, c.qlo, c.vblocks, c.WT, c.nb
            ps, t_ps = self.psr.next()

            def pv(e):
                for b in range(nb):
                    vap, nk = vblocks[b]
                    ins_ = e.matmul(ps[:, :r], vap, WT[:nk, b, :r], start=(b == 0), stop=(b == nb - 1))
                return ins_
            R.op("pe", pv, ins=[c.t_WT, t_V, t_Vc, t_Vn], outs=[t_ps])
            R.op("dve", lambda e: e.tensor_tensor(out=Y[:, qlo:qlo + r], in0=ps[:, :r], in1=sz[:, qlo:qlo + r], op=ALU.mult),
                 ins=[t_ps, t_sz], outs=[t_Y])

        def run_pipeline(descs):
            n = len(descs)
            ctx = {}
            for j in range(-2, n + 1):
                if 0 <= j - 1 < n:
                    st_pv_y(ctx[j - 1])
                    del ctx[j - 1]
                if 0 <= j < n:
                    st_exp2(ctx[j])
                if 0 <= j + 2 < n:
                    ctx[j + 2] = st_z_el(descs[j + 2])
                if 0 <= j < n:
                    st_t_cp(ctx[j])
                if 0 <= j + 1 < n:
                    st_scan_arg(ctx[j + 1])
                if 0 <= j + 2 < n:
                    st_nlf(ctx[j + 2])

        for h in heads:
            hs = slice(h * 128, (h + 1) * 128)
            if len(qblocks) < 16:
                R.op("pool", lambda e: e.memset(Y[:, :], 0.0), outs=[t_Y])
            R.dma("pool", qT[:, :], self.qcT32[hs, :], outs=[t_q])
            R.dma("sp", sz[:, :], self.szT[2][hs, :], outs=[t_sz])
            R.dma("pool", kld[:, :, :], self.p_sb_k[l][:, hs].rearrange("(b p) d -> p b d", p=128), outs=[t_kld])
            R.dma("pool", V[:, :, :], self.p_sb_v[l][:, hs].rearrange("(b p) d -> p b d", p=128), outs=[t_V])
            self.transpose_bf16([(kld[:, b, :], 128, 128) for b in range(16)],
                                lambda i: kT[:, i * 128:(i + 1) * 128], ins=[t_kld], outs_tile=t_kT, evac_eng="dve")
            if do_sample:
                R.dma("pool", kld2[:, :, :], self.c_sb_k[l][:, hs].rearrange("(b p) d -> p b d", p=128), outs=[t_kld2])
                R.dma("pool", Vc[:, :, :], self.c_sb_v[l][:, hs].rearrange("(b p) d -> p b d", p=128), outs=[t_Vc])
                R.dma("pool", ksn[:, :], self.s_sb_k[l][:, hs], outs=[t_ksn])
                R.dma("pool", Vn[:, :], self.s_sb_v[l][:, hs], outs=[t_Vn])
                self.transpose_bf16([(kld2[:, b, :], 128, 128) for b in range(16)],
                                    lambda i: kTs[:, i * 128:(i + 1) * 128], ins=[t_kld2], outs_tile=t_kTs, evac_eng="dve")
                self.transpose_bf16([(ksn[:, :], TS, 128)], lambda i: kTs[:, T:T + TS], ins=[t_ksn], outs_tile=t_kTs,
                                    evac_eng="dve")
            descs = []
            for qb in qblocks:
                S = 128 * (qb + 1)
                descs.append((h, qb * 128, 128, kT, S, [(V[:, b, :], 128) for b in range(qb + 1)], qb * 128))
            if do_sample:
                descs.append((h, T, TS, kTs, SM, [(Vc[:, b, :], 128) for b in range(16)] + [(Vn[:, :], TS)], T))
            run_pipeline(descs)
            R.dma("sp", self.yT[2][hs, :], Y[:, :], ins=[t_Y])

    def setup_band_consts(self):
        R, st = self.R, self.st
        self.Jb = self.sb(st, "Jb", [128, 128], BF16)
        self.J32 = self.sb(st, "J32", [128, 128])
        self.ones32 = self.sb(st, "ones32", [128, 128])
        J32, Jb, ones32 = self.J32, self.Jb, self.ones32
        t = Tile()
        R.op("pool", lambda e: e.memset(J32[:], 0.0), outs=[t])
        R.op("pool", lambda e: e.affine_select(out=J32[:], in_=J32[:], pattern=[[1, 128]], compare_op=ALU.not_equal,
                                                fill=1.0, base=-127, channel_multiplier=1), outs=[t])
        R.op("pool", lambda e: e.tensor_copy(out=Jb[:], in_=J32[:]), ins=[t], outs=[t])
        R.op("pool", lambda e: e.memset(ones32[:], 1.0), outs=[t])
        self.ext = self.dscr("ext_bias", [NH, 768])
        R.barrier()
        R.flush()

    def phase_band(self, st, l, heads=range(NH), mblocks=range(16), do_sample=True):
        R = self.R
        ident, identb, Jb, ones32 = self.ident, self.identb, self.Jb, self.ones32
        rb = self.sb(st, "b_rb", [NH, 257])
        exs = self.sb(st, "b_exs", [NH, 768])
        t_rb, t_ext = Tile(), Tile()
        R.dma("sp", rb[:, :], self.rel_bias[l], outs=[t_rb])
        R.op("dve", lambda e: e.tensor_copy(out=exs[:, 0:256], in_=rb[:, 1:257]), ins=[t_rb], outs=[t_ext])
        R.op("dve", lambda e: e.tensor_copy(out=exs[:, 256:768], in_=rb[:, 256:257].to_broadcast([NH, 512])),
             ins=[t_rb], outs=[t_ext])
        t_extd = Tile()
        R.dma("sp", self.ext[:, :], exs[:, :], ins=[t_ext], outs=[t_extd])

        HK = self.sb(st, "b_HK", [128, 5, 128], BF16)
        qld = self.sb(st, "b_qld", [128, 16, 128], BF16)
        qls = self.sb(st, "b_qls", [TS, 128], BF16)
        kld = self.sb(st, "b_kld", [128, 16, 128], BF16)
        kls = self.sb(st, "b_kls", [TS, 128], BF16)
        kcl = self.sb(st, "b_kcl", [128, 4, 128], BF16)
        qT = self.sb(st, "b_qT", [128, NT], BF16)
        kT = self.sb(st, "b_kT", [128, NT], BF16)
        kcT = self.sb(st, "b_kcT", [128, BAND], BF16)
        V = self.sb(st, "b_V", [128, 16, 128], BF16)
        Vn = self.sb(st, "b_Vn", [TS, 128], BF16)
        Vc = self.sb(st, "b_Vc", [128, 4, 128], BF16)
        sz = self.sb(st, "b_sz", [128, NT])
        Y = self.sb(st, "b_Y", [128, NT], BF16)
        t_HK, t_qld, t_kld, t_kcl, t_qT, t_kT, t_kcT, t_V, t_sz, t_Y = [Tile() for _ in range(10)]
        Pb = Rot([self.sb(st, "b_P%d" % i, [128, 640], BF16) for i in range(3)])
        PTb = Rot([self.sb(st, "b_PT%d" % i, [128, 5, 128], BF16) for i in range(3)])
        smb = Rot([self.sb(st, "b_sm%d" % i, [128, 8]) for i in range(3)])
        D32b = Rot([self.sb(st, "b_D%d" % i, [128, 128]) for i in range(3)])
        tmpb = Rot([self.sb(st, "b_tmp%d" % i, [128, 128]) for i in range(2)])
        for (P_, tP) in Pb.items:
            R.op("pool", lambda e, P_=P_: e.memset(P_[:, :], 0.0), outs=[tP])

        class Ctx:
            pass

        def b_qk(desc):
            (qlo, r, segs, vblocks) = desc
            c = Ctx()
            c.qlo, c.r, c.segs, c.vblocks = qlo, r, segs, vblocks
            c.psA, c.t_A = self.psr.next()
            c.psB, c.t_B = self.psr.next()
            psA, psB, t_A, t_B = c.psA, c.psB, c.t_A, c.t_B
            pss = [(psA, t_A), (psB, t_B)]
            for (pi, c0, n, kap, kc0) in segs:
                ps, t_ps = pss[pi]
                cc = kc0 // 128

                def mm(e, ps=ps, c0=c0, n=n, kap=kap, cc=cc):
                    e.matmul(ps[:r, c0:c0 + n], qT[:, qlo:qlo + r], kap, start=True, stop=False)
                    return e.matmul(ps[:r, c0:c0 + n], HK[:, 4 - cc, 0:r], Jb[:, 0:n], start=False, stop=True)
                R.op("pe", mm, ins=[t_qT, t_kT, t_kcT, t_HK], outs=[t_ps])
            c.sm, c.t_sm = smb.next()
            sm, t_sm = c.sm, c.t_sm
            nA = sum(n for (pi, c0, n, kap, kc0) in segs if pi == 0)
            a0 = min([c0 for (pi, c0, n, kap, kc0) in segs if pi == 0] + [512])
            nB = sum(n for (pi, c0, n, kap, kc0) in segs if pi == 1)
            R.op("dve", lambda e: e.tensor_reduce(out=sm[:r, 1:2], in_=psB[:r, 0:nB], axis=AX.X, op=ALU.max),
                 ins=[t_B], outs=[t_sm])
            if nA > 0:
                R.op("dve", lambda e: e.tensor_reduce(out=sm[:r, 0:1], in_=psA[:r, a0:a0 + nA], axis=AX.X, op=ALU.max),
                     ins=[t_A], outs=[t_sm])
                R.op("dve", lambda e: e.tensor_scalar(out=sm[:r, 2:3], in0=sm[:r, 0:1], scalar1=sm[:r, 1:2], scalar2=-1.0,
                                                      op0=ALU.max, op1=ALU.mult), ins=[t_sm], outs=[t_sm])
            else:
                R.op("dve", lambda e: e.tensor_scalar(out=sm[:r, 2:3], in0=sm[:r, 1:2], scalar1=-1.0, scalar2=None,
                                                      op0=ALU.mult), ins=[t_sm], outs=[t_sm])
            return c

        def b_exp(c):
            r, segs, sm, t_sm, psA, psB, t_A, t_B = c.r, c.segs, c.sm, c.t_sm, c.psA, c.psB, c.t_A, c.t_B
            lo = segs[0][4]
            c.P, c.t_P = Pb.next()
            P_, t_P = c.P, c.t_P
            if r == 128:
                halves = [(0, 64, lo, 576), (64, 128, max(lo, 64), 640)]
            else:
                halves = [(0, r, 0, BAND + TS)]
            ncol = 3
            for (p0, p1, v0, v1) in halves:
                if v0 < 512:
                    e1 = min(v1, 512)
                    R.op("act", lambda e, p0=p0, p1=p1, v0=v0, e1=e1, ncol=ncol: e.activation(
                        out=P_[p0:p1, v0:e1], in_=psA[p0:p1, v0:e1], func=AF.Exp, bias=sm[p0:p1, 2:3],
                        accum_out=sm[p0:p1, ncol:ncol + 1]), ins=[t_A, t_sm], outs=[t_P, t_sm])
                else:
                    R.op("pool", lambda e, p0=p0, p1=p1, ncol=ncol: e.memset(sm[p0:p1, ncol:ncol + 1], 0.0), outs=[t_sm])
                R.op("act", lambda e, p0=p0, p1=p1, v1=v1, ncol=ncol: e.activation(
                    out=P_[p0:p1, 512:v1], in_=psB[p0:p1, 0:v1 - 512], func=AF.Exp, bias=sm[p0:p1, 2:3],
                    accum_out=sm[p0:p1, ncol + 1:ncol + 2]), ins=[t_B, t_sm], outs=[t_P, t_sm])
            R.op("dve", lambda e: e.tensor_tensor(out=sm[:r, 5:6], in0=sm[:r, 3:4], in1=sm[:r, 4:5], op=ALU.add),
                 ins=[t_sm], outs=[t_sm])
            R.op("dve", lambda e: e.reciprocal(out=sm[:r, 6:7], in_=sm[:r, 5:6]), ins=[t_sm], outs=[t_sm])
            c.D32, c.t_D = D32b.next()
            D32 = c.D32
            R.op("dve", lambda e: e.tensor_scalar(out=D32[:r, :r], in0=ident[:r, :r], scalar1=sm[:r, 6:7], scalar2=None,
                                                  op0=ALU.mult), ins=[t_sm], outs=[c.t_D])

        def b_t_cp(c):
            r, vblocks, P_ = c.r, c.vblocks, c.P
            c.PT, c.t_PT = PTb.next()
            PT = c.PT
            nb = len(vblocks)
            b0 = 0
            while b0 < nb:
                grp = list(range(b0, min(b0 + 4, nb)))
                ps, t_ps = self.psr.next()
                psb = ps[:].bitcast(BF16)

                def tr(e, grp=grp, psb=psb):
                    for j, b in enumerate(grp):
                        (vap, nk, kc0) = vblocks[b]
                        ins_ = e.transpose(psb[:nk, j * 128:j * 128 + r], P_[:r, kc0:kc0 + nk], identb[:r, :r])
                    return ins_
                R.op("pe", tr, ins=[c.t_P], outs=[t_ps])
                full = [b for b in grp if vblocks[b][1] == 128]
                if full and r == 128:
                    nf = len(full)
                    R.op("act", lambda e, psb=psb, f0=full[0], nf=nf: e.activation(
                        out=PT[:, f0:f0 + nf, :], in_=psb[:, 0:nf * 128].rearrange("p (b q) -> p b q", q=128), func=AF.Copy),
                        ins=[t_ps], outs=[c.t_PT])
                else:
                    for j, b in enumerate(grp):
                        if vblocks[b][1] == 128:
                            R.op("act", lambda e, psb=psb, j=j, b=b: e.activation(
                                out=PT[:, b, :r], in_=psb[:, j * 128:j * 128 + r], func=AF.Copy), ins=[t_ps], outs=[c.t_PT])
                for j, b in enumerate(grp):
                    nk = vblocks[b][1]
                    if nk < 128:
                        R.op("act", lambda e, psb=psb, j=j, b=b, nk=nk: e.activation(
                            out=PT[:nk, b, :r], in_=psb[:nk, j * 128:j * 128 + r], func=AF.Copy), ins=[t_ps], outs=[c.t_PT])
                b0 += 4

        def b_pv_y(c):
            r, qlo, vblocks, PT, D32 = c.r, c.qlo, c.vblocks, c.PT, c.D32
            pso, t_o = self.psr.next()
            nb = len(vblocks)

            def pv(e):
                for b, (vap, nk, kc0) in enumerate(vblocks):
                    ins_ = e.matmul(pso[:, :r], vap, PT[:nk, b, :r], start=(b == 0), stop=(b == nb - 1))
                return ins_
            R.op("pe", pv, ins=[c.t_PT, t_V], outs=[t_o])
            psr_, t_r = self.psr.next()
            R.op("pe", lambda e: e.matmul(psr_[:, :r], ones32[:r, :], D32[:r, :r], start=True, stop=True),
                 ins=[c.t_D], outs=[t_r])
            tmp, t_tmp = tmpb.next()
            R.op("dve", lambda e: e.tensor_tensor(out=tmp[:, :r], in0=pso[:, :r], in1=sz[:, qlo:qlo + r], op=ALU.mult),
                 ins=[t_o, t_sz], outs=[t_tmp])
            R.op("dve", lambda e: e.tensor_tensor(out=Y[:, qlo:qlo + r], in0=psr_[:, :r], in1=tmp[:, :r], op=ALU.mult),
                 ins=[t_r, t_tmp], outs=[t_Y])

        def run_pipeline(descs):
            n = len(descs)
            ctx = {}
            for j in range(-2, n + 1):
                if 0 <= j - 1 < n:
                    b_pv_y(ctx[j - 1])
                    del ctx[j - 1]
                if 0 <= j + 2 < n:
                    ctx[j + 2] = b_qk(descs[j + 2])
                if 0 <= j + 1 < n:
                    b_exp(ctx[j + 1])
                if 0 <= j < n:
                    b_t_cp(ctx[j])

        for h in heads:
            hs = slice(h * 128, (h + 1) * 128)
            descs = []
            if len(mblocks) < 16:
                R.op("pool", lambda e: e.memset(Y[:, :], 0.0), outs=[t_Y])
            R.dma("pool", HK[:, :, :], bass.AP(self.ext.tensor, self.ext[h, 0:1].offset, [[1, 128], [128, 5], [1, 128]]),
                  ins=[t_extd], outs=[t_HK])
            R.dma("pool", qld[:, :, :], self.qb32[0:T, hs].rearrange("(b p) d -> p b d", p=128), outs=[t_qld])
            R.dma("pool", qls[:, :], self.qb32[T:NT, hs], outs=[t_qld])
            R.dma("pool", kld[:, :, :], self.kb[0:T, hs].rearrange("(b p) d -> p b d", p=128), outs=[t_kld])
            R.dma("pool", kls[:, :], self.kb[T:NT, hs], outs=[t_kld])
            R.dma("pool", kcl[:, :, :], self.c_bd_k[l][:, hs].rearrange("(b p) d -> p b d", p=128), outs=[t_kcl])
            R.dma("pool", V[:, :, :], self.vb[0:T, hs].rearrange("(b p) d -> p b d", p=128), outs=[t_V])
            R.dma("pool", Vn[:, :], self.vb[T:NT, hs], outs=[t_V])
            R.dma("pool", Vc[:, :, :], self.c_bd_v[l][:, hs].rearrange("(b p) d -> p b d", p=128), outs=[t_V])
            R.dma("sp", sz[:, :], self.szT[1][hs, :], outs=[t_sz])
            self.transpose_bf16([(qld[:, b, :], 128, 128) for b in range(16)] + [(qls[:, :], TS, 128)],
                                lambda i: qT[:, i * 128:i * 128 + (128 if i < 16 else TS)], ins=[t_qld], outs_tile=t_qT,
                                evac_eng="dve")
            self.transpose_bf16([(kld[:, b, :], 128, 128) for b in range(16)] + [(kls[:, :], TS, 128)],
                                lambda i: kT[:, i * 128:i * 128 + (128 if i < 16 else TS)], ins=[t_kld], outs_tile=t_kT,
                                evac_eng="dve")
            self.transpose_bf16([(kcl[:, b, :], 128, 128) for b in range(4)],
                                lambda i: kcT[:, i * 128:(i + 1) * 128], ins=[t_kcl], outs_tile=t_kcT, evac_eng="dve")
            for m in mblocks:
                lo = max(0, 512 - 128 * m)
                w0 = 128 * m - 512
                segs, vbl = [], []
                for c in range(lo // 128, 5):
                    kc0 = c * 128
                    segs.append((0 if c < 4 else 1, kc0 if c < 4 else 0, 128, kT[:, w0 + kc0:w0 + kc0 + 128], kc0))
                    vbl.append((V[:, (w0 + kc0) // 128, :], 128, kc0))
                descs.append((m * 128, 128, segs, vbl))
            if do_sample:
                segs = [(0, c * 128, 128, kcT[:, c * 128:(c + 1) * 128], c * 128) for c in range(4)]
                segs.append((1, 0, TS, kT[:, T:T + TS], 512))
                vbl = [(Vc[:, c, :], 128, c * 128) for c in range(4)] + [(Vn[:, :], TS, 512)]
                descs.append((T, TS, segs, vbl))
            run_pipeline(descs)
            R.dma("sp", self.yT[1][hs, :], Y[:, :], ins=[t_Y])

    def phase_merge(self, st, l, slab_range=None):
        R = self.R
        yts = []
        t_y = Tile()
        for i in range(3):
            yt = self.sb(st, "f_y%d" % i, [128, 8, NT], BF16)
            R.dma("sp", yt[:, :, :], self.yT[i].rearrange("(k p) n -> p k n", p=128), outs=[t_y])
            yts.append(yt)
        slabs = [Rot([self.sb(st, "f_slab%d_%d" % (i, j), [128, 8, 256], BF16) for j in range(2)]) for i in range(3)]
        gts = [Rot([self.sb(st, "f_g%d_%d" % (i, j), [128, NT]) for j in range(2)]) for i in range(3)]
        accs = Rot([self.sb(st, "f_acc%d" % j, [128, CH]) for j in range(3)])
        tmps = Rot([self.sb(st, "f_tmp%d" % j, [128, CH]) for j in range(3)])
        stg = Rot([self.sb(st, "f_stg%d" % j, [128, NT], BF16) for j in range(2)])
        for s in (slab_range if slab_range is not None else range(D // 256)):
            c0 = s * 256
            sl = [self.load_slab(slabs[i], self.w_br[i][l], 8, c0, 256) for i in range(3)]
            for cbl in range(2):
                fb = c0 // 128 + cbl
                gs = []
                for i in range(3):
                    g, t_g = gts[i].next()
                    R.dma("sp", g[:, :], self.gT[i * D + fb * 128:i * D + (fb + 1) * 128, :], outs=[t_g])
                    gs.append((g, t_g))
                so, t_so = stg.next()
                for ch in range(NCH):
                    cs = slice(ch * CH, (ch + 1) * CH)
                    pss = []
                    for i in range(3):
                        slab, t_slab = sl[i]
                        pss.append(self.mm_F(slab, t_slab, cbl, yts[i], 8, ch, [t_y.w]))
                    acc, t_acc = accs.next()
                    tmp, t_tmp = tmps.next()
                    R.op("dve", lambda e, acc=acc, ps=pss[0][0], g=gs[0][0], cs=cs: e.tensor_tensor(
                        out=acc[:, :], in0=ps[:, :CH], in1=g[:, cs], op=ALU.mult), ins=[pss[0][1], gs[0][1]], outs=[t_acc])
                    R.op("dve", lambda e, tmp=tmp, ps=pss[1][0], g=gs[1][0], cs=cs: e.tensor_tensor(
                        out=tmp[:, :], in0=ps[:, :CH], in1=g[:, cs], op=ALU.mult), ins=[pss[1][1], gs[1][1]], outs=[t_tmp])
                    R.op("pool", lambda e, acc=acc, tmp=tmp: e.tensor_tensor(out=acc[:, :], in0=acc[:, :], in1=tmp[:, :], op=ALU.add),
                         ins=[t_tmp], outs=[t_acc])
                    R.op("dve", lambda e, tmp=tmp, ps=pss[2][0], g=gs[2][0], cs=cs: e.tensor_tensor(
                        out=tmp[:, :], in0=ps[:, :CH], in1=g[:, cs], op=ALU.mult), ins=[pss[2][1], gs[2][1]], outs=[t_tmp])
                    R.op("pool", lambda e, acc=acc, tmp=tmp, so=so, cs=cs: e.tensor_tensor(out=so[:, cs], in0=acc[:, :], in1=tmp[:, :],
                                                                                         op=ALU.add),
                         ins=[t_tmp, t_acc], outs=[t_so])
                R.dma("sp", self.mixT[fb * 128:(fb + 1) * 128, :], so[:, :], ins=[t_so])

    def phase_out(self, st, l, actT, slab_range=None):
        R = self.R
        t_a = Tile()
        R.dma("sp", actT[:, :, :], self.mixT.rearrange("(k p) n -> p k n", p=128), outs=[t_a])
        slabs = Rot([self.sb(st, "o_slab%d" % i, [128, KT, 256], BF16) for i in range(2)])
        xts = Rot([self.sb(st, "o_x%d" % i, [128, 256]) for i in range(4)])
        for s in (slab_range if slab_range is not None else range(D // 256)):
            c0 = s * 256
            slab, t_slab = self.load_slab(slabs, self.w_out[l], KT, c0, 256)
            for tt in range(NTT):
                r = trows(tt)
                xt, t_x = xts.next()
                R.dma("sp", xt[:r, :], self.x_rows(l, tt)[:, c0:c0 + 256], outs=[t_x])
                ps, t_ps = self.mm_T(slab, t_slab, 256, actT, KT, tt, [t_a.w])
                R.op("dve", lambda e, xt=xt, ps=ps, r=r: e.tensor_tensor(out=xt[:r, :], in0=ps[:r, :256], in1=xt[:r, :], op=ALU.add),
                     ins=[t_ps], outs=[t_x])
                R.dma("sp", self.y_rows(l, tt)[:, c0:c0 + 256], xt[:r, :], ins=[t_x])

    def setup_s5_consts(self, st=None):
        R = self.R
        st = st if st is not None else self.st
        t = Tile()
        self.I2 = self.sb(st, "s5_I2", [64, 32])
        self.SelAll = self.sb(st, "s5_Sel", [128, 64, 128], BF16)
        rm = self.sb(st, "s5_rm", [128, 8])
        ident, I2, SelAll = self.ident, self.I2, self.SelAll
        cst = self.cst
        R.op("pool", lambda e: e.memset(cst[:, 3:4], math.pi / 2), outs=[t])
        self.halfpi = cst[:, 3:4]
        R.barrier()
        R.flush()
        R.op("pool", lambda e: e.tensor_copy(out=I2[0:32, :], in_=ident[0:32, 0:32]), outs=[t])
        R.op("pool", lambda e: e.tensor_copy(out=I2[32:64, :], in_=ident[32:64, 32:64]), outs=[t])
        R.op("pool", lambda e: e.memset(rm[:, :], 1.0), outs=[t])
        R.op("pool", lambda e: e.affine_select(out=rm[:, :], in_=rm[:, :], pattern=[[-16, 8]], compare_op=ALU.is_ge,
                                                fill=0.0, base=0, channel_multiplier=1), outs=[t])
        R.op("pool", lambda e: e.affine_select(out=rm[:, :], in_=rm[:, :], pattern=[[16, 8]], compare_op=ALU.is_ge,
                                                fill=0.0, base=15, channel_multiplier=-1), outs=[t])
        with ExitStack() as tmp:
            SI = self.sb(tmp, "s5_SI", [128, 15, 128])
            R.op("pool", lambda e: e.memset(SI[:, :, :], 0.0), outs=[t])
            for s in range(-7, 8):
                R.op("pool", lambda e, s=s: e.affine_select(out=SI[:, s + 7, :], in_=SI[:, s + 7, :], pattern=[[1, 128]],
                                                             compare_op=ALU.not_equal, fill=1.0, base=-16 * s,
                                                             channel_multiplier=-1), outs=[t])
            for a in range(8):
                for b in range(8):
                    R.op("dve", lambda e, a=a, b=b: e.tensor_scalar(out=SelAll[:, a * 8 + b, :], in0=SI[:, b - a + 7, :],
                                                                    scalar1=rm[:, a:a + 1], scalar2=None, op0=ALU.mult),
                         ins=[t], outs=[t])
            R.barrier()
            R.flush()

    def phase_s5(self, st, l, pairs=range(32)):
        R = self.R
        self.setup_s5_consts(st)
        ident, I2, SelAll = self.ident, self.I2, self.SelAll
        NC_ = NT // 8
        PADL = 128
        t = Tile()

        def tt(eng, out, a, b, op):
            R.op(eng, lambda e: e.tensor_tensor(out=out, in0=a, in1=b, op=op), ins=[t], outs=[t])

        def ts(eng, out, a, s1, op0, s2=None, op1=None):
            if op1 is None:
                R.op(eng, lambda e: e.tensor_scalar(out=out, in0=a, scalar1=s1, scalar2=None, op0=op0), ins=[t], outs=[t])
            else:
                R.op(eng, lambda e: e.tensor_scalar(out=out, in0=a, scalar1=s1, scalar2=s2, op0=op0, op1=op1),
                     ins=[t], outs=[t])

        LPre = self.sb(st, "s5_LPre", [128, 9, 32])
        LPim = self.sb(st, "s5_LPim", [128, 9, 32])
        AKre = self.sb(st, "s5_AKre", [128, 8, 32])
        AKim = self.sb(st, "s5_AKim", [128, 8, 32])
        AKni = self.sb(st, "s5_AKni", [128, 8, 32])
        Vre = self.sb(st, "s5_Vre", [128, 32, 8, 16], BF16)
        Vim = self.sb(st, "s5_Vim", [128, 32, 8, 16], BF16)
        KTp = self.sb(st, "s5_KT", [128, 64, 128], BF16)
        WPre = self.sb(st, "s5_WPre", [128, 64, 128], BF16)
        WPim = self.sb(st, "s5_WPim", [128, 64, 128], BF16)
        H0re = self.sb(st, "s5_H0re", [128, 32])
        H0im = self.sb(st, "s5_H0im", [128, 32])
        dcol, t_dcol = self.load_cols(st, "s5_dcol", self.ssm_d[l].rearrange("(k p) -> k p", p=128), 8)

        with ExitStack() as tb:
            gl = self.sb(tb, "s5_gl", [64, 64])
            Lh = self.sb(tb, "s5_Lh", [64, 128])
            are = self.sb(tb, "s5_are", [128, 32])
            aim = self.sb(tb, "s5_aim", [128, 32])
            dtp = self.sb(tb, "s5_dtp", [128, 32])
            adt = self.sb(tb, "s5_adt", [128, 32])
            th = self.sb(tb, "s5_th", [128, 32])
            w = [self.sb(tb, "s5_w%d" % i, [128, 32]) for i in range(8)]
            ldc = self.sb(tb, "s5_ldc", [64, 1])
            R.op("pool", lambda e: e.memset(Lh[:, :], 0.0), ins=[t], outs=[t])

            def to_pair(dst, fill_gl):
                fill_gl()
                R.op("dve", lambda e: e.tensor_copy(out=Lh[0:32, 0:64], in_=gl[0:32, :]), ins=[t], outs=[t])
                R.op("dve", lambda e: e.tensor_copy(out=Lh[32:64, 64:128], in_=gl[32:64, :]), ins=[t], outs=[t])
                ps, t_ps = self.psr.next()
                R.op("pe", lambda e: e.matmul(ps[:, 0:32], Lh[:, :], I2[:, :], start=True, stop=True), ins=[t], outs=[t_ps])
                R.op("dve", lambda e: e.tensor_copy(out=dst, in_=ps[:, 0:32]), ins=[t_ps, t], outs=[t])

            to_pair(are[:, :], lambda: R.dma("sp", gl[:, :], self.a_re[l], ins=[t], outs=[t]))
            to_pair(aim[:, :], lambda: R.dma("sp", gl[:, :], self.a_im[l], ins=[t], outs=[t]))
            to_pair(H0re[:, :], lambda: R.dma("sp", gl[:, :], self.st_re[l], ins=[t], outs=[t]))
            to_pair(H0im[:, :], lambda: R.dma("sp", gl[:, :], self.st_im[l], ins=[t], outs=[t]))

            def fill_dt():
                R.dma("sp", ldc[:, :], self.log_dt[l].unsqueeze(1), ins=[t], outs=[t])
                R.op("act", lambda e: e.activation(out=ldc[:, :], in_=ldc[:, :], func=AF.Exp), ins=[t], outs=[t])
                R.op("dve", lambda e: e.tensor_copy(out=gl[:, :], in_=ldc[:, 0:1].to_broadcast([64, 64])), ins=[t], outs=[t])
            to_pair(dtp[:, :], fill_dt)
            tt("dve", adt[:, :], are[:, :], dtp[:, :], ALU.mult)
            tt("dve", th[:, :], aim[:, :], dtp[:, :], ALU.mult)
            R.op("act", lambda e: e.activation(out=w[0][:, :], in_=adt[:, :], func=AF.Exp, scale=1.0 / 32), ins=[t], outs=[t])
            R.op("act", lambda e: e.activation(out=w[1][:, :], in_=th[:, :], func=AF.Sin, scale=1.0 / 32), ins=[t], outs=[t])
            R.op("act", lambda e: e.activation(out=w[2][:, :], in_=th[:, :], func=AF.Sin, scale=1.0 / 32,
                                               bias=self.halfpi), ins=[t], outs=[t])
            tt("dve", w[3][:, :], w[0][:, :], w[2][:, :], ALU.mult)
            tt("dve", w[4][:, :], w[0][:, :], w[1][:, :], ALU.mult)

            def csq(o_re, o_im, a_re_, a_im_):
                tt("dve", w[6][:, :], a_re_, a_re_, ALU.mult)
                tt("dve", w[7][:, :], a_im_, a_im_, ALU.mult)
                tt("dve", w[5][:, :], a_re_, a_im_, ALU.mult)
                tt("dve", o_re, w[6][:, :], w[7][:, :], ALU.subtract)
                ts("dve", o_im, w[5][:, :], 2.0, ALU.mult)

            def cmul(o_re, o_im, a_re_, a_im_, b_re_, b_im_, sh=None):
                F = [128, 32] if sh is None else sh
                t1 = w[6][:, :] if sh is None else big[0]
                t2 = w[7][:, :] if sh is None else big[1]
                tt("dve", t1, a_re_, b_re_, ALU.mult)
                tt("dve", t2, a_im_, b_im_, ALU.mult)
                tt("dve", o_re, t1, t2, ALU.subtract)
                tt("dve", t1, a_re_, b_im_, ALU.mult)
                tt("dve", t2, a_im_, b_re_, ALU.mult)
                tt("dve", o_im, t1, t2, ALU.add)

            cur = (w[3], w[4])
            alt = (w[0], w[1])
            for i in range(5):
                dst = (LPre[:, 1, :], LPim[:, 1, :]) if i == 4 else (alt[0][:, :], alt[1][:, :])
                csq(dst[0], dst[1], cur[0][:, :], cur[1][:, :])
                cur, alt = alt, cur
            R.op("pool", lambda e: e.memset(LPre[:, 0, :], 1.0), ins=[t], outs=[t])
            R.op("pool", lambda e: e.memset(LPim[:, 0, :], 0.0), ins=[t], outs=[t])
            L_ = lambda k: (LPre[:, k, :], LPim[:, k, :])
            csq(*L_(2), *L_(1))
            cmul(*L_(3), *L_(2), *L_(1))
            csq(*L_(4), *L_(2))
            cmul(*L_(5), *L_(4), *L_(1))
            csq(*L_(6), *L_(3))
            cmul(*L_(7), *L_(4), *L_(3))
            csq(*L_(8), *L_(4))
            R.op("dve", lambda e: e.tensor_copy(out=AKre[:, 0, :], in_=LPre[:, 8, :]), ins=[t], outs=[t])
            R.op("dve", lambda e: e.tensor_copy(out=AKim[:, 0, :], in_=LPim[:, 8, :]), ins=[t], outs=[t])
            for k in range(1, 8):
                csq(AKre[:, k, :], AKim[:, k, :], AKre[:, k - 1, :], AKim[:, k - 1, :])
            ts("dve", AKni[:, :, :], AKim[:, :, :], -1.0, ALU.mult)
            kre = self.sb(tb, "s5_kre", [128, 32])
            kim = self.sb(tb, "s5_kim", [128, 32])
            tt("dve", w[0][:, :], are[:, :], are[:, :], ALU.mult)
            tt("dve", w[1][:, :], aim[:, :], aim[:, :], ALU.mult)
            tt("dve", w[0][:, :], w[0][:, :], w[1][:, :], ALU.add)
            R.op("dve", lambda e: e.reciprocal(out=w[0][:, :], in_=w[0][:, :]), ins=[t], outs=[t])
            ts("dve", w[1][:, :], LPre[:, 1, :], -1.0, ALU.add)
            tt("dve", w[2][:, :], w[1][:, :], are[:, :], ALU.mult)
            tt("dve", w[3][:, :], LPim[:, 1, :], aim[:, :], ALU.mult)
            tt("dve", w[2][:, :], w[2][:, :], w[3][:, :], ALU.add)
            tt("dve", kre[:, :], w[2][:, :], w[0][:, :], ALU.mult)
            tt("dve", w[2][:, :], LPim[:, 1, :], are[:, :], ALU.mult)
            tt("dve", w[3][:, :], w[1][:, :], aim[:, :], ALU.mult)
            tt("dve", w[2][:, :], w[2][:, :], w[3][:, :], ALU.subtract)
            tt("dve", kim[:, :], w[2][:, :], w[0][:, :], ALU.mult)
            Bre = self.sb(tb, "s5_Bre", [128, 32, 16])
            Bim = self.sb(tb, "s5_Bim", [128, 32, 16])
            BBre = self.sb(tb, "s5_BBre", [128, 32, 16])
            BBim = self.sb(tb, "s5_BBim", [128, 32, 16])
            big = [self.sb(tb, "s5_big%d" % i, [128, 32, 16])[:, :, :] for i in range(2)]
            for h in range(2):
                R.dma("sp", Bre[h * 64:(h + 1) * 64, :, :], self.b_re[l][32 * h:32 * h + 32].rearrange("g n p -> n g p"),
                      ins=[t], outs=[t])
                R.dma("sp", Bim[h * 64:(h + 1) * 64, :, :], self.b_im[l][32 * h:32 * h + 32].rearrange("g n p -> n g p"),
                      ins=[t], outs=[t])
            bc = lambda ap: ap.unsqueeze(2).to_broadcast([128, 32, 16])
            cmul(BBre[:, :, :], BBim[:, :, :], bc(kre[:, :]), bc(kim[:, :]), Bre[:, :, :], Bim[:, :, :], sh=1)
            MXre = self.sb(tb, "s5_MXre", [128, 32, 15, 16])
            MXim = self.sb(tb, "s5_MXim", [128, 32, 15, 16])
            R.op("pool", lambda e: e.memset(MXre[:, :, :, :], 0.0), ins=[t], outs=[t])
            R.op("pool", lambda e: e.memset(MXim[:, :, :, :], 0.0), ins=[t], outs=[t])
            for d in range(8):
                e_ = 7 - d
                if d == 0:
                    R.op("dve", lambda e: e.tensor_copy(out=MXre[:, :, 7, :], in_=BBre[:, :, :]), ins=[t], outs=[t])
                    R.op("dve", lambda e: e.tensor_copy(out=MXim[:, :, 7, :], in_=BBim[:, :, :]), ins=[t], outs=[t])
                else:
                    cmul(MXre[:, :, e_, :], MXim[:, :, e_, :], bc(LPre[:, d, :]), bc(LPim[:, d, :]), BBre[:, :, :], BBim[:, :, :],
                         sh=1)
            CTre = self.sb(tb, "s5_CTre", [128, 32, 16])
            CTim = self.sb(tb, "s5_CTim", [128, 32, 16])
            CTni = self.sb(tb, "s5_CTni", [128, 32, 16])
            Cpad = self.sb(tb, "s5_Cpad", [128, 8, 128])
            for (src, dstC) in ((self.c_re, CTre), (self.c_im, CTim)):
                R.op("pool", lambda e: e.memset(Cpad[:, :, :], 0.0), ins=[t], outs=[t])
                for rt in range(8):
                    h = rt // 4
                    R.dma("sp", Cpad[:, rt, h * 64:(h + 1) * 64],
                          src[l][rt * 8:(rt + 1) * 8].rearrange("g q n -> (g q) n"), ins=[t], outs=[t])
                ps, t_ps = self.psr.next()

                def mmc(e, ps=ps):
                    for rt4 in range(4):
                        e.matmul(ps[:, rt4 * 128:(rt4 + 1) * 128], Cpad[:, rt4, :], ident[:, :], start=True, stop=False)
                        ins_ = e.matmul(ps[:, rt4 * 128:(rt4 + 1) * 128], Cpad[:, rt4 + 4, :], ident[:, :], start=False, stop=True)
                    return ins_
                R.op("pe", mmc, ins=[t], outs=[t_ps])
                R.op("dve", lambda e, ps=ps, dstC=dstC: e.tensor_copy(out=dstC[:, :, :].rearrange("p g q -> p (g q)"), in_=ps[:, :]),
                     ins=[t_ps, t], outs=[t])
            ts("dve", CTni[:, :, :], CTim[:, :, :], -1.0, ALU.mult)
            for j in range(8):
                lr, li = bc(LPre[:, j + 1, :]), bc(LPim[:, j + 1, :])
                tt("dve", big[0], CTre[:, :, :], lr, ALU.mult)
                tt("dve", big[1], CTim[:, :, :], li, ALU.mult)
                tt("dve", Vre[:, :, j, :], big[0], big[1], ALU.subtract)
                tt("dve", big[0], CTre[:, :, :], li, ALU.mult)
                tt("dve", big[1], CTni[:, :, :], lr, ALU.mult)
                tt("dve", Vim[:, :, j, :], big[1], big[0], ALU.subtract)
            R.op("pool", lambda e: e.memset(WPre[:, :, :], 0.0), ins=[t], outs=[t])
            R.op("pool", lambda e: e.memset(WPim[:, :, :], 0.0), ins=[t], outs=[t])
            t_tabs_done = t.w
            for g in range(G):
                h, gg = g // 32, g % 32
                hp = slice(h * 64, (h + 1) * 64)
                ps, t_ps = self.psr.next()

                def mmk(e, ps=ps, hp=hp, gg=gg):
                    for j in range(8):
                        lo_ = (7 - j)
                        mre = MXre[hp, gg, lo_:lo_ + 8, :].rearrange("p e q -> p (e q)")
                        mim = MXim[hp, gg, lo_:lo_ + 8, :].rearrange("p e q -> p (e q)")
                        e.matmul(ps[:, j * 16:(j + 1) * 16], mre, CTre[hp, gg, :], start=True, stop=False)
                        e.matmul(ps[:, j * 16:(j + 1) * 16], mim, CTni[hp, gg, :], start=False, stop=True)
                    wre = MXre[hp, gg, 0:8, :].rearrange("p e q -> p (e q)")
                    wim = MXim[hp, gg, 0:8, :].rearrange("p e q -> p (e q)")
                    e.matmul(ps[:, 128:192], wre, ident[hp, hp], start=True, stop=True)
                    return e.matmul(ps[:, 192:256], wim, ident[hp, hp], start=True, stop=True)
                R.op("pe", mmk, outs=[t_ps], extra=[t_tabs_done])
                R.op("act", lambda e, ps=ps, g=g: e.activation(out=KTp[:, g, :], in_=ps[:, 0:128], func=AF.Copy),
                     ins=[t_ps], extra=[t_tabs_done])
                R.op("dve", lambda e, ps=ps, g=g, hp=hp: e.tensor_copy(out=WPre[:, g, hp], in_=ps[:, 128:192]),
                     ins=[t_ps], extra=[t_tabs_done])
                R.op("act", lambda e, ps=ps, g=g, hp=hp: e.activation(out=WPim[:, g, hp], in_=ps[:, 192:256], func=AF.Copy),
                     ins=[t_ps], extra=[t_tabs_done])
            R.barrier()
            R.flush()

        ub = self.sb(st, "s5_ub", [128, 8, NT], BF16)
        t_ub = Tile()
        R.dma("pool", ub[:, :, :], self.uT.rearrange("(k p) n -> p k n", p=128), outs=[t_ub])
        Ub = Rot([self.sb(st, "s5_U%d" % i, [128, NC_], BF16) for i in range(4)])
        Zre = Rot([self.sb(st, "s5_Zre%d" % i, [128, PADL + 256]) for i in range(2)])
        Zim = Rot([self.sb(st, "s5_Zim%d" % i, [128, PADL + 256]) for i in range(2)])
        Xre = Rot([self.sb(st, "s5_Xre%d" % i, [128, PADL + 256]) for i in range(2)])
        Xim = Rot([self.sb(st, "s5_Xim%d" % i, [128, PADL + 256]) for i in range(2)])
        for rot in (Zre, Zim, Xre, Xim):
            for (a, ta) in rot.items:
                R.op("pool", lambda e, a=a: e.memset(a[:, 0:PADL], 0.0), outs=[ta])
        Zs = Rot([self.sb(st, "s5_Zs%d" % i, [128, 8]) for i in range(2)])
        Spb = Rot([self.sb(st, "s5_Sp%d" % i, [128, 2, NC_], BF16) for i in range(2)])
        Ysb = self.sb(st, "s5_Ysb", [128, 16, NC_], BF16)
        t_Ysb = Tile()
        Sfin = self.sb(st, "s5_Sfin", [128, 4, 32])
        t_Sfin = Tile()
        u32 = Rot([self.sb(st, "s5_u32_%d" % i, [128, NT]) for i in range(2)])
        ya32 = Rot([self.sb(st, "s5_ya32_%d" % i, [128, NT]) for i in range(2)])

        def p1(gg):
            zre, t_zre = Zre.next()
            zim, t_zim = Zim.next()
            us = []
            for h in range(2):
                g = 32 * h + gg
                b, g8 = g // 8, g % 8
                U, t_U = Ub.next()
                ps, t_ps = self.psr.next()

                def mmu(e, ps=ps, b=b, g8=g8):
                    for i in range(8):
                        ins_ = e.matmul(ps[:, :NC_], SelAll[:, g8 * 8 + i, :],
                                        ub[:, b, :].rearrange("p (c i) -> p i c", i=8)[:, i, :], start=(i == 0), stop=(i == 7))
                    return ins_
                R.op("pe", mmu, ins=[t_ub], outs=[t_ps])
                R.op("act", lambda e, U=U, ps=ps: e.activation(out=U[:, :], in_=ps[:, :NC_], func=AF.Copy), ins=[t_ps], outs=[t_U])
                us.append((U, t_U, g))
            psr_, t_pr = self.psr.next()
            psi_, t_pi = self.psr.next()

            def mmz(e):
                for h in range(2):
                    U, t_U, g = us[h]
                    e.matmul(psr_[:, :NC_], WPre[:, g, :], U[:, :], start=(h == 0), stop=(h == 1))
                for h in range(2):
                    U, t_U, g = us[h]
                    ins_ = e.matmul(psi_[:, :NC_], WPim[:, g, :], U[:, :], start=(h == 0), stop=(h == 1))
                return ins_
            R.op("pe", mmz, ins=[us[0][1], us[1][1]], outs=[t_pr, t_pi])
            zs, t_zs = Zs.next()
            R.op("dve", lambda e: e.tensor_copy(out=zre[:, PADL:PADL + 256], in_=psr_[:, 0:256]), ins=[t_pr], outs=[t_zre])
            R.op("dve", lambda e: e.tensor_copy(out=zim[:, PADL:PADL + 256], in_=psi_[:, 0:256]), ins=[t_pi], outs=[t_zim])
            R.op("dve", lambda e: e.tensor_copy(out=zs[:, 0:2], in_=psr_[:, 256:258]), ins=[t_pr], outs=[t_zs])
            R.op("dve", lambda e: e.tensor_copy(out=zs[:, 2:4], in_=psi_[:, 256:258]), ins=[t_pi], outs=[t_zs])
            xre, t_xre = Xre.next()
            xim, t_xim = Xim.next()
            c = {"gg": gg, "us": us, "zs": zs, "t_zs": t_zs,
                 "cur": (zre, t_zre, zim, t_zim), "nxt": (xre, t_xre, xim, t_xim)}
            return c

        def scan_step(c, k):
            gg = c["gg"]
            if True:
                d = 1 << k
                cr, t_cr, ci, t_ci = c["cur"]
                nr, t_nr, ni, t_ni = c["nxt"]
                a_r, a_i, a_n = AKre[:, k, gg:gg + 1], AKim[:, k, gg:gg + 1], AKni[:, k, gg:gg + 1]
                lo_, hi_ = PADL, PADL + 256
                R.op("dve", lambda e, nr=nr, cr=cr, a_r=a_r, d=d: e.scalar_tensor_tensor(
                    out=nr[:, lo_:hi_], in0=cr[:, lo_ - d:hi_ - d], scalar=a_r, in1=cr[:, lo_:hi_], op0=ALU.mult, op1=ALU.add),
                    ins=[t_cr], outs=[t_nr])
                R.op("dve", lambda e, nr=nr, ci=ci, a_n=a_n, d=d: e.scalar_tensor_tensor(
                    out=nr[:, lo_:hi_], in0=ci[:, lo_ - d:hi_ - d], scalar=a_n, in1=nr[:, lo_:hi_], op0=ALU.mult, op1=ALU.add),
                    ins=[t_ci], outs=[t_nr])
                R.op("dve", lambda e, ni=ni, ci=ci, a_r=a_r, d=d: e.scalar_tensor_tensor(
                    out=ni[:, lo_:hi_], in0=ci[:, lo_ - d:hi_ - d], scalar=a_r, in1=ci[:, lo_:hi_], op0=ALU.mult, op1=ALU.add),
                    ins=[t_ci], outs=[t_ni])
                R.op("dve", lambda e, ni=ni, cr=cr, a_i=a_i, d=d: e.scalar_tensor_tensor(
                    out=ni[:, lo_:hi_], in0=cr[:, lo_ - d:hi_ - d], scalar=a_i, in1=ni[:, lo_:hi_], op0=ALU.mult, op1=ALU.add),
                    ins=[t_cr], outs=[t_ni])
                c["cur"], c["nxt"] = c["nxt"], c["cur"]

        def p3(c):
            gg, us, zs, t_zs = c["gg"], c["us"], c["zs"], c["t_zs"]
            sre, t_sre, sim, t_sim = c["cur"]
            a_r, a_i, a_n = AKre[:, 0, gg:gg + 1], AKim[:, 0, gg:gg + 1], AKni[:, 0, gg:gg + 1]
            prev_re, prev_im = H0re[:, gg:gg + 1], H0im[:, gg:gg + 1]
            for s_ in range(2):
                o_re, o_im = zs[:, 4 + s_:5 + s_], zs[:, 6 + s_:7 + s_]
                z_re, z_im = zs[:, s_:s_ + 1], zs[:, 2 + s_:3 + s_]
                for (o, p1, c1, p2, c2, z) in ((o_re, prev_re, a_r, prev_im, a_n, z_re), (o_im, prev_im, a_r, prev_re, a_i, z_im)):
                    R.op("dve", lambda e, o=o, p1=p1, c1=c1, z=z: e.scalar_tensor_tensor(
                        out=o, in0=p1, scalar=c1, in1=z, op0=ALU.mult, op1=ALU.add), ins=[t_zs], outs=[t_zs])
                    R.op("dve", lambda e, o=o, p2=p2, c2=c2: e.scalar_tensor_tensor(
                        out=o, in0=p2, scalar=c2, in1=o, op0=ALU.mult, op1=ALU.add), ins=[t_zs], outs=[t_zs])
                prev_re, prev_im = o_re, o_im
            sp, t_sp = Spb.next()
            R.op("act", lambda e: e.activation(out=sp[:, 0, 0:256], in_=sre[:, PADL - 1:PADL + 255], func=AF.Copy),
                 ins=[t_sre], outs=[t_sp])
            R.op("act", lambda e: e.activation(out=sp[:, 1, 0:256], in_=sim[:, PADL - 1:PADL + 255], func=AF.Copy),
                 ins=[t_sim], outs=[t_sp])
            R.op("pool", lambda e: e.tensor_copy(out=sp[:, 0, 256:257], in_=H0re[:, gg:gg + 1]), outs=[t_sp])
            R.op("pool", lambda e: e.tensor_copy(out=sp[:, 1, 256:257], in_=H0im[:, gg:gg + 1]), outs=[t_sp])
            R.op("pool", lambda e: e.tensor_copy(out=sp[:, 0, 257:258], in_=zs[:, 4:5]), ins=[t_zs], outs=[t_sp])
            R.op("pool", lambda e: e.tensor_copy(out=sp[:, 1, 257:258], in_=zs[:, 6:7]), ins=[t_zs], outs=[t_sp])
            R.op("pool", lambda e: e.tensor_copy(out=Sfin[:, 0, gg:gg + 1], in_=sre[:, PADL + 255:PADL + 256]), ins=[t_sre], outs=[t_Sfin])
            R.op("pool", lambda e: e.tensor_copy(out=Sfin[:, 1, gg:gg + 1], in_=sim[:, PADL + 255:PADL + 256]), ins=[t_sim], outs=[t_Sfin])
            R.op("pool", lambda e: e.tensor_copy(out=Sfin[:, 2, gg:gg + 1], in_=zs[:, 5:6]), ins=[t_zs], outs=[t_Sfin])
            R.op("pool", lambda e: e.tensor_copy(out=Sfin[:, 3, gg:gg + 1], in_=zs[:, 7:8]), ins=[t_zs], outs=[t_Sfin])
            for h in range(2):
                U, t_U, g = us[h]
                hp = slice(h * 64, (h + 1) * 64)
                ps, t_ps = self.psr.next()

                def mmy(e, ps=ps, U=U, g=g, hp=hp):
                    e.matmul(ps[:, :NC_], KTp[:, g, :], U[:, :], start=True, stop=False)
                    e.matmul(ps[:, :NC_], Vre[hp, gg, :, :].rearrange("p j q -> p (j q)"), sp[hp, 0, :], start=False, stop=False)
                    return e.matmul(ps[:, :NC_], Vim[hp, gg, :, :].rearrange("p j q -> p (j q)"), sp[hp, 1, :], start=False, stop=True)
                R.op("pe", mmy, ins=[t_U, t_sp], outs=[t_ps])
                slot = (gg % 8) + 8 * h
                R.op("act", lambda e, ps=ps, slot=slot: e.activation(out=Ysb[:, slot, :], in_=ps[:, :NC_], func=AF.Copy),
                     ins=[t_ps], outs=[t_Ysb])

        def finish_blocks(b0):
            for h in range(2):
                b = b0 + 4 * h
                u, t_u = u32.next()
                ya, t_ya = ya32.next()
                R.dma("sp", u[:, :], self.uT[b * 128:(b + 1) * 128, :], outs=[t_u])
                for j in range(8):
                    ps, t_ps = self.psr.next()

                    def mmi(e, ps=ps, j=j, h=h):
                        for g8 in range(8):
                            ins_ = e.matmul(ps[:, :NC_], SelAll[:, j * 8 + g8, :], Ysb[:, 8 * h + g8, :], start=(g8 == 0), stop=(g8 == 7))
                        return ins_
                    R.op("pe", mmi, ins=[t_Ysb], outs=[t_ps])
                    uv = u[:, :].rearrange("p (c i) -> p i c", i=8)[:, j, :]
                    yv = ya[:, :].rearrange("p (c i) -> p i c", i=8)[:, j, :]
                    R.op("dve", lambda e, uv=uv, yv=yv, ps=ps, b=b: e.scalar_tensor_tensor(
                        out=yv, in0=uv, scalar=dcol[:, b:b + 1], in1=ps[:, :NC_], op0=ALU.mult, op1=ALU.add),
                        ins=[t_ps, t_u, t_dcol], outs=[t_ya])
                R.op("act", lambda e, ya=ya: e.activation(out=ya[:, :], in_=ya[:, :], func=AF.Gelu), ins=[t_ya], outs=[t_ya])
                R.dma("sp", self.yaT32[b * 128:(b + 1) * 128, :], ya[:, :], ins=[t_ya])

        pl = list(pairs)
        for i0 in range(0, len(pl), 2):
            grp = pl[i0:i0 + 2]
            cs_ = [p1(gg) for gg in grp]
            for k in range(8):
                for c in cs_:
                    scan_step(c, k)
            for c in cs_:
                p3(c)
            if grp[-1] % 8 == 7:
                finish_blocks(grp[-1] // 8)
        stg = self.sb(st, "s5_stg", [32, 4, 128])
        t_stg = Tile()
        for i, dst in enumerate((self.p_ss_re, self.p_ss_im, self.s_ss_re, self.s_ss_im)):
            ps, t_ps = self.psr.next()
            R.op("pe", lambda e, ps=ps, i=i: e.matmul(ps[:32, 0:128], Sfin[:, i, :], ident[:, :], start=True, stop=True),
                 ins=[t_Sfin], outs=[t_ps])
            R.op("dve", lambda e, ps=ps, i=i: e.tensor_copy(out=stg[:, i, :], in_=ps[:32, 0:128]), ins=[t_ps], outs=[t_stg])
            for h in range(2):
                R.dma("sp", dst[l, 32 * h:32 * h + 32, :], stg[:, i, h * 64:(h + 1) * 64], ins=[t_stg])

    def phase_glu(self, st, l):
        R = self.R
        bgcol, t_bg = self.load_cols(st, "s5_bgcol", self.b_glu[l].rearrange("(k p) -> k p", p=128), 8)
        yab = self.sb(st, "s5_yab", [128, 8, NT], BF16)
        t_yab = Tile()
        R.dma("pool", yab[:, :, :], self.yaT32.rearrange("(k p) n -> p k n", p=128), outs=[t_yab])
        slabs = Rot([self.sb(st, "s5_slab%d" % i, [128, 8, 256], BF16) for i in range(2)])
        yrow = Rot([self.sb(st, "s5_yrow%d" % i, [128, NT]) for i in range(2)])
        zrow = Rot([self.sb(st, "s5_zrow%d" % i, [128, NT]) for i in range(2)])
        gate = Rot([self.sb(st, "s5_gate%d" % i, [128, CH]) for i in range(3)])
        outs_ = Rot([self.sb(st, "s5_out%d" % i, [128, NT], BF16) for i in range(2)])
        for s in range(MW // 256):
            slab, t_slab = self.load_slab(slabs, self.w_glu[l], 8, s * 256, 256)
            for cbl in range(2):
                cb = s * 2 + cbl
                yr, t_yr = yrow.next()
                zr, t_zr = zrow.next()
                R.dma("sp", yr[:, :], self.yaT32[cb * 128:(cb + 1) * 128, :], outs=[t_yr])
                R.dma("sp", zr[:, :], self.szT[0][cb * 128:(cb + 1) * 128, :], outs=[t_zr])
                R.op("pool", lambda e, yr=yr, zr=zr: e.tensor_tensor(out=yr[:, :], in0=yr[:, :], in1=zr[:, :], op=ALU.mult),
                     ins=[t_zr], outs=[t_yr])
                o, t_o = outs_.next()
                for ch in range(NCH):
                    cs = slice(ch * CH, (ch + 1) * CH)
                    ps, t_ps = self.mm_F(slab, t_slab, cbl, yab, 8, ch, [t_yab.w])
                    gt, t_gt = gate.next()
                    R.op("act", lambda e, gt=gt, ps=ps, cb=cb: e.activation(out=gt[:, :], in_=ps[:, :CH], func=AF.Sigmoid,
                                                                          bias=bgcol[:, cb:cb + 1]), ins=[t_ps, t_bg], outs=[t_gt])
                    R.op("dve", lambda e, gt=gt, yr=yr, o=o, cs=cs: e.tensor_tensor(out=o[:, cs], in0=gt[:, :], in1=yr[:, cs], op=ALU.mult),
                         ins=[t_gt, t_yr], outs=[t_o])
                R.dma("sp", self.yT[0][cb * 128:(cb + 1) * 128, :], o[:, :], ins=[t_o])


def build_program():
    P = Prog()
    P.setup_consts()
    P.setup_masks()
    P.setup_band_consts()
    for l in range(DEPTH):
        with ExitStack() as sa:
            actT = P.sb(sa, "actT", [128, KT, NT], BF16)
            P.run_phase(lambda st: P.phase_norm(st, l, actT))
            P.run_phase(lambda st: P.phase_gemm_in(st, l, actT, []))
        P.run_phase(lambda st: P.phase_s5(st, l))
        P.run_phase(lambda st: P.phase_glu(st, l))
        P.run_phase(lambda st: P.phase_band(st, l))
        P.run_phase(lambda st: P.phase_sb(st, l))
        P.run_phase(lambda st: P.phase_merge(st, l))
        with ExitStack() as sa:
            actT = P.sb(sa, "actT", [128, KT, NT], BF16)
            P.run_phase(lambda st: P.phase_out(st, l, actT))
    return P


_PER_CORE_5D = ("cache_sb_k", "cache_sb_v", "cache_band_k", "cache_band_v")
_WEIGHTS = ("norm_g", "w_in", "ssm_a_re", "ssm_a_im", "ssm_log_dt", "ssm_b_re", "ssm_b_im", "ssm_c_re", "ssm_c_im",
            "ssm_d", "w_glu", "b_glu", "q_norm_g", "k_norm_g", "rel_bias", "w_br_a", "w_br_b", "w_br_c", "gate_b", "w_out")


def kernel(**inputs):
    NB = 8
    P = build_program()
    f32 = lambda a: np.ascontiguousarray(np.asarray(a, dtype=np.float32))
    w = {k: f32(inputs[k]) for k in _WEIGHTS}
    in_maps = []
    for b in range(NB):
        m = dict(w)
        m["x_prompt"] = f32(inputs["x_prompt"][b])
        m["x_sample"] = f32(inputs["x_sample"][b])
        for k in _PER_CORE_5D:
            a = np.asarray(inputs[k])
            m[k] = f32(a[:, b].reshape(a.shape[0], a.shape[2], MW))
        m["state_ssm_re"] = f32(np.asarray(inputs["state_ssm_re"])[:, b])
        m["state_ssm_im"] = f32(np.asarray(inputs["state_ssm_im"])[:, b])
        in_maps.append(m)
    res = run_bass_kernel_spmd(P.nc, in_maps, core_ids=list(range(NB)))
    r = res.results
    L = DEPTH
    st = lambda name: np.stack([np.asarray(r[b][name]) for b in range(NB)], axis=0)
    y_p = st("y_prompt")
    y_s = st("y_sample")

    def kv(name, rows):
        a = st(name)
        return np.ascontiguousarray(a.transpose(1, 0, 2, 3)).reshape(L, NB, rows, NH, DH)

    def ss(name):
        return np.ascontiguousarray(st(name).transpose(1, 0, 2, 3))

    return (y_p, y_s,
            kv("p_sb_k", T), kv("p_sb_v", T), kv("p_band_k", BAND), kv("p_band_v", BAND), ss("p_ssm_re"), ss("p_ssm_im"),
            kv("s_sb_k", TS), kv("s_sb_v", TS), kv("s_band_k", TS), kv("s_band_v", TS), ss("s_ssm_re"), ss("s_ssm_im"))
```

```python
import math
from contextlib import ExitStack
import numpy as np
import concourse.bass as bass
import concourse.mybir as mybir
from concourse.bass_utils import run_bass_kernel_spmd

F32 = mybir.dt.float32
BF16 = mybir.dt.bfloat16
AF = mybir.ActivationFunctionType
ALU = mybir.AluOpType
AX = mybir.AxisListType

T = 2048
TS = 16
NT = T + TS
D = 4096
MW = 1024
NIN = 22528
NH = 8
DH = 128
DEPTH = 2
KT = D // 128
CH = 344
NCH = NT // CH
G = 64
SN = 64
SP_ = 16
BAND = 512
EPS = 1e-6
SCALE = 1.0 / math.sqrt(DH)
NTT = 17


def trows(tt):
    return 128 if tt < 16 else TS


class Tile:
    __slots__ = ("w", "r")

    def __init__(self):
        self.w = None
        self.r = {}


class Eng:
    def __init__(self, name, sem):
        self.name = name
        self.sem = sem
        self.cnt = 0
        self.ops = []
        self.waited = {}

    def add(self, fn, deps, inc):
        waits = []
        for d in deps:
            if d is None:
                continue
            sem, v = d
            if self.waited.get(sem, 0) >= v:
                continue
            self.waited[sem] = v
            waits.append((sem, v))
        self.ops.append((waits, fn, inc))


class Rec:
    NDMA = 8

    def __init__(self, nc, st):
        self.nc = nc
        self.E = {}
        for n in ("sp", "act", "dve", "pool", "pe"):
            self.E[n] = Eng(n, st.enter_context(nc.semaphore("prog_" + n)))
        self.dq = {}
        for q in ("sp", "pool", "act"):
            sems = [st.enter_context(nc.semaphore("dma_%s_%d" % (q, i))) for i in range(self.NDMA)]
            self.dq[q] = {"sems": sems, "n": 0, "tok": [None] * self.NDMA}

    @staticmethod
    def _deps(ins, outs, extra):
        deps = list(extra)
        for t in ins:
            deps.append(t.w)
        for t in outs:
            deps.append(t.w)
            deps.extend(t.r.items())
        return deps

    @staticmethod
    def _mark(tok, ins, outs):
        for t in ins:
            s, v = tok
            if t.r.get(s, 0) < v:
                t.r[s] = v
        for t in outs:
            t.w = tok
            t.r = {}

    def op(self, eng, fn, ins=(), outs=(), extra=()):
        E = self.E[eng]
        deps = self._deps(ins, outs, extra)
        E.cnt += 1
        tok = (E.sem, E.cnt)
        E.add(fn, deps, (E.sem, 1))
        self._mark(tok, ins, outs)
        return tok

    def dma(self, q, out, in_, ins=(), outs=(), extra=(), slow=False):
        E = self.E[q]
        Q = self.dq[q]
        slot = Q["n"] % self.NDMA
        Q["n"] += 1
        deps = self._deps(ins, outs, extra)
        deps.append(Q["tok"][slot])
        sem = Q["sems"][slot]
        prev = Q["tok"][slot][1] if Q["tok"][slot] else 0
        tok = (sem, prev + 16)
        Q["tok"][slot] = tok
        E.add(lambda e: e.dma_start(out=out, in_=in_, allow_slow_non_contiguous=slow), deps, (sem, 16))
        self._mark(tok, ins, outs)
        return tok

    def all_tokens(self):
        toks = [(E.sem, E.cnt) for E in self.E.values() if E.cnt > 0]
        for Q in self.dq.values():
            toks.extend(t for t in Q["tok"] if t is not None)
        return toks

    def barrier(self):
        toks = self.all_tokens()
        for E in self.E.values():
            E.add(None, toks, None)

    def flush(self, name=None):
        nc = self.nc

        def replay(E):
            def f(e):
                for waits, fn, inc in E.ops:
                    for sem, v in waits:
                        e.wait_ge(sem, v)
                    if fn is not None:
                        ins = fn(e)
                        if inc is not None:
                            ins.then_inc(inc[0], inc[1])
                E.ops = []
            return f

        with nc.Block() as block:
            block.sync(replay(self.E["sp"]))
            block.scalar(replay(self.E["act"]))
            block.vector(replay(self.E["dve"]))
            block.gpsimd(replay(self.E["pool"]))
            block.tensor(replay(self.E["pe"]))


class Rot:
    def __init__(self, aps):
        self.items = [(a, Tile()) for a in aps]
        self.i = 0

    def next(self):
        it = self.items[self.i % len(self.items)]
        self.i += 1
        return it


class Prog:
    def __init__(self, dbg=None):
        self.dbg = dbg or {}
        self.nc = bass.Bass("TRN2", target_bir_lowering=False)
        self.st = ExitStack()
        self.R = Rec(self.nc, self.st)
        self.declare_io()

    def din(self, name, shape, dt=F32):
        if name in self.dbg.get("shrink", ()):
            return None
        return self.nc.dram_tensor(name, list(shape), dt, kind="ExternalInput").ap()

    def dout(self, name, shape, dt=F32):
        if name in self.dbg.get("shrink", ()):
            return self.nc.dram_tensor(name, list(shape), dt, kind="Internal").ap()
        return self.nc.dram_tensor(name, list(shape), dt, kind="ExternalOutput").ap()

    def dscr(self, name, shape, dt=F32):
        kind = "ExternalOutput" if name in self.dbg.get("dump", ()) else "Internal"
        return self.nc.dram_tensor(name, list(shape), dt, kind=kind).ap()

    def sb(self, st, name, shape, dt=F32):
        self.uid = getattr(self, "uid", 0) + 1
        return st.enter_context(self.nc.sbuf_tensor("%s_%d" % (name, self.uid), list(shape), dt))

    def declare_io(self):
        L = DEPTH
        self.x_p = self.din("x_prompt", [T, D])
        self.x_s = self.din("x_sample", [TS, D])
        self.c_sb_k = self.din("cache_sb_k", [L, T, MW])
        self.c_sb_v = self.din("cache_sb_v", [L, T, MW])
        self.c_bd_k = self.din("cache_band_k", [L, BAND, MW])
        self.c_bd_v = self.din("cache_band_v", [L, BAND, MW])
        self.st_re = self.din("state_ssm_re", [L, G, SN])
        self.st_im = self.din("state_ssm_im", [L, G, SN])
        self.norm_g = self.din("norm_g", [L, D])
        self.w_in = self.din("w_in", [L, D, NIN])
        self.a_re = self.din("ssm_a_re", [L, G, SN])
        self.a_im = self.din("ssm_a_im", [L, G, SN])
        self.log_dt = self.din("ssm_log_dt", [L, G])
        self.b_re = self.din("ssm_b_re", [L, G, SN, SP_])
        self.b_im = self.din("ssm_b_im", [L, G, SN, SP_])
        self.c_re = self.din("ssm_c_re", [L, G, SP_, SN])
        self.c_im = self.din("ssm_c_im", [L, G, SP_, SN])
        self.ssm_d = self.din("ssm_d", [L, MW])
        self.w_glu = self.din("w_glu", [L, MW, MW])
        self.b_glu = self.din("b_glu", [L, MW])
        self.qn_g = self.din("q_norm_g", [L, DH])
        self.kn_g = self.din("k_norm_g", [L, DH])
        self.rel_bias = self.din("rel_bias", [L, NH, 257])
        self.w_br = [self.din("w_br_a", [L, MW, D]), self.din("w_br_b", [L, MW, D]), self.din("w_br_c", [L, MW, D])]
        self.gate_b = self.din("gate_b", [L, 3 * D])
        self.w_out = self.din("w_out", [L, D, D])
        self.y_p = self.dout("y_prompt", [T, D])
        self.y_s = self.dout("y_sample", [TS, D])
        self.p_sb_k = self.dout("p_sb_k", [L, T, MW])
        self.p_sb_v = self.dout("p_sb_v", [L, T, MW])
        self.p_bd_k = self.dout("p_band_k", [L, BAND, MW])
        self.p_bd_v = self.dout("p_band_v", [L, BAND, MW])
        self.p_ss_re = self.dout("p_ssm_re", [L, G, SN])
        self.p_ss_im = self.dout("p_ssm_im", [L, G, SN])
        self.s_sb_k = self.dout("s_sb_k", [L, TS, MW])
        self.s_sb_v = self.dout("s_sb_v", [L, TS, MW])
        self.s_bd_k = self.dout("s_band_k", [L, TS, MW])
        self.s_bd_v = self.dout("s_band_v", [L, TS, MW])
        self.s_ss_re = self.dout("s_ssm_re", [L, G, SN])
        self.s_ss_im = self.dout("s_ssm_im", [L, G, SN])
        self.y1 = self.dscr("y1", [NT, D])
        self.uT = self.dscr("uT", [MW, NT])
        self.szT = [self.dscr("sz%dT" % i, [MW, NT]) for i in range(3)]
        self.qcT32 = self.dscr("qcT32", [MW, NT])
        self.gT = self.dscr("gT", [3 * D, NT])
        self.qb32 = self.dscr("qb32", [NT, MW])
        self.kb = self.dscr("kb", [NT, MW])
        self.vb = self.dscr("vb", [NT, MW])
        self.yT = [self.dscr("y%dT" % i, [MW, NT], BF16) for i in range(3)]
        self.mixT = self.dscr("mixT", [D, NT], BF16)
        self.yaT32 = self.dscr("yaT32", [MW, NT])

    def x_rows(self, l, tt):
        if l == 0:
            return self.x_p[tt * 128:(tt + 1) * 128, :] if tt < 16 else self.x_s[:, :]
        return self.y1[tt * 128:tt * 128 + trows(tt), :]

    def y_rows(self, l, tt):
        if l == DEPTH - 1:
            return self.y_p[tt * 128:(tt + 1) * 128, :] if tt < 16 else self.y_s[:, :]
        return self.y1[tt * 128:tt * 128 + trows(tt), :]

    def setup_consts(self):
        nc, R, st = self.nc, self.R, self.st
        self.ident = self.sb(st, "ident", [128, 128])
        self.identb = self.sb(st, "identb", [128, 128], BF16)
        self.t_ident = Tile()
        self.ps = [st.enter_context(nc.psum_tensor("ps%d" % i, [128, 512], F32)) for i in range(8)]
        self.psr = Rot([p for p in self.ps])
        ident, identb = self.ident, self.identb
        R.op("pool", lambda e: e.memset(ident[:], 0.0), outs=[self.t_ident])
        R.op("pool", lambda e: e.affine_select(out=ident[:], in_=ident[:], pattern=[[-1, 128]],
                                                compare_op=ALU.not_equal, fill=1.0, base=0,
                                                channel_multiplier=1), outs=[self.t_ident])
        R.op("pool", lambda e: e.tensor_copy(out=identb[:], in_=ident[:]), ins=[self.t_ident], outs=[self.t_ident])
        self.cst = self.sb(st, "cst", [128, 4])
        cst = self.cst
        self.eps_col = cst[:, 0:1]
        R.op("pool", lambda e: e.memset(cst[:, 0:1], EPS), outs=[self.t_ident])
        R.op("pool", lambda e: e.memset(cst[:, 1:2], 1.0), outs=[self.t_ident])
        R.op("pool", lambda e: e.memset(cst[:, 2:3], 0.0), outs=[self.t_ident])
        R.barrier()
        R.flush()

    def load_cols(self, st, name, src2d, n):
        R = self.R
        tmp = self.sb(st, name + "_rows", [128, 128])
        dst = self.sb(st, name, [128, n])
        t_tmp, t_dst = Tile(), Tile()
        R.dma("sp", tmp[:n, :], src2d, outs=[t_tmp])
        ps, t_ps = self.psr.next()
        ident = self.ident
        R.op("pe", lambda e: e.matmul(ps[:, :n], tmp[:n, :], ident[:n, :n], start=True, stop=True),
             ins=[t_tmp], outs=[t_ps])
        R.op("dve", lambda e: e.tensor_copy(out=dst[:, :n], in_=ps[:, :n]), ins=[t_ps], outs=[t_dst])
        return dst, t_dst

    def phase_norm(self, st, l, actT):
        R = self.R
        ready = []
        xts = Rot([self.sb(st, "n_xt%d" % i, [128, D]) for i in range(2)])
        junk = self.sb(st, "n_junk", [128, D], BF16)
        t_junk = Tile()
        gcol, t_g = self.load_cols(st, "n_gcol", self.norm_g[l].rearrange("(k p) -> k p", p=128), KT)
        small = Rot([self.sb(st, "n_sm%d" % i, [128, 4]) for i in range(2)])
        dg = Rot([self.sb(st, "n_dg%d" % i, [128, 128]) for i in range(2)])
        ident = self.ident
        for tt in range(NTT):
            r = trows(tt)
            xt, t_x = xts.next()
            R.dma("sp", xt[:r, :], self.x_rows(l, tt), outs=[t_x])
            sm, t_sm = small.next()
            R.op("act", lambda e, xt=xt, sm=sm, r=r: e.activation(out=junk[:r, :], in_=xt[:r, :], func=AF.Square,
                                                                 accum_out=sm[:r, 0:1]),
                 ins=[t_x], outs=[t_junk, t_sm])
            R.op("act", lambda e, sm=sm, r=r: e.activation(out=sm[:r, 1:2], in_=sm[:r, 0:1], func=AF.Sqrt,
                                                           scale=1.0 / D, bias=self.eps_col[:r, :]),
                 ins=[t_sm], outs=[t_sm])
            R.op("dve", lambda e, sm=sm, r=r: e.reciprocal(out=sm[:r, 2:3], in_=sm[:r, 1:2]), ins=[t_sm], outs=[t_sm])
            d, t_d = dg.next()
            R.op("dve", lambda e, d=d, sm=sm, r=r: e.tensor_scalar(out=d[:r, :r], in0=ident[:r, :r], scalar1=sm[:r, 2:3],
                                                                  scalar2=None, op0=ALU.mult),
                 ins=[t_sm, self.t_ident], outs=[t_d])
            for kg in range(KT // 4):
                ps, t_ps = self.psr.next()

                def mm(e, xt=xt, d=d, ps=ps, kg=kg, r=r):
                    for j in range(4):
                        k = kg * 4 + j
                        ins = e.matmul(ps[:, j * 128:j * 128 + r], xt[:r, k * 128:(k + 1) * 128], d[:r, :r],
                                       start=True, stop=True)
                    return ins
                R.op("pe", mm, ins=[t_x, t_d], outs=[t_ps])
                for j in range(4):
                    k = kg * 4 + j
                    dst = actT[:, k, tt * 128:tt * 128 + r]
                    if True:
                        ready.append(R.op("dve", lambda e, ps=ps, j=j, k=k, dst=dst, r=r: e.tensor_scalar(
                            out=dst, in0=ps[:, j * 128:j * 128 + r], scalar1=gcol[:, k:k + 1], scalar2=None,
                            op0=ALU.mult), ins=[t_ps, t_g]))
        return ready

    def load_slab(self, slabs, W, kt, c0, ncol):
        slab, t_slab = slabs.next()
        self.R.dma("pool", slab[:, :kt, :ncol], W[:, c0:c0 + ncol].rearrange("(k p) n -> p k n", p=128),
                   outs=[t_slab])
        return slab, t_slab

    def mm_F(self, slab, t_slab, cbl, actT, kt, ch, ready):
        ps, t_ps = self.psr.next()

        def mm(e):
            for k in range(kt):
                ins = e.matmul(ps[:, :CH], slab[:, k, cbl * 128:(cbl + 1) * 128], actT[:, k, ch * CH:(ch + 1) * CH],
                               start=(k == 0), stop=(k == kt - 1))
            return ins
        self.R.op("pe", mm, ins=[t_slab], outs=[t_ps], extra=ready)
        return ps, t_ps

    def mm_T(self, slab, t_slab, ncol, actT, kt, tt, ready):
        ps, t_ps = self.psr.next()
        r = trows(tt)

        def mm(e):
            for k in range(kt):
                ins = e.matmul(ps[:r, :ncol], actT[:, k, tt * 128:tt * 128 + r], slab[:, k, :ncol],
                               start=(k == 0), stop=(k == kt - 1))
            return ins
        self.R.op("pe", mm, ins=[t_slab], outs=[t_ps], extra=ready)
        return ps, t_ps

    def phase_gemm_in(self, st, l, actT, ready, slab_range=None):
        R = self.R
        W = self.w_in[l]
        slabs = Rot([self.sb(st, "g_slab%d" % i, [128, KT, 256], BF16) for i in range(2)])
        stF = Rot([self.sb(st, "g_stF%d" % i, [128, NT]) for i in range(2)])
        stT = Rot([self.sb(st, "g_stT%d" % i, [128, 256]) for i in range(4)])
        gb, t_c = self.load_cols(st, "g_gateb", self.gate_b[l].rearrange("(j p) -> j p", p=128), 96)
        qg = self.sb(st, "g_qg", [128, DH])
        kg = self.sb(st, "g_kg", [128, DH])
        R.dma("sp", qg[:], self.qn_g[l].partition_broadcast(128), outs=[t_c])
        R.dma("sp", kg[:], self.kn_g[l].partition_broadcast(128), outs=[t_c])
        R.op("dve", lambda e: e.tensor_scalar(out=qg[:], in0=qg[:], scalar1=SCALE, scalar2=None, op0=ALU.mult),
             ins=[t_c], outs=[t_c])
        small = Rot([self.sb(st, "g_sm%d" % i, [128, 8]) for i in range(4)])
        junk = self.sb(st, "g_junk", [128, 128], BF16)
        t_junk = Tile()

        regions = ["u", "z0", "qb", "kb", "vb", "z1", "qc", "kc", "vc", "z2"]
        nsl = NIN // 256
        for s in (slab_range if slab_range is not None else range(nsl)):
            c0 = s * 256
            reg = regions[c0 // MW] if c0 < 10 * MW else "g"
            rc = c0 % MW if reg != "g" else c0 - 10 * MW
            slab, t_slab = self.load_slab(slabs, W, KT, c0, 256)
            if reg in ("u", "z0", "z1", "z2", "qc", "g"):
                for cbl in range(2):
                    row0 = rc + cbl * 128
                    stg, t_stg = stF.next()
                    for ch in range(NCH):
                        ps, t_ps = self.mm_F(slab, t_slab, cbl, actT, KT, ch, ready)
                        dst = stg[:, ch * CH:(ch + 1) * CH]
                        if reg == "u":
                            R.op("dve", lambda e, dst=dst, ps=ps: e.tensor_copy(out=dst, in_=ps[:, :CH]),
                                 ins=[t_ps], outs=[t_stg])
                        elif reg == "qc":
                            R.op("dve", lambda e, dst=dst, ps=ps: e.tensor_copy(out=dst, in_=ps[:, :CH]),
                                 ins=[t_ps], outs=[t_stg])
                        elif reg == "g":
                            j = row0 // 128
                            R.op("act", lambda e, dst=dst, ps=ps, j=j: e.activation(
                                out=dst, in_=ps[:, :CH], func=AF.Sigmoid, bias=gb[:, j:j + 1]),
                                ins=[t_ps, t_c], outs=[t_stg])
                        else:
                            R.op("act", lambda e, dst=dst, ps=ps: e.activation(out=dst, in_=ps[:, :CH], func=AF.Silu),
                                 ins=[t_ps], outs=[t_stg])
                    if reg == "u":
                        dram = self.uT[row0:row0 + 128, :]
                    elif reg == "qc":
                        dram = self.qcT32[row0:row0 + 128, :]
                    elif reg == "g":
                        dram = self.gT[row0:row0 + 128, :]
                    else:
                        dram = self.szT[int(reg[1])][row0:row0 + 128, :]
                    R.dma("sp", dram, stg[:, :], ins=[t_stg])
            else:
                for tt in range(NTT):
                    r = trows(tt)
                    ps, t_ps = self.mm_T(slab, t_slab, 256, actT, KT, tt, ready)
                    stg, t_stg = stT.next()
                    if reg in ("qb", "kb"):
                        sm, t_sm = small.next()
                        for h in range(2):
                            R.op("act", lambda e, ps=ps, sm=sm, h=h, r=r: e.activation(
                                out=junk[:r, :], in_=ps[:r, h * 128:(h + 1) * 128], func=AF.Square,
                                accum_out=sm[:r, h:h + 1]), ins=[t_ps], outs=[t_junk, t_sm])
                        R.op("act", lambda e, sm=sm, r=r: e.activation(out=sm[:r, 2:4], in_=sm[:r, 0:2], func=AF.Sqrt,
                                                                       scale=1.0 / DH, bias=self.eps_col[:r, :]),
                             ins=[t_sm], outs=[t_sm])
                        R.op("dve", lambda e, sm=sm, r=r: e.reciprocal(out=sm[:r, 4:6], in_=sm[:r, 2:4]),
                             ins=[t_sm], outs=[t_sm])
                        gvec = qg if reg == "qb" else kg
                        for h in range(2):
                            R.op("dve", lambda e, ps=ps, sm=sm, h=h, r=r, stg=stg, gvec=gvec: e.scalar_tensor_tensor(
                                out=stg[:r, h * 128:(h + 1) * 128], in0=ps[:r, h * 128:(h + 1) * 128],
                                scalar=sm[:r, 4 + h:5 + h], in1=gvec[:r, :], op0=ALU.mult, op1=ALU.mult),
                                ins=[t_ps, t_sm, t_c], outs=[t_stg])
                    else:
                        R.op("dve", lambda e, ps=ps, stg=stg, r=r: e.tensor_copy(out=stg[:r, :], in_=ps[:r, :256]),
                             ins=[t_ps], outs=[t_stg])
                    cs = slice(rc, rc + 256)
                    tok0 = tt * 128
                    if reg == "qb":
                        R.dma("sp", self.qb32[tok0:tok0 + r, cs], stg[:r, :], ins=[t_stg])
                    elif reg in ("kb", "vb"):
                        scr = self.kb if reg == "kb" else self.vb
                        R.dma("sp", scr[tok0:tok0 + r, cs], stg[:r, :], ins=[t_stg])
                        pout = self.p_bd_k if reg == "kb" else self.p_bd_v
                        sout = self.s_bd_k if reg == "kb" else self.s_bd_v
                        if tt == 16:
                            R.dma("sp", sout[l, :, cs], stg[:r, :], ins=[t_stg])
                        elif tok0 >= T - BAND:
                            o0 = tok0 - (T - BAND)
                            R.dma("sp", pout[l, o0:o0 + 128, cs], stg[:r, :], ins=[t_stg])
                    else:
                        pout = self.p_sb_k if reg == "kc" else self.p_sb_v
                        sout = self.s_sb_k if reg == "kc" else self.s_sb_v
                        if tt == 16:
                            R.dma("sp", sout[l, :, cs], stg[:r, :], ins=[t_stg])
                        else:
                            R.dma("sp", pout[l, tok0:tok0 + 128, cs], stg[:r, :], ins=[t_stg])

    def run_phase(self, fn):
        with ExitStack() as st:
            fn(st)
            self.R.barrier()
            self.R.flush()

    def setup_masks(self):
        R, st = self.R, self.st
        self.m01 = self.sb(st, "m01", [128, 128])
        self.mneg = self.sb(st, "mneg", [128, 128])
        self.ones_col = self.cst[:, 1:2]
        m01, mneg = self.m01, self.mneg
        t = Tile()
        R.op("pool", lambda e: e.memset(m01[:], 1.0), outs=[t])
        R.op("pool", lambda e: e.affine_select(out=m01[:], in_=m01[:], pattern=[[-1, 128]], compare_op=ALU.is_gt,
                                                fill=0.0, base=0, channel_multiplier=1), outs=[t])
        R.op("pool", lambda e: e.memset(mneg[:], 0.0), outs=[t])
        R.op("pool", lambda e: e.affine_select(out=mneg[:], in_=mneg[:], pattern=[[-1, 128]], compare_op=ALU.is_gt,
                                                fill=-1e30, base=0, channel_multiplier=1), outs=[t])
        R.barrier()
        R.flush()

    def transpose_bf16(self, src_blocks, dst_fn, ins, outs_tile, evac_eng="act"):
        R = self.R
        identb = self.identb
        i = 0
        toks = []
        while i < len(src_blocks):
            grp = src_blocks[i:i + 4]
            ps, t_ps = self.psr.next()
            psb = ps[:].bitcast(BF16)

            def tr(e, grp=grp, psb=psb):
                for j, (ap, nk, nq) in enumerate(grp):
                    ins_ = e.transpose(psb[:nq, j * 128:j * 128 + nk], ap, identb[:nk, :nk])
                return ins_
            R.op("pe", tr, ins=ins, outs=[t_ps])
            for j, (ap, nk, nq) in enumerate(grp):
                dst = dst_fn(i + j)
                if evac_eng == "act":
                    toks.append(R.op("act", lambda e, dst=dst, psb=psb, j=j, nk=nk, nq=nq: e.activation(
                        out=dst, in_=psb[:nq, j * 128:j * 128 + nk], func=AF.Copy), ins=[t_ps], outs=[outs_tile]))
                else:
                    toks.append(R.op("dve", lambda e, dst=dst, psb=psb, j=j, nk=nk, nq=nq: e.tensor_copy(
                        out=dst, in_=psb[:nq, j * 128:j * 128 + nk]), ins=[t_ps], outs=[outs_tile]))
            i += 4
        return toks

    def phase_sb(self, st, l, heads=range(NH), qblocks=range(16), do_sample=True):
        R = self.R
        SM = T + TS
        qT = self.sb(st, "c_qT", [128, NT], BF16)
        kld = self.sb(st, "c_kld", [128, 16, 128], BF16)
        kT = self.sb(st, "c_kT", [128, T], BF16)
        kld2 = self.sb(st, "c_kld2", [128, 16, 128], BF16)
        t_kld2 = Tile()
        kTs = self.sb(st, "c_kTs", [128, SM], BF16)
        ksn = self.sb(st, "c_ksn", [TS, 128], BF16)
        V = self.sb(st, "c_V", [128, 16, 128], BF16)
        Vc = self.sb(st, "c_Vc", [128, 16, 128], BF16)
        Vn = self.sb(st, "c_Vn", [TS, 128], BF16)
        sz = self.sb(st, "c_sz", [128, NT])
        Y = self.sb(st, "c_Y", [128, NT], BF16)
        t_q, t_kld, t_kT, t_kTs, t_ksn, t_V, t_Vc, t_Vn, t_sz, t_Y = [Tile() for _ in range(10)]
        Eb = Rot([self.sb(st, "c_E%d" % i, [128, SM]) for i in range(3)])
        NLb = Rot([self.sb(st, "c_NL%d" % i, [128, SM]) for i in range(4)])
        Gb = Rot([self.sb(st, "c_G%d" % i, [128, SM]) for i in range(2)])
        Wb = Rot([self.sb(st, "c_W%d" % i, [128, SM], BF16) for i in range(2)])
        WTb = Rot([self.sb(st, "c_WT%d" % i, [128, 17, 128], BF16) for i in range(3)])
        ng = Rot([self.sb(st, "c_ng%d" % i, [128, 1]) for i in range(3)])
        identb = self.identb
        m01, mneg = self.m01, self.mneg
        ones_col = self.ones_col

        class Ctx:
            pass

        def st_z_el(desc):
            (h, qlo, r, kTa, S, vblocks, d0) = desc
            c = Ctx()
            c.qlo, c.r, c.S, c.vblocks, c.d0 = qlo, r, S, vblocks, d0
            c.E, c.t_E = Eb.next()
            c.pss = []
            nchunk = (S + 511) // 512
            for ci in range(nchunk):
                n = min(512, S - ci * 512)
                ps, t_ps = self.psr.next()
                R.op("pe", lambda e, ps=ps, ci=ci, n=n: e.matmul(ps[:r, :n], qT[:, qlo:qlo + r], kTa[:, ci * 512:ci * 512 + n],
                                                                 start=True, stop=True),
                     ins=[t_q, t_kT, t_kTs], outs=[t_ps])
                cs = slice(ci * 512, ci * 512 + n)
                E = c.E
                R.op("act", lambda e, ps=ps, n=n, cs=cs, E=E: e.activation(out=E[:r, cs], in_=ps[:r, :n], func=AF.Exp, scale=-SCALE),
                     ins=[t_ps], outs=[c.t_E])
                R.op("act", lambda e, cs=cs, E=E: e.activation(out=E[:r, cs], in_=E[:r, cs], func=AF.Ln, bias=ones_col[:r, :]),
                     ins=[c.t_E], outs=[c.t_E])
                c.pss.append((ps, t_ps, n, cs))
            return c

        def st_nlf(c):
            r, d0, E = c.r, c.d0, c.E
            c.NL, c.t_NL = NLb.next()
            NL = c.NL
            for (ps, t_ps, n, cs) in c.pss:
                R.op("dve", lambda e, ps=ps, n=n, cs=cs: e.scalar_tensor_tensor(
                    out=NL[:r, cs], in0=ps[:r, :n], scalar=SCALE, in1=E[:r, cs], op0=ALU.mult, op1=ALU.add),
                    ins=[t_ps, c.t_E], outs=[c.t_NL])
            R.op("dve", lambda e: e.tensor_tensor(out=NL[:r, d0:d0 + r], in0=NL[:r, d0:d0 + r], in1=m01[:r, :r], op=ALU.mult),
                 ins=[c.t_NL], outs=[c.t_NL])

        def st_scan_arg(c):
            r, S, d0, E, NL = c.r, c.S, c.d0, c.E, c.NL
            Gt, t_G = Gb.next()
            c.ngc, c.t_ng = ng.next()
            ngc = c.ngc
            R.op("dve", lambda e: e.tensor_tensor_scan(out=Gt[:r, :S], data0=ones_col[:r, :].to_broadcast([r, S]),
                                                       data1=NL[:r, :S], initial=0.0, op0=ALU.mult, op1=ALU.add),
                 ins=[c.t_NL], outs=[t_G])
            R.op("dve", lambda e: e.tensor_scalar(out=ngc[:r, :], in0=Gt[:r, S - 1:S], scalar1=-1.0, scalar2=None, op0=ALU.mult),
                 ins=[t_G], outs=[c.t_ng])
            R.op("pool", lambda e: e.tensor_tensor(out=NL[:r, :S], in0=Gt[:r, :S], in1=E[:r, :S], op=ALU.subtract),
                 ins=[t_G, c.t_E], outs=[c.t_NL])
            R.op("pool", lambda e: e.tensor_tensor(out=NL[:r, d0:d0 + r], in0=NL[:r, d0:d0 + r], in1=mneg[:r, :r], op=ALU.add),
                 ins=[c.t_NL], outs=[c.t_NL])

        def st_exp2(c):
            r, S, NL, ngc = c.r, c.S, c.NL, c.ngc
            c.W, c.t_W = Wb.next()
            W = c.W
            R.op("act", lambda e: e.activation(out=W[:r, :S], in_=NL[:r, :S], func=AF.Exp, bias=ngc[:r, :]),
                 ins=[c.t_NL, c.t_ng], outs=[c.t_W])

        def st_t_cp(c):
            r, S, W = c.r, c.S, c.W
            c.WT, c.t_WT = WTb.next()
            WT = c.WT
            nb = (S + 127) // 128
            c.nb = nb
            b0 = 0
            while b0 < nb:
                grp = list(range(b0, min(b0 + 4, nb)))
                ps, t_ps = self.psr.next()
                psb = ps[:].bitcast(BF16)

                def tr(e, grp=grp, psb=psb):
                    for j, b in enumerate(grp):
                        nk = min(128, S - b * 128)
                        ins_ = e.transpose(psb[:nk, j * 128:j * 128 + r], W[:r, b * 128:b * 128 + nk], identb[:r, :r])
                    return ins_
                R.op("pe", tr, ins=[c.t_W], outs=[t_ps])
                full = [b for b in grp if S - b * 128 >= 128]
                if full:
                    nf = len(full)
                    if r == 128:
                        R.op("act", lambda e, psb=psb, f0=full[0], nf=nf: e.activation(
                            out=WT[:, f0:f0 + nf, :], in_=psb[:, 0:nf * 128].rearrange("p (b q) -> p b q", q=128), func=AF.Copy),
                            ins=[t_ps], outs=[c.t_WT])
                    else:
                        for j, b in enumerate(full):
                            R.op("act", lambda e, psb=psb, j=j, b=b: e.activation(
                                out=WT[:, b, :r], in_=psb[:, j * 128:j * 128 + r], func=AF.Copy), ins=[t_ps], outs=[c.t_WT])
                for j, b in enumerate(grp):
                    nk = min(128, S - b * 128)
                    if nk < 128:
                        R.op("act", lambda e, psb=psb, j=j, b=b, nk=nk: e.activation(
                            out=WT[:nk, b, :r], in_=psb[:nk, j * 128:j * 128 + r], func=AF.Copy), ins=[t_ps], outs=[c.t_WT])
                b0 += 4

        def st_pv_y(c):
            r, qlo, vblocks, WT, nb = c.r, c.qlo, c.vblocks, c.WT, c.nb
            ps, t_ps = self.psr.next()

            def pv(e):
                for b in range(nb):
                    vap, nk = vblocks[b]
                    ins_ = e.matmul(ps[:, :r], vap, WT[:nk, b, :r], start=(b == 0), stop=(b == nb - 1))
                return ins_
            R.op("pe", pv, ins=[c.t_WT, t_V, t_Vc, t_Vn], outs=[t_ps])
            R.op("dve", lambda e: e.tensor_tensor(out=Y[:, qlo:qlo + r], in0=ps[:, :r], in1=sz[:, qlo:qlo + r], op=ALU.mult),
                 ins=[t_ps, t_sz], outs=[t_Y])

        def run_pipeline(descs):
            n = len(descs)
            ctx = {}
            for j in range(-2, n + 1):
                if 0 <= j - 1 < n:
                    st_pv_y(ctx[j - 1])
                    del ctx[j - 1]
                if 0 <= j < n:
                    st_exp2(ctx[j])
                if 0 <= j + 2 < n:
                    ctx[j + 2] = st_z_el(descs[j + 2])
                if 0 <= j < n:
                    st_t_cp(ctx[j])
                if 0 <= j + 1 < n:
                    st_scan_arg(ctx[j + 1])
                if 0 <= j + 2 < n:
                    st_nlf(ctx[j + 2])

        for h in heads:
            hs = slice(h * 128, (h + 1) * 128)
            if len(qblocks) < 16:
                R.op("pool", lambda e: e.memset(Y[:, :], 0.0), outs=[t_Y])
            R.dma("pool", qT[:, :], self.qcT32[hs, :], outs=[t_q])
            R.dma("sp", sz[:, :], self.szT[2][hs, :], outs=[t_sz])
            R.dma("pool", kld[:, :, :], self.p_sb_k[l][:, hs].rearrange("(b p) d -> p b d", p=128), outs=[t_kld])
            R.dma("pool", V[:, :, :], self.p_sb_v[l][:, hs].rearrange("(b p) d -> p b d", p=128), outs=[t_V])
            self.transpose_bf16([(kld[:, b, :], 128, 128) for b in range(16)],
                                lambda i: kT[:, i * 128:(i + 1) * 128], ins=[t_kld], outs_tile=t_kT, evac_eng="dve")
            if do_sample:
                R.dma("pool", kld2[:, :, :], self.c_sb_k[l][:, hs].rearrange("(b p) d -> p b d", p=128), outs=[t_kld2])
                R.dma("pool", Vc[:, :, :], self.c_sb_v[l][:, hs].rearrange("(b p) d -> p b d", p=128), outs=[t_Vc])
                R.dma("pool", ksn[:, :], self.s_sb_k[l][:, hs], outs=[t_ksn])
                R.dma("pool", Vn[:, :], self.s_sb_v[l][:, hs], outs=[t_Vn])
                self.transpose_bf16([(kld2[:, b, :], 128, 128) for b in range(16)],
                                    lambda i: kTs[:, i * 128:(i + 1) * 128], ins=[t_kld2], outs_tile=t_kTs, evac_eng="dve")
                self.transpose_bf16([(ksn[:, :], TS, 128)], lambda i: kTs[:, T:T + TS], ins=[t_ksn], outs_tile=t_kTs,
                                    evac_eng="dve")
            descs = []
            for qb in qblocks:
                S = 128 * (qb + 1)
                descs.append((h, qb * 128, 128, kT, S, [(V[:, b, :], 128) for b in range(qb + 1)], qb * 128))
            if do_sample:
                descs.append((h, T, TS, kTs, SM, [(Vc[:, b, :], 128) for b in range(16)] + [(Vn[:, :], TS)], T))
            run_pipeline(descs)
            R.dma("sp", self.yT[2][hs, :], Y[:, :], ins=[t_Y])

    def setup_band_consts(self):
        R, st = self.R, self.st
        self.Jb = self.sb(st, "Jb", [128, 128], BF16)
        self.J32 = self.sb(st, "J32", [128, 128])
        self.ones32 = self.sb(st, "ones32", [128, 128])
        J32, Jb, ones32 = self.J32, self.Jb, self.ones32
        t = Tile()
        R.op("pool", lambda e: e.memset(J32[:], 0.0), outs=[t])
        R.op("pool", lambda e: e.affine_select(out=J32[:], in_=J32[:], pattern=[[1, 128]], compare_op=ALU.not_equal,
                                                fill=1.0, base=-127, channel_multiplier=1), outs=[t])
        R.op("pool", lambda e: e.tensor_copy(out=Jb[:], in_=J32[:]), ins=[t], outs=[t])
        R.op("pool", lambda e: e.memset(ones32[:], 1.0), outs=[t])
        self.ext = self.dscr("ext_bias", [NH, 768])
        R.barrier()
        R.flush()

    def phase_band(self, st, l, heads=range(NH), mblocks=range(16), do_sample=True):
        R = self.R
        ident, identb, Jb, ones32 = self.ident, self.identb, self.Jb, self.ones32
        rb = self.sb(st, "b_rb", [NH, 257])
        exs = self.sb(st, "b_exs", [NH, 768])
        t_rb, t_ext = Tile(), Tile()
        R.dma("sp", rb[:, :], self.rel_bias[l], outs=[t_rb])
        R.op("dve", lambda e: e.tensor_copy(out=exs[:, 0:256], in_=rb[:, 1:257]), ins=[t_rb], outs=[t_ext])
        R.op("dve", lambda e: e.tensor_copy(out=exs[:, 256:768], in_=rb[:, 256:257].to_broadcast([NH, 512])),
             ins=[t_rb], outs=[t_ext])
        t_extd = Tile()
        R.dma("sp", self.ext[:, :], exs[:, :], ins=[t_ext], outs=[t_extd])

        HK = self.sb(st, "b_HK", [128, 5, 128], BF16)
        qld = self.sb(st, "b_qld", [128, 16, 128], BF16)
        qls = self.sb(st, "b_qls", [TS, 128], BF16)
        kld = self.sb(st, "b_kld", [128, 16, 128], BF16)
        kls = self.sb(st, "b_kls", [TS, 128], BF16)
        kcl = self.sb(st, "b_kcl", [128, 4, 128], BF16)
        qT = self.sb(st, "b_qT", [128, NT], BF16)
        kT = self.sb(st, "b_kT", [128, NT], BF16)
        kcT = self.sb(st, "b_kcT", [128, BAND], BF16)
        V = self.sb(st, "b_V", [128, 16, 128], BF16)
        Vn = self.sb(st, "b_Vn", [TS, 128], BF16)
        Vc = self.sb(st, "b_Vc", [128, 4, 128], BF16)
        sz = self.sb(st, "b_sz", [128, NT])
        Y = self.sb(st, "b_Y", [128, NT], BF16)
        t_HK, t_qld, t_kld, t_kcl, t_qT, t_kT, t_kcT, t_V, t_sz, t_Y = [Tile() for _ in range(10)]
        Pb = Rot([self.sb(st, "b_P%d" % i, [128, 640], BF16) for i in range(3)])
        PTb = Rot([self.sb(st, "b_PT%d" % i, [128, 5, 128], BF16) for i in range(3)])
        smb = Rot([self.sb(st, "b_sm%d" % i, [128, 8]) for i in range(3)])
        D32b = Rot([self.sb(st, "b_D%d" % i, [128, 128]) for i in range(3)])
        tmpb = Rot([self.sb(st, "b_tmp%d" % i, [128, 128]) for i in range(2)])
        for (P_, tP) in Pb.items:
            R.op("pool", lambda e, P_=P_: e.memset(P_[:, :], 0.0), outs=[tP])

        class Ctx:
            pass

        def b_qk(desc):
            (qlo, r, segs, vblocks) = desc
            c = Ctx()
            c.qlo, c.r, c.segs, c.vblocks = qlo, r, segs, vblocks
            c.psA, c.t_A = self.psr.next()
            c.psB, c.t_B = self.psr.next()
            psA, psB, t_A, t_B = c.psA, c.psB, c.t_A, c.t_B
            pss = [(psA, t_A), (psB, t_B)]
            for (pi, c0, n, kap, kc0) in segs:
                ps, t_ps = pss[pi]
                cc = kc0 // 128

                def mm(e, ps=ps, c0=c0, n=n, kap=kap, cc=cc):
                    e.matmul(ps[:r, c0:c0 + n], qT[:, qlo:qlo + r], kap, start=True, stop=False)
                    return e.matmul(ps[:r, c0:c0 + n], HK[:, 4 - cc, 0:r], Jb[:, 0:n], start=False, stop=True)
                R.op("pe", mm, ins=[t_qT, t_kT, t_kcT, t_HK], outs=[t_ps])
            c.sm, c.t_sm = smb.next()
            sm, t_sm = c.sm, c.t_sm
            nA = sum(n for (pi, c0, n, kap, kc0) in segs if pi == 0)
            a0 = min([c0 for (pi, c0, n, kap, kc0) in segs if pi == 0] + [512])
            nB = sum(n for (pi, c0, n, kap, kc0) in segs if pi == 1)
            R.op("dve", lambda e: e.tensor_reduce(out=sm[:r, 1:2], in_=psB[:r, 0:nB], axis=AX.X, op=ALU.max),
                 ins=[t_B], outs=[t_sm])
            if nA > 0:
                R.op("dve", lambda e: e.tensor_reduce(out=sm[:r, 0:1], in_=psA[:r, a0:a0 + nA], axis=AX.X, op=ALU.max),
                     ins=[t_A], outs=[t_sm])
                R.op("dve", lambda e: e.tensor_scalar(out=sm[:r, 2:3], in0=sm[:r, 0:1], scalar1=sm[:r, 1:2], scalar2=-1.0,
                                                      op0=ALU.max, op1=ALU.mult), ins=[t_sm], outs=[t_sm])
            else:
                R.op("dve", lambda e: e.tensor_scalar(out=sm[:r, 2:3], in0=sm[:r, 1:2], scalar1=-1.0, scalar2=None,
                                                      op0=ALU.mult), ins=[t_sm], outs=[t_sm])
            return c

        def b_exp(c):
            r, segs, sm, t_sm, psA, psB, t_A, t_B = c.r, c.segs, c.sm, c.t_sm, c.psA, c.psB, c.t_A, c.t_B
            lo = segs[0][4]
            c.P, c.t_P = Pb.next()
            P_, t_P = c.P, c.t_P
            if r == 128:
                halves = [(0, 64, lo, 576), (64, 128, max(lo, 64), 640)]
            else:
                halves = [(0, r, 0, BAND + TS)]
            ncol = 3
            for (p0, p1, v0, v1) in halves:
                if v0 < 512:
                    e1 = min(v1, 512)
                    R.op("act", lambda e, p0=p0, p1=p1, v0=v0, e1=e1, ncol=ncol: e.activation(
                        out=P_[p0:p1, v0:e1], in_=psA[p0:p1, v0:e1], func=AF.Exp, bias=sm[p0:p1, 2:3],
                        accum_out=sm[p0:p1, ncol:ncol + 1]), ins=[t_A, t_sm], outs=[t_P, t_sm])
                else:
                    R.op("pool", lambda e, p0=p0, p1=p1, ncol=ncol: e.memset(sm[p0:p1, ncol:ncol + 1], 0.0), outs=[t_sm])
                R.op("act", lambda e, p0=p0, p1=p1, v1=v1, ncol=ncol: e.activation(
                    out=P_[p0:p1, 512:v1], in_=psB[p0:p1, 0:v1 - 512], func=AF.Exp, bias=sm[p0:p1, 2:3],
                    accum_out=sm[p0:p1, ncol + 1:ncol + 2]), ins=[t_B, t_sm], outs=[t_P, t_sm])
            R.op("dve", lambda e: e.tensor_tensor(out=sm[:r, 5:6], in0=sm[:r, 3:4], in1=sm[:r, 4:5], op=ALU.add),
                 ins=[t_sm], outs=[t_sm])
            R.op("dve", lambda e: e.reciprocal(out=sm[:r, 6:7], in_=sm[:r, 5:6]), ins=[t_sm], outs=[t_sm])
            c.D32, c.t_D = D32b.next()
            D32 = c.D32
            R.op("dve", lambda e: e.tensor_scalar(out=D32[:r, :r], in0=ident[:r, :r], scalar1=sm[:r, 6:7], scalar2=None,
                                                  op0=ALU.mult), ins=[t_sm], outs=[c.t_D])

        def b_t_cp(c):
            r, vblocks, P_ = c.r, c.vblocks, c.P
            c.PT, c.t_PT = PTb.next()
            PT = c.PT
            nb = len(vblocks)
            b0 = 0
            while b0 < nb:
                grp = list(range(b0, min(b0 + 4, nb)))
                ps, t_ps = self.psr.next()
                psb = ps[:].bitcast(BF16)

                def tr(e, grp=grp, psb=psb):
                    for j, b in enumerate(grp):
                        (vap, nk, kc0) = vblocks[b]
                        ins_ = e.transpose(psb[:nk, j * 128:j * 128 + r], P_[:r, kc0:kc0 + nk], identb[:r, :r])
                    return ins_
                R.op("pe", tr, ins=[c.t_P], outs=[t_ps])
                full = [b for b in grp if vblocks[b][1] == 128]
                if full and r == 128:
                    nf = len(full)
                    R.op("act", lambda e, psb=psb, f0=full[0], nf=nf: e.activation(
                        out=PT[:, f0:f0 + nf, :], in_=psb[:, 0:nf * 128].rearrange("p (b q) -> p b q", q=128), func=AF.Copy),
                        ins=[t_ps], outs=[c.t_PT])
                else:
                    for j, b in enumerate(grp):
                        if vblocks[b][1] == 128:
                            R.op("act", lambda e, psb=psb, j=j, b=b: e.activation(
                                out=PT[:, b, :r], in_=psb[:, j * 128:j * 128 + r], func=AF.Copy), ins=[t_ps], outs=[c.t_PT])
                for j, b in enumerate(grp):
                    nk = vblocks[b][1]
                    if nk < 128:
                        R.op("act", lambda e, psb=psb, j=j, b=b, nk=nk: e.activation(
                            out=PT[:nk, b, :r], in_=psb[:nk, j * 128:j * 128 + r], func=AF.Copy), ins=[t_ps], outs=[c.t_PT])
                b0 += 4

        def b_pv_y(c):
            r, qlo, vblocks, PT, D32 = c.r, c.qlo, c.vblocks, c.PT, c.D32
            pso, t_o = self.psr.next()
            nb = len(vblocks)

            def pv(e):
                for b, (vap, nk, kc0) in enumerate(vblocks):
                    ins_ = e.matmul(pso[:, :r], vap, PT[:nk, b, :r], start=(b == 0), stop=(b == nb - 1))
                return ins_
            R.op("pe", pv, ins=[c.t_PT, t_V], outs=[t_o])
            psr_, t_r = self.psr.next()
            R.op("pe", lambda e: e.matmul(psr_[:, :r], ones32[:r, :], D32[:r, :r], start=True, stop=True),
                 ins=[c.t_D], outs=[t_r])
            tmp, t_tmp = tmpb.next()
            R.op("dve", lambda e: e.tensor_tensor(out=tmp[:, :r], in0=pso[:, :r], in1=sz[:, qlo:qlo + r], op=ALU.mult),
                 ins=[t_o, t_sz], outs=[t_tmp])
            R.op("dve", lambda e: e.tensor_tensor(out=Y[:, qlo:qlo + r], in0=psr_[:, :r], in1=tmp[:, :r], op=ALU.mult),
                 ins=[t_r, t_tmp], outs=[t_Y])

        def run_pipeline(descs):
            n = len(descs)
            ctx = {}
            for j in range(-2, n + 1):
                if 0 <= j - 1 < n:
                    b_pv_y(ctx[j - 1])
                    del ctx[j - 1]
                if 0 <= j + 2 < n:
                    ctx[j + 2] = b_qk(descs[j + 2])
                if 0 <= j + 1 < n:
                    b_exp(ctx[j + 1])
                if 0 <= j < n:
                    b_t_cp(ctx[j])

        for h in heads:
            hs = slice(h * 128, (h + 1) * 128)
            descs = []
            if len(mblocks) < 16:
                R.op("pool", lambda e: e.memset(Y[:, :], 0.0), outs=[t_Y])
            R.dma("pool", HK[:, :, :], bass.AP(self.ext.tensor, self.ext[h, 0:1].offset, [[1, 128], [128, 5], [1, 128]]),
                  ins=[t_extd], outs=[t_HK])
            R.dma("pool", qld[:, :, :], self.qb32[0:T, hs].rearrange("(b p) d -> p b d", p=128), outs=[t_qld])
            R.dma("pool", qls[:, :], self.qb32[T:NT, hs], outs=[t_qld])
            R.dma("pool", kld[:, :, :], self.kb[0:T, hs].rearrange("(b p) d -> p b d", p=128), outs=[t_kld])
            R.dma("pool", kls[:, :], self.kb[T:NT, hs], outs=[t_kld])
            R.dma("pool", kcl[:, :, :], self.c_bd_k[l][:, hs].rearrange("(b p) d -> p b d", p=128), outs=[t_kcl])
            R.dma("pool", V[:, :, :], self.vb[0:T, hs].rearrange("(b p) d -> p b d", p=128), outs=[t_V])
            R.dma("pool", Vn[:, :], self.vb[T:NT, hs], outs=[t_V])
            R.dma("pool", Vc[:, :, :], self.c_bd_v[l][:, hs].rearrange("(b p) d -> p b d", p=128), outs=[t_V])
            R.dma("sp", sz[:, :], self.szT[1][hs, :], outs=[t_sz])
            self.transpose_bf16([(qld[:, b, :], 128, 128) for b in range(16)] + [(qls[:, :], TS, 128)],
                                lambda i: qT[:, i * 128:i * 128 + (128 if i < 16 else TS)], ins=[t_qld], outs_tile=t_qT,
                                evac_eng="dve")
            self.transpose_bf16([(kld[:, b, :], 128, 128) for b in range(16)] + [(kls[:, :], TS, 128)],
                                lambda i: kT[:, i * 128:i * 128 + (128 if i < 16 else TS)], ins=[t_kld], outs_tile=t_kT,
                                evac_eng="dve")
            self.transpose_bf16([(kcl[:, b, :], 128, 128) for b in range(4)],
                                lambda i: kcT[:, i * 128:(i + 1) * 128], ins=[t_kcl], outs_tile=t_kcT, evac_eng="dve")
            for m in mblocks:
                lo = max(0, 512 - 128 * m)
                w0 = 128 * m - 512
                segs, vbl = [], []
                for c in range(lo // 128, 5):
                    kc0 = c * 128
                    segs.append((0 if c < 4 else 1, kc0 if c < 4 else 0, 128, kT[:, w0 + kc0:w0 + kc0 + 128], kc0))
                    vbl.append((V[:, (w0 + kc0) // 128, :], 128, kc0))
                descs.append((m * 128, 128, segs, vbl))
            if do_sample:
                segs = [(0, c * 128, 128, kcT[:, c * 128:(c + 1) * 128], c * 128) for c in range(4)]
                segs.append((1, 0, TS, kT[:, T:T + TS], 512))
                vbl = [(Vc[:, c, :], 128, c * 128) for c in range(4)] + [(Vn[:, :], TS, 512)]
                descs.append((T, TS, segs, vbl))
            run_pipeline(descs)
            R.dma("sp", self.yT[1][hs, :], Y[:, :], ins=[t_Y])

    def phase_merge(self, st, l, slab_range=None):
        R = self.R
        yts = []
        t_y = Tile()
        for i in range(3):
            yt = self.sb(st, "f_y%d" % i, [128, 8, NT], BF16)
            R.dma("sp", yt[:, :, :], self.yT[i].rearrange("(k p) n -> p k n", p=128), outs=[t_y])
            yts.append(yt)
        slabs = [Rot([self.sb(st, "f_slab%d_%d" % (i, j), [128, 8, 256], BF16) for j in range(2)]) for i in range(3)]
        gts = [Rot([self.sb(st, "f_g%d_%d" % (i, j), [128, NT]) for j in range(2)]) for i in range(3)]
        accs = Rot([self.sb(st, "f_acc%d" % j, [128, CH]) for j in range(3)])
        tmps = Rot([self.sb(st, "f_tmp%d" % j, [128, CH]) for j in range(3)])
        stg = Rot([self.sb(st, "f_stg%d" % j, [128, NT], BF16) for j in range(2)])
        for s in (slab_range if slab_range is not None else range(D // 256)):
            c0 = s * 256
            sl = [self.load_slab(slabs[i], self.w_br[i][l], 8, c0, 256) for i in range(3)]
            for cbl in range(2):
                fb = c0 // 128 + cbl
                gs = []
                for i in range(3):
                    g, t_g = gts[i].next()
                    R.dma("sp", g[:, :], self.gT[i * D + fb * 128:i * D + (fb + 1) * 128, :], outs=[t_g])
                    gs.append((g, t_g))
                so, t_so = stg.next()
                for ch in range(NCH):
                    cs = slice(ch * CH, (ch + 1) * CH)
                    pss = []
                    for i in range(3):
                        slab, t_slab = sl[i]
                        pss.append(self.mm_F(slab, t_slab, cbl, yts[i], 8, ch, [t_y.w]))
                    acc, t_acc = accs.next()
                    tmp, t_tmp = tmps.next()
                    R.op("dve", lambda e, acc=acc, ps=pss[0][0], g=gs[0][0], cs=cs: e.tensor_tensor(
                        out=acc[:, :], in0=ps[:, :CH], in1=g[:, cs], op=ALU.mult), ins=[pss[0][1], gs[0][1]], outs=[t_acc])
                    R.op("dve", lambda e, tmp=tmp, ps=pss[1][0], g=gs[1][0], cs=cs: e.tensor_tensor(
                        out=tmp[:, :], in0=ps[:, :CH], in1=g[:, cs], op=ALU.mult), ins=[pss[1][1], gs[1][1]], outs=[t_tmp])
                    R.op("pool", lambda e, acc=acc, tmp=tmp: e.tensor_tensor(out=acc[:, :], in0=acc[:, :], in1=tmp[:, :], op=ALU.add),
                         ins=[t_tmp], outs=[t_acc])
                    R.op("dve", lambda e, tmp=tmp, ps=pss[2][0], g=gs[2][0], cs=cs: e.tensor_tensor(
                        out=tmp[:, :], in0=ps[:, :CH], in1=g[:, cs], op=ALU.mult), ins=[pss[2][1], gs[2][1]], outs=[t_tmp])
                    R.op("pool", lambda e, acc=acc, tmp=tmp, so=so, cs=cs: e.tensor_tensor(out=so[:, cs], in0=acc[:, :], in1=tmp[:, :],
                                                                                         op=ALU.add),
                         ins=[t_tmp, t_acc], outs=[t_so])
                R.dma("sp", self.mixT[fb * 128:(fb + 1) * 128, :], so[:, :], ins=[t_so])

    def phase_out(self, st, l, actT, slab_range=None):
        R = self.R
        t_a = Tile()
        R.dma("sp", actT[:, :, :], self.mixT.rearrange("(k p) n -> p k n", p=128), outs=[t_a])
        OC = 512
        slabs = Rot([self.sb(st, "o_slab%d" % i, [128, KT, OC], BF16) for i in range(2)])
        xts = Rot([self.sb(st, "o_x%d" % i, [128, OC]) for i in range(3)])
        for s in (slab_range if slab_range is not None else range(D // OC)):
            c0 = s * OC
            slab, t_slab = self.load_slab(slabs, self.w_out[l], KT, c0, OC)
            for tt in range(NTT):
                r = trows(tt)
                xt, t_x = xts.next()
                R.dma("sp", xt[:r, :], self.x_rows(l, tt)[:, c0:c0 + OC], outs=[t_x])
                ps, t_ps = self.mm_T(slab, t_slab, OC, actT, KT, tt, [t_a.w])
                R.op("dve", lambda e, xt=xt, ps=ps, r=r: e.tensor_tensor(out=xt[:r, :], in0=ps[:r, :OC], in1=xt[:r, :], op=ALU.add),
                     ins=[t_ps], outs=[t_x])
                R.dma("sp", self.y_rows(l, tt)[:, c0:c0 + OC], xt[:r, :], ins=[t_x])

    def setup_s5_consts(self, st=None):
        R = self.R
        st = st if st is not None else self.st
        t = Tile()
        self.I2 = self.sb(st, "s5_I2", [64, 32])
        self.SelAll = self.sb(st, "s5_Sel", [128, 64, 128], BF16)
        rm = self.sb(st, "s5_rm", [128, 8])
        ident, I2, SelAll = self.ident, self.I2, self.SelAll
        cst = self.cst
        R.op("pool", lambda e: e.memset(cst[:, 3:4], math.pi / 2), outs=[t])
        self.halfpi = cst[:, 3:4]
        R.barrier()
        R.flush()
        R.op("pool", lambda e: e.tensor_copy(out=I2[0:32, :], in_=ident[0:32, 0:32]), outs=[t])
        R.op("pool", lambda e: e.tensor_copy(out=I2[32:64, :], in_=ident[32:64, 32:64]), outs=[t])
        R.op("pool", lambda e: e.memset(rm[:, :], 1.0), outs=[t])
        R.op("pool", lambda e: e.affine_select(out=rm[:, :], in_=rm[:, :], pattern=[[-16, 8]], compare_op=ALU.is_ge,
                                                fill=0.0, base=0, channel_multiplier=1), outs=[t])
        R.op("pool", lambda e: e.affine_select(out=rm[:, :], in_=rm[:, :], pattern=[[16, 8]], compare_op=ALU.is_ge,
                                                fill=0.0, base=15, channel_multiplier=-1), outs=[t])
        with ExitStack() as tmp:
            SI = self.sb(tmp, "s5_SI", [128, 15, 128])
            R.op("pool", lambda e: e.memset(SI[:, :, :], 0.0), outs=[t])
            for s in range(-7, 8):
                R.op("pool", lambda e, s=s: e.affine_select(out=SI[:, s + 7, :], in_=SI[:, s + 7, :], pattern=[[1, 128]],
                                                             compare_op=ALU.not_equal, fill=1.0, base=-16 * s,
                                                             channel_multiplier=-1), outs=[t])
            for a in range(8):
                for b in range(8):
                    R.op("dve", lambda e, a=a, b=b: e.tensor_scalar(out=SelAll[:, a * 8 + b, :], in0=SI[:, b - a + 7, :],
                                                                    scalar1=rm[:, a:a + 1], scalar2=None, op0=ALU.mult),
                         ins=[t], outs=[t])
            R.barrier()
            R.flush()

    def phase_s5(self, st, l, pairs=range(32)):
        R = self.R
        self.setup_s5_consts(st)
        ident, identb, I2, SelAll = self.ident, self.identb, self.I2, self.SelAll
        NC_ = NT // 8
        PADL = 128
        t = Tile()

        def tt(eng, out, a, b, op):
            R.op(eng, lambda e: e.tensor_tensor(out=out, in0=a, in1=b, op=op), ins=[t], outs=[t])

        def ts(eng, out, a, s1, op0, s2=None, op1=None):
            if op1 is None:
                R.op(eng, lambda e: e.tensor_scalar(out=out, in0=a, scalar1=s1, scalar2=None, op0=op0), ins=[t], outs=[t])
            else:
                R.op(eng, lambda e: e.tensor_scalar(out=out, in0=a, scalar1=s1, scalar2=s2, op0=op0, op1=op1),
                     ins=[t], outs=[t])

        LPre = self.sb(st, "s5_LPre", [128, 9, 32])
        LPim = self.sb(st, "s5_LPim", [128, 9, 32])
        AKre = self.sb(st, "s5_AKre", [128, 8, 32])
        AKim = self.sb(st, "s5_AKim", [128, 8, 32])
        AKni = self.sb(st, "s5_AKni", [128, 8, 32])
        Vre = self.sb(st, "s5_Vre", [128, 32, 8, 16], BF16)
        Vim = self.sb(st, "s5_Vim", [128, 32, 8, 16], BF16)
        KTp = self.sb(st, "s5_KT", [128, 64, 128], BF16)
        WPre = self.sb(st, "s5_WPre", [128, 64, 128], BF16)
        WPim = self.sb(st, "s5_WPim", [128, 64, 128], BF16)
        H0re = self.sb(st, "s5_H0re", [128, 32])
        H0im = self.sb(st, "s5_H0im", [128, 32])
        dcol, t_dcol = self.load_cols(st, "s5_dcol", self.ssm_d[l].rearrange("(k p) -> k p", p=128), 8)

        with ExitStack() as tb:
            gl = self.sb(tb, "s5_gl", [64, 64])
            Lh = self.sb(tb, "s5_Lh", [64, 128])
            are = self.sb(tb, "s5_are", [128, 32])
            aim = self.sb(tb, "s5_aim", [128, 32])
            dtp = self.sb(tb, "s5_dtp", [128, 32])
            adt = self.sb(tb, "s5_adt", [128, 32])
            th = self.sb(tb, "s5_th", [128, 32])
            w = [self.sb(tb, "s5_w%d" % i, [128, 32]) for i in range(8)]
            ldc = self.sb(tb, "s5_ldc", [64, 1])
            R.op("pool", lambda e: e.memset(Lh[:, :], 0.0), ins=[t], outs=[t])

            def to_pair(dst, fill_gl):
                fill_gl()
                R.op("dve", lambda e: e.tensor_copy(out=Lh[0:32, 0:64], in_=gl[0:32, :]), ins=[t], outs=[t])
                R.op("dve", lambda e: e.tensor_copy(out=Lh[32:64, 64:128], in_=gl[32:64, :]), ins=[t], outs=[t])
                ps, t_ps = self.psr.next()
                R.op("pe", lambda e: e.matmul(ps[:, 0:32], Lh[:, :], I2[:, :], start=True, stop=True), ins=[t], outs=[t_ps])
                R.op("dve", lambda e: e.tensor_copy(out=dst, in_=ps[:, 0:32]), ins=[t_ps, t], outs=[t])

            to_pair(are[:, :], lambda: R.dma("sp", gl[:, :], self.a_re[l], ins=[t], outs=[t]))
            to_pair(aim[:, :], lambda: R.dma("sp", gl[:, :], self.a_im[l], ins=[t], outs=[t]))
            to_pair(H0re[:, :], lambda: R.dma("sp", gl[:, :], self.st_re[l], ins=[t], outs=[t]))
            to_pair(H0im[:, :], lambda: R.dma("sp", gl[:, :], self.st_im[l], ins=[t], outs=[t]))

            def fill_dt():
                R.dma("sp", ldc[:, :], self.log_dt[l].unsqueeze(1), ins=[t], outs=[t])
                R.op("act", lambda e: e.activation(out=ldc[:, :], in_=ldc[:, :], func=AF.Exp), ins=[t], outs=[t])
                R.op("dve", lambda e: e.tensor_copy(out=gl[:, :], in_=ldc[:, 0:1].to_broadcast([64, 64])), ins=[t], outs=[t])
            to_pair(dtp[:, :], fill_dt)
            tt("dve", adt[:, :], are[:, :], dtp[:, :], ALU.mult)
            tt("dve", th[:, :], aim[:, :], dtp[:, :], ALU.mult)
            R.op("act", lambda e: e.activation(out=w[0][:, :], in_=adt[:, :], func=AF.Exp, scale=1.0 / 32), ins=[t], outs=[t])
            R.op("act", lambda e: e.activation(out=w[1][:, :], in_=th[:, :], func=AF.Sin, scale=1.0 / 32), ins=[t], outs=[t])
            R.op("act", lambda e: e.activation(out=w[2][:, :], in_=th[:, :], func=AF.Sin, scale=1.0 / 32,
                                               bias=self.halfpi), ins=[t], outs=[t])
            tt("dve", w[3][:, :], w[0][:, :], w[2][:, :], ALU.mult)
            tt("dve", w[4][:, :], w[0][:, :], w[1][:, :], ALU.mult)

            def csq(o_re, o_im, a_re_, a_im_):
                tt("dve", w[6][:, :], a_re_, a_re_, ALU.mult)
                tt("dve", w[7][:, :], a_im_, a_im_, ALU.mult)
                tt("dve", w[5][:, :], a_re_, a_im_, ALU.mult)
                tt("dve", o_re, w[6][:, :], w[7][:, :], ALU.subtract)
                ts("dve", o_im, w[5][:, :], 2.0, ALU.mult)

            def cmul(o_re, o_im, a_re_, a_im_, b_re_, b_im_, sh=None):
                F = [128, 32] if sh is None else sh
                t1 = w[6][:, :] if sh is None else big[0]
                t2 = w[7][:, :] if sh is None else big[1]
                tt("dve", t1, a_re_, b_re_, ALU.mult)
                tt("dve", t2, a_im_, b_im_, ALU.mult)
                tt("dve", o_re, t1, t2, ALU.subtract)
                tt("dve", t1, a_re_, b_im_, ALU.mult)
                tt("dve", t2, a_im_, b_re_, ALU.mult)
                tt("dve", o_im, t1, t2, ALU.add)

            cur = (w[3], w[4])
            alt = (w[0], w[1])
            for i in range(5):
                dst = (LPre[:, 1, :], LPim[:, 1, :]) if i == 4 else (alt[0][:, :], alt[1][:, :])
                csq(dst[0], dst[1], cur[0][:, :], cur[1][:, :])
                cur, alt = alt, cur
            R.op("pool", lambda e: e.memset(LPre[:, 0, :], 1.0), ins=[t], outs=[t])
            R.op("pool", lambda e: e.memset(LPim[:, 0, :], 0.0), ins=[t], outs=[t])
            L_ = lambda k: (LPre[:, k, :], LPim[:, k, :])
            csq(*L_(2), *L_(1))
            cmul(*L_(3), *L_(2), *L_(1))
            csq(*L_(4), *L_(2))
            cmul(*L_(5), *L_(4), *L_(1))
            csq(*L_(6), *L_(3))
            cmul(*L_(7), *L_(4), *L_(3))
            csq(*L_(8), *L_(4))
            R.op("dve", lambda e: e.tensor_copy(out=AKre[:, 0, :], in_=LPre[:, 8, :]), ins=[t], outs=[t])
            R.op("dve", lambda e: e.tensor_copy(out=AKim[:, 0, :], in_=LPim[:, 8, :]), ins=[t], outs=[t])
            for k in range(1, 8):
                csq(AKre[:, k, :], AKim[:, k, :], AKre[:, k - 1, :], AKim[:, k - 1, :])
            ts("dve", AKni[:, :, :], AKim[:, :, :], -1.0, ALU.mult)
            kre = self.sb(tb, "s5_kre", [128, 32])
            kim = self.sb(tb, "s5_kim", [128, 32])
            tt("dve", w[0][:, :], are[:, :], are[:, :], ALU.mult)
            tt("dve", w[1][:, :], aim[:, :], aim[:, :], ALU.mult)
            tt("dve", w[0][:, :], w[0][:, :], w[1][:, :], ALU.add)
            R.op("dve", lambda e: e.reciprocal(out=w[0][:, :], in_=w[0][:, :]), ins=[t], outs=[t])
            ts("dve", w[1][:, :], LPre[:, 1, :], -1.0, ALU.add)
            tt("dve", w[2][:, :], w[1][:, :], are[:, :], ALU.mult)
            tt("dve", w[3][:, :], LPim[:, 1, :], aim[:, :], ALU.mult)
            tt("dve", w[2][:, :], w[2][:, :], w[3][:, :], ALU.add)
            tt("dve", kre[:, :], w[2][:, :], w[0][:, :], ALU.mult)
            tt("dve", w[2][:, :], LPim[:, 1, :], are[:, :], ALU.mult)
            tt("dve", w[3][:, :], w[1][:, :], aim[:, :], ALU.mult)
            tt("dve", w[2][:, :], w[2][:, :], w[3][:, :], ALU.subtract)
            tt("dve", kim[:, :], w[2][:, :], w[0][:, :], ALU.mult)
            Bre = self.sb(tb, "s5_Bre", [128, 32, 16])
            Bim = self.sb(tb, "s5_Bim", [128, 32, 16])
            BBre = self.sb(tb, "s5_BBre", [128, 32, 16])
            BBim = self.sb(tb, "s5_BBim", [128, 32, 16])
            big = [self.sb(tb, "s5_big%d" % i, [128, 32, 16])[:, :, :] for i in range(2)]
            for h in range(2):
                R.dma("sp", Bre[h * 64:(h + 1) * 64, :, :], self.b_re[l][32 * h:32 * h + 32].rearrange("g n p -> n g p"),
                      ins=[t], outs=[t])
                R.dma("sp", Bim[h * 64:(h + 1) * 64, :, :], self.b_im[l][32 * h:32 * h + 32].rearrange("g n p -> n g p"),
                      ins=[t], outs=[t])
            bc = lambda ap: ap.unsqueeze(2).to_broadcast([128, 32, 16])
            cmul(BBre[:, :, :], BBim[:, :, :], bc(kre[:, :]), bc(kim[:, :]), Bre[:, :, :], Bim[:, :, :], sh=1)
            MXre = self.sb(tb, "s5_MXre", [128, 32, 15, 16], BF16)
            MXim = self.sb(tb, "s5_MXim", [128, 32, 15, 16], BF16)
            R.op("pool", lambda e: e.memset(MXre[:, :, :, :], 0.0), ins=[t], outs=[t])
            R.op("pool", lambda e: e.memset(MXim[:, :, :, :], 0.0), ins=[t], outs=[t])
            for d in range(8):
                e_ = 7 - d
                if d == 0:
                    R.op("dve", lambda e: e.tensor_copy(out=MXre[:, :, 7, :], in_=BBre[:, :, :]), ins=[t], outs=[t])
                    R.op("dve", lambda e: e.tensor_copy(out=MXim[:, :, 7, :], in_=BBim[:, :, :]), ins=[t], outs=[t])
                else:
                    cmul(MXre[:, :, e_, :], MXim[:, :, e_, :], bc(LPre[:, d, :]), bc(LPim[:, d, :]), BBre[:, :, :], BBim[:, :, :],
                         sh=1)
            CTre = self.sb(tb, "s5_CTre", [128, 32, 16])
            CTim = self.sb(tb, "s5_CTim", [128, 32, 16])
            CTni = self.sb(tb, "s5_CTni", [128, 32, 16])
            Cpad = self.sb(tb, "s5_Cpad", [128, 8, 128])
            for (src, dstC) in ((self.c_re, CTre), (self.c_im, CTim)):
                R.op("pool", lambda e: e.memset(Cpad[:, :, :], 0.0), ins=[t], outs=[t])
                for rt in range(8):
                    h = rt // 4
                    R.dma("sp", Cpad[:, rt, h * 64:(h + 1) * 64],
                          src[l][rt * 8:(rt + 1) * 8].rearrange("g q n -> (g q) n"), ins=[t], outs=[t])
                ps, t_ps = self.psr.next()

                def mmc(e, ps=ps):
                    for rt4 in range(4):
                        e.matmul(ps[:, rt4 * 128:(rt4 + 1) * 128], Cpad[:, rt4, :], ident[:, :], start=True, stop=False)
                        ins_ = e.matmul(ps[:, rt4 * 128:(rt4 + 1) * 128], Cpad[:, rt4 + 4, :], ident[:, :], start=False, stop=True)
                    return ins_
                R.op("pe", mmc, ins=[t], outs=[t_ps])
                R.op("dve", lambda e, ps=ps, dstC=dstC: e.tensor_copy(out=dstC[:, :, :].rearrange("p g q -> p (g q)"), in_=ps[:, :]),
                     ins=[t_ps, t], outs=[t])
            ts("dve", CTni[:, :, :], CTim[:, :, :], -1.0, ALU.mult)
            CTre_b = self.sb(tb, "s5_CTre_b", [128, 32, 16], BF16)
            CTni_b = self.sb(tb, "s5_CTni_b", [128, 32, 16], BF16)
            R.op("dve", lambda e: e.tensor_copy(out=CTre_b[:, :, :], in_=CTre[:, :, :]), ins=[t], outs=[t])
            R.op("dve", lambda e: e.tensor_copy(out=CTni_b[:, :, :], in_=CTni[:, :, :]), ins=[t], outs=[t])
            for j in range(8):
                lr, li = bc(LPre[:, j + 1, :]), bc(LPim[:, j + 1, :])
                tt("dve", big[0], CTre[:, :, :], lr, ALU.mult)
                tt("dve", big[1], CTim[:, :, :], li, ALU.mult)
                tt("dve", Vre[:, :, j, :], big[0], big[1], ALU.subtract)
                tt("dve", big[0], CTre[:, :, :], li, ALU.mult)
                tt("dve", big[1], CTni[:, :, :], lr, ALU.mult)
                tt("dve", Vim[:, :, j, :], big[1], big[0], ALU.subtract)
            R.op("pool", lambda e: e.memset(WPre[:, :, :], 0.0), ins=[t], outs=[t])
            R.op("pool", lambda e: e.memset(WPim[:, :, :], 0.0), ins=[t], outs=[t])
            t_tabs_done = t.w
            for g in range(G):
                h, gg = g // 32, g % 32
                hp = slice(h * 64, (h + 1) * 64)
                ps, t_ps = self.psr.next()

                def mmk(e, ps=ps, hp=hp, gg=gg):
                    for j in range(8):
                        lo_ = (7 - j)
                        mre = MXre[hp, gg, lo_:lo_ + 8, :].rearrange("p e q -> p (e q)")
                        mim = MXim[hp, gg, lo_:lo_ + 8, :].rearrange("p e q -> p (e q)")
                        e.matmul(ps[:, j * 16:(j + 1) * 16], mre, CTre_b[hp, gg, :], start=True, stop=False)
                        e.matmul(ps[:, j * 16:(j + 1) * 16], mim, CTni_b[hp, gg, :], start=False, stop=True)
                    wre = MXre[hp, gg, 0:8, :].rearrange("p e q -> p (e q)")
                    wim = MXim[hp, gg, 0:8, :].rearrange("p e q -> p (e q)")
                    e.matmul(ps[:, 128:192], wre, identb[hp, hp], start=True, stop=True)
                    return e.matmul(ps[:, 192:256], wim, identb[hp, hp], start=True, stop=True)
                R.op("pe", mmk, outs=[t_ps], extra=[t_tabs_done])
                R.op("act", lambda e, ps=ps, g=g: e.activation(out=KTp[:, g, :], in_=ps[:, 0:128], func=AF.Copy),
                     ins=[t_ps], extra=[t_tabs_done])
                R.op("dve", lambda e, ps=ps, g=g, hp=hp: e.tensor_copy(out=WPre[:, g, hp], in_=ps[:, 128:192]),
                     ins=[t_ps], extra=[t_tabs_done])
                R.op("act", lambda e, ps=ps, g=g, hp=hp: e.activation(out=WPim[:, g, hp], in_=ps[:, 192:256], func=AF.Copy),
                     ins=[t_ps], extra=[t_tabs_done])
            R.barrier()
            R.flush()

        ub = self.sb(st, "s5_ub", [128, 8, NT], BF16)
        t_ub = Tile()
        R.dma("pool", ub[:, :, :], self.uT.rearrange("(k p) n -> p k n", p=128), outs=[t_ub])
        Ub = Rot([self.sb(st, "s5_U%d" % i, [128, NC_], BF16) for i in range(4)])
        Zre = Rot([self.sb(st, "s5_Zre%d" % i, [128, PADL + 256]) for i in range(2)])
        Zim = Rot([self.sb(st, "s5_Zim%d" % i, [128, PADL + 256]) for i in range(2)])
        Xre = Rot([self.sb(st, "s5_Xre%d" % i, [128, PADL + 256]) for i in range(2)])
        Xim = Rot([self.sb(st, "s5_Xim%d" % i, [128, PADL + 256]) for i in range(2)])
        for rot in (Zre, Zim, Xre, Xim):
            for (a, ta) in rot.items:
                R.op("pool", lambda e, a=a: e.memset(a[:, 0:PADL], 0.0), outs=[ta])
        Zs = Rot([self.sb(st, "s5_Zs%d" % i, [128, 8]) for i in range(2)])
        Spb = Rot([self.sb(st, "s5_Sp%d" % i, [128, 2, NC_], BF16) for i in range(2)])
        Ysb = self.sb(st, "s5_Ysb", [128, 16, NC_], BF16)
        t_Ysb = Tile()
        Sfin = self.sb(st, "s5_Sfin", [128, 4, 32])
        t_Sfin = Tile()
        u32 = Rot([self.sb(st, "s5_u32_%d" % i, [128, NT]) for i in range(2)])
        ya32 = Rot([self.sb(st, "s5_ya32_%d" % i, [128, NT]) for i in range(2)])

        def p1(gg):
            zre, t_zre = Zre.next()
            zim, t_zim = Zim.next()
            us = []
            for h in range(2):
                g = 32 * h + gg
                b, g8 = g // 8, g % 8
                U, t_U = Ub.next()
                ps, t_ps = self.psr.next()

                def mmu(e, ps=ps, b=b, g8=g8):
                    for i in range(8):
                        ins_ = e.matmul(ps[:, :NC_], SelAll[:, g8 * 8 + i, :],
                                        ub[:, b, :].rearrange("p (c i) -> p i c", i=8)[:, i, :], start=(i == 0), stop=(i == 7))
                    return ins_
                R.op("pe", mmu, ins=[t_ub], outs=[t_ps])
                R.op("act", lambda e, U=U, ps=ps: e.activation(out=U[:, :], in_=ps[:, :NC_], func=AF.Copy), ins=[t_ps], outs=[t_U])
                us.append((U, t_U, g))
            psr_, t_pr = self.psr.next()
            psi_, t_pi = self.psr.next()

            def mmz(e):
                for h in range(2):
                    U, t_U, g = us[h]
                    e.matmul(psr_[:, :NC_], WPre[:, g, :], U[:, :], start=(h == 0), stop=(h == 1))
                for h in range(2):
                    U, t_U, g = us[h]
                    ins_ = e.matmul(psi_[:, :NC_], WPim[:, g, :], U[:, :], start=(h == 0), stop=(h == 1))
                return ins_
            R.op("pe", mmz, ins=[us[0][1], us[1][1]], outs=[t_pr, t_pi])
            zs, t_zs = Zs.next()
            R.op("dve", lambda e: e.tensor_copy(out=zre[:, PADL:PADL + 256], in_=psr_[:, 0:256]), ins=[t_pr], outs=[t_zre])
            R.op("dve", lambda e: e.tensor_copy(out=zim[:, PADL:PADL + 256], in_=psi_[:, 0:256]), ins=[t_pi], outs=[t_zim])
            R.op("dve", lambda e: e.tensor_copy(out=zs[:, 0:2], in_=psr_[:, 256:258]), ins=[t_pr], outs=[t_zs])
            R.op("dve", lambda e: e.tensor_copy(out=zs[:, 2:4], in_=psi_[:, 256:258]), ins=[t_pi], outs=[t_zs])
            xre, t_xre = Xre.next()
            xim, t_xim = Xim.next()
            c = {"gg": gg, "us": us, "zs": zs, "t_zs": t_zs,
                 "cur": (zre, t_zre, zim, t_zim), "nxt": (xre, t_xre, xim, t_xim)}
            return c

        def scan_step(c, k):
            gg = c["gg"]
            if True:
                d = 1 << k
                cr, t_cr, ci, t_ci = c["cur"]
                nr, t_nr, ni, t_ni = c["nxt"]
                a_r, a_i, a_n = AKre[:, k, gg:gg + 1], AKim[:, k, gg:gg + 1], AKni[:, k, gg:gg + 1]
                lo_, hi_ = PADL, PADL + 256
                R.op("dve", lambda e, nr=nr, cr=cr, a_r=a_r, d=d: e.scalar_tensor_tensor(
                    out=nr[:, lo_:hi_], in0=cr[:, lo_ - d:hi_ - d], scalar=a_r, in1=cr[:, lo_:hi_], op0=ALU.mult, op1=ALU.add),
                    ins=[t_cr], outs=[t_nr])
                R.op("dve", lambda e, nr=nr, ci=ci, a_n=a_n, d=d: e.scalar_tensor_tensor(
                    out=nr[:, lo_:hi_], in0=ci[:, lo_ - d:hi_ - d], scalar=a_n, in1=nr[:, lo_:hi_], op0=ALU.mult, op1=ALU.add),
                    ins=[t_ci], outs=[t_nr])
                R.op("dve", lambda e, ni=ni, ci=ci, a_r=a_r, d=d: e.scalar_tensor_tensor(
                    out=ni[:, lo_:hi_], in0=ci[:, lo_ - d:hi_ - d], scalar=a_r, in1=ci[:, lo_:hi_], op0=ALU.mult, op1=ALU.add),
                    ins=[t_ci], outs=[t_ni])
                R.op("dve", lambda e, ni=ni, cr=cr, a_i=a_i, d=d: e.scalar_tensor_tensor(
                    out=ni[:, lo_:hi_], in0=cr[:, lo_ - d:hi_ - d], scalar=a_i, in1=ni[:, lo_:hi_], op0=ALU.mult, op1=ALU.add),
                    ins=[t_cr], outs=[t_ni])
                c["cur"], c["nxt"] = c["nxt"], c["cur"]

        def p3(c):
            gg, us, zs, t_zs = c["gg"], c["us"], c["zs"], c["t_zs"]
            sre, t_sre, sim, t_sim = c["cur"]
            a_r, a_i, a_n = AKre[:, 0, gg:gg + 1], AKim[:, 0, gg:gg + 1], AKni[:, 0, gg:gg + 1]
            prev_re, prev_im = H0re[:, gg:gg + 1], H0im[:, gg:gg + 1]
            for s_ in range(2):
                o_re, o_im = zs[:, 4 + s_:5 + s_], zs[:, 6 + s_:7 + s_]
                z_re, z_im = zs[:, s_:s_ + 1], zs[:, 2 + s_:3 + s_]
                for (o, p1, c1, p2, c2, z) in ((o_re, prev_re, a_r, prev_im, a_n, z_re), (o_im, prev_im, a_r, prev_re, a_i, z_im)):
                    R.op("dve", lambda e, o=o, p1=p1, c1=c1, z=z: e.scalar_tensor_tensor(
                        out=o, in0=p1, scalar=c1, in1=z, op0=ALU.mult, op1=ALU.add), ins=[t_zs], outs=[t_zs])
                    R.op("dve", lambda e, o=o, p2=p2, c2=c2: e.scalar_tensor_tensor(
                        out=o, in0=p2, scalar=c2, in1=o, op0=ALU.mult, op1=ALU.add), ins=[t_zs], outs=[t_zs])
                prev_re, prev_im = o_re, o_im
            sp, t_sp = Spb.next()
            R.op("act", lambda e: e.activation(out=sp[:, 0, 0:256], in_=sre[:, PADL - 1:PADL + 255], func=AF.Copy),
                 ins=[t_sre], outs=[t_sp])
            R.op("act", lambda e: e.activation(out=sp[:, 1, 0:256], in_=sim[:, PADL - 1:PADL + 255], func=AF.Copy),
                 ins=[t_sim], outs=[t_sp])
            R.op("pool", lambda e: e.tensor_copy(out=sp[:, 0, 256:257], in_=H0re[:, gg:gg + 1]), outs=[t_sp])
            R.op("pool", lambda e: e.tensor_copy(out=sp[:, 1, 256:257], in_=H0im[:, gg:gg + 1]), outs=[t_sp])
            R.op("pool", lambda e: e.tensor_copy(out=sp[:, 0, 257:258], in_=zs[:, 4:5]), ins=[t_zs], outs=[t_sp])
            R.op("pool", lambda e: e.tensor_copy(out=sp[:, 1, 257:258], in_=zs[:, 6:7]), ins=[t_zs], outs=[t_sp])
            R.op("pool", lambda e: e.tensor_copy(out=Sfin[:, 0, gg:gg + 1], in_=sre[:, PADL + 255:PADL + 256]), ins=[t_sre], outs=[t_Sfin])
            R.op("pool", lambda e: e.tensor_copy(out=Sfin[:, 1, gg:gg + 1], in_=sim[:, PADL + 255:PADL + 256]), ins=[t_sim], outs=[t_Sfin])
            R.op("pool", lambda e: e.tensor_copy(out=Sfin[:, 2, gg:gg + 1], in_=zs[:, 5:6]), ins=[t_zs], outs=[t_Sfin])
            R.op("pool", lambda e: e.tensor_copy(out=Sfin[:, 3, gg:gg + 1], in_=zs[:, 7:8]), ins=[t_zs], outs=[t_Sfin])
            for h in range(2):
                U, t_U, g = us[h]
                hp = slice(h * 64, (h + 1) * 64)
                ps, t_ps = self.psr.next()

                def mmy(e, ps=ps, U=U, g=g, hp=hp):
                    e.matmul(ps[:, :NC_], KTp[:, g, :], U[:, :], start=True, stop=False)
                    e.matmul(ps[:, :NC_], Vre[hp, gg, :, :].rearrange("p j q -> p (j q)"), sp[hp, 0, :], start=False, stop=False)
                    return e.matmul(ps[:, :NC_], Vim[hp, gg, :, :].rearrange("p j q -> p (j q)"), sp[hp, 1, :], start=False, stop=True)
                R.op("pe", mmy, ins=[t_U, t_sp], outs=[t_ps])
                slot = (gg % 8) + 8 * h
                R.op("act", lambda e, ps=ps, slot=slot: e.activation(out=Ysb[:, slot, :], in_=ps[:, :NC_], func=AF.Copy),
                     ins=[t_ps], outs=[t_Ysb])

        def finish_blocks(b0):
            for h in range(2):
                b = b0 + 4 * h
                u, t_u = u32.next()
                ya, t_ya = ya32.next()
                R.dma("sp", u[:, :], self.uT[b * 128:(b + 1) * 128, :], outs=[t_u])
                for j in range(8):
                    ps, t_ps = self.psr.next()

                    def mmi(e, ps=ps, j=j, h=h):
                        for g8 in range(8):
                            ins_ = e.matmul(ps[:, :NC_], SelAll[:, j * 8 + g8, :], Ysb[:, 8 * h + g8, :], start=(g8 == 0), stop=(g8 == 7))
                        return ins_
                    R.op("pe", mmi, ins=[t_Ysb], outs=[t_ps])
                    uv = u[:, :].rearrange("p (c i) -> p i c", i=8)[:, j, :]
                    yv = ya[:, :].rearrange("p (c i) -> p i c", i=8)[:, j, :]
                    R.op("dve", lambda e, uv=uv, yv=yv, ps=ps, b=b: e.scalar_tensor_tensor(
                        out=yv, in0=uv, scalar=dcol[:, b:b + 1], in1=ps[:, :NC_], op0=ALU.mult, op1=ALU.add),
                        ins=[t_ps, t_u, t_dcol], outs=[t_ya])
                R.op("act", lambda e, ya=ya: e.activation(out=ya[:, :], in_=ya[:, :], func=AF.Gelu), ins=[t_ya], outs=[t_ya])
                R.dma("sp", self.yaT32[b * 128:(b + 1) * 128, :], ya[:, :], ins=[t_ya])

        pl = list(pairs)
        for i0 in range(0, len(pl), 2):
            grp = pl[i0:i0 + 2]
            cs_ = [p1(gg) for gg in grp]
            for k in range(8):
                for c in cs_:
                    scan_step(c, k)
            for c in cs_:
                p3(c)
            if grp[-1] % 8 == 7:
                finish_blocks(grp[-1] // 8)
        stg = self.sb(st, "s5_stg", [32, 4, 128])
        t_stg = Tile()
        for i, dst in enumerate((self.p_ss_re, self.p_ss_im, self.s_ss_re, self.s_ss_im)):
            ps, t_ps = self.psr.next()
            R.op("pe", lambda e, ps=ps, i=i: e.matmul(ps[:32, 0:128], Sfin[:, i, :], ident[:, :], start=True, stop=True),
                 ins=[t_Sfin], outs=[t_ps])
            R.op("dve", lambda e, ps=ps, i=i: e.tensor_copy(out=stg[:, i, :], in_=ps[:32, 0:128]), ins=[t_ps], outs=[t_stg])
            for h in range(2):
                R.dma("sp", dst[l, 32 * h:32 * h + 32, :], stg[:, i, h * 64:(h + 1) * 64], ins=[t_stg])

    def phase_glu(self, st, l):
        R = self.R
        bgcol, t_bg = self.load_cols(st, "s5_bgcol", self.b_glu[l].rearrange("(k p) -> k p", p=128), 8)
        yab = self.sb(st, "s5_yab", [128, 8, NT], BF16)
        t_yab = Tile()
        R.dma("pool", yab[:, :, :], self.yaT32.rearrange("(k p) n -> p k n", p=128), outs=[t_yab])
        slabs = Rot([self.sb(st, "s5_slab%d" % i, [128, 8, 256], BF16) for i in range(2)])
        yrow = Rot([self.sb(st, "s5_yrow%d" % i, [128, NT]) for i in range(2)])
        zrow = Rot([self.sb(st, "s5_zrow%d" % i, [128, NT]) for i in range(2)])
        gate = Rot([self.sb(st, "s5_gate%d" % i, [128, CH]) for i in range(3)])
        outs_ = Rot([self.sb(st, "s5_out%d" % i, [128, NT], BF16) for i in range(2)])
        for s in range(MW // 256):
            slab, t_slab = self.load_slab(slabs, self.w_glu[l], 8, s * 256, 256)
            for cbl in range(2):
                cb = s * 2 + cbl
                yr, t_yr = yrow.next()
                zr, t_zr = zrow.next()
                R.dma("sp", yr[:, :], self.yaT32[cb * 128:(cb + 1) * 128, :], outs=[t_yr])
                R.dma("sp", zr[:, :], self.szT[0][cb * 128:(cb + 1) * 128, :], outs=[t_zr])
                R.op("pool", lambda e, yr=yr, zr=zr: e.tensor_tensor(out=yr[:, :], in0=yr[:, :], in1=zr[:, :], op=ALU.mult),
                     ins=[t_zr], outs=[t_yr])
                o, t_o = outs_.next()
                for ch in range(NCH):
                    cs = slice(ch * CH, (ch + 1) * CH)
                    ps, t_ps = self.mm_F(slab, t_slab, cbl, yab, 8, ch, [t_yab.w])
                    gt, t_gt = gate.next()
                    R.op("act", lambda e, gt=gt, ps=ps, cb=cb: e.activation(out=gt[:, :], in_=ps[:, :CH], func=AF.Sigmoid,
                                                                          bias=bgcol[:, cb:cb + 1]), ins=[t_ps, t_bg], outs=[t_gt])
                    R.op("dve", lambda e, gt=gt, yr=yr, o=o, cs=cs: e.tensor_tensor(out=o[:, cs], in0=gt[:, :], in1=yr[:, cs], op=ALU.mult),
                         ins=[t_gt, t_yr], outs=[t_o])
                R.dma("sp", self.yT[0][cb * 128:(cb + 1) * 128, :], o[:, :], ins=[t_o])


def build_program():
    P = Prog()
    P.setup_consts()
    P.setup_masks()
    P.setup_band_consts()
    for l in range(DEPTH):
        with ExitStack() as sa:
            actT = P.sb(sa, "actT", [128, KT, NT], BF16)
            P.run_phase(lambda st: P.phase_norm(st, l, actT))
            P.run_phase(lambda st: P.phase_gemm_in(st, l, actT, []))
        P.run_phase(lambda st: P.phase_s5(st, l))
        P.run_phase(lambda st: P.phase_glu(st, l))
        P.run_phase(lambda st: P.phase_band(st, l))
        P.run_phase(lambda st: P.phase_sb(st, l))
        P.run_phase(lambda st: P.phase_merge(st, l))
        with ExitStack() as sa:
            actT = P.sb(sa, "actT", [128, KT, NT], BF16)
            P.run_phase(lambda st: P.phase_out(st, l, actT))
    return P


_PER_CORE_5D = ("cache_sb_k", "cache_sb_v", "cache_band_k", "cache_band_v")
_WEIGHTS = ("norm_g", "w_in", "ssm_a_re", "ssm_a_im", "ssm_log_dt", "ssm_b_re", "ssm_b_im", "ssm_c_re", "ssm_c_im",
            "ssm_d", "w_glu", "b_glu", "q_norm_g", "k_norm_g", "rel_bias", "w_br_a", "w_br_b", "w_br_c", "gate_b", "w_out")


def kernel(**inputs):
    NB = 8
    P = build_program()
    f32 = lambda a: np.ascontiguousarray(np.asarray(a, dtype=np.float32))
    w = {k: f32(inputs[k]) for k in _WEIGHTS}
    in_maps = []
    for b in range(NB):
        m = dict(w)
        m["x_prompt"] = f32(inputs["x_prompt"][b])
        m["x_sample"] = f32(inputs["x_sample"][b])
        for k in _PER_CORE_5D:
            a = np.asarray(inputs[k])
            m[k] = f32(a[:, b].reshape(a.shape[0], a.shape[2], MW))
        m["state_ssm_re"] = f32(np.asarray(inputs["state_ssm_re"])[:, b])
        m["state_ssm_im"] = f32(np.asarray(inputs["state_ssm_im"])[:, b])
        in_maps.append(m)
    res = run_bass_kernel_spmd(P.nc, in_maps, core_ids=list(range(NB)))
    r = res.results
    L = DEPTH
    st = lambda name: np.stack([np.asarray(r[b][name]) for b in range(NB)], axis=0)
    y_p = st("y_prompt")
    y_s = st("y_sample")

    def kv(name, rows):
        a = st(name)
        return np.ascontiguousarray(a.transpose(1, 0, 2, 3)).reshape(L, NB, rows, NH, DH)

    def ss(name):
        return np.ascontiguousarray(st(name).transpose(1, 0, 2, 3))

    return (y_p, y_s,
            kv("p_sb_k", T), kv("p_sb_v", T), kv("p_band_k", BAND), kv("p_band_v", BAND), ss("p_ssm_re"), ss("p_ssm_im"),
            kv("s_sb_k", TS), kv("s_sb_v", TS), kv("s_band_k", TS), kv("s_band_v", TS), ss("s_ssm_re"), ss("s_ssm_im"))
```

```python
import math
from contextlib import ExitStack
import numpy as np
import concourse.bass as bass
import concourse.mybir as mybir
from concourse.bass_utils import run_bass_kernel_spmd

F32 = mybir.dt.float32
BF16 = mybir.dt.bfloat16
AF = mybir.ActivationFunctionType
ALU = mybir.AluOpType
AX = mybir.AxisListType

T = 2048
TS = 16
NT = T + TS
D = 4096
MW = 1024
NIN = 22528
NH = 8
DH = 128
DEPTH = 2
KT = D // 128
CH = 344
NCH = NT // CH
G = 64
SN = 64
SP_ = 16
BAND = 512
EPS = 1e-6
SCALE = 1.0 / math.sqrt(DH)
NTT = 17


def trows(tt):
    return 128 if tt < 16 else TS


class Tile:
    __slots__ = ("w", "r")

    def __init__(self):
        self.w = None
        self.r = {}


class Eng:
    def __init__(self, name, sem):
        self.name = name
        self.sem = sem
        self.cnt = 0
        self.ops = []
        self.waited = {}

    def add(self, fn, deps, inc):
        waits = []
        for d in deps:
            if d is None:
                continue
            sem, v = d
            if self.waited.get(sem, 0) >= v:
                continue
            self.waited[sem] = v
            waits.append((sem, v))
        self.ops.append((waits, fn, inc))


class Rec:
    NDMA = 8

    def __init__(self, nc, st):
        self.nc = nc
        self.E = {}
        for n in ("sp", "act", "dve", "pool", "pe"):
            self.E[n] = Eng(n, st.enter_context(nc.semaphore("prog_" + n)))
        self.dq = {}
        for q in ("sp", "pool", "act"):
            sems = [st.enter_context(nc.semaphore("dma_%s_%d" % (q, i))) for i in range(self.NDMA)]
            self.dq[q] = {"sems": sems, "n": 0, "tok": [None] * self.NDMA}

    @staticmethod
    def _deps(ins, outs, extra):
        deps = list(extra)
        for t in ins:
            deps.append(t.w)
        for t in outs:
            deps.append(t.w)
            deps.extend(t.r.items())
        return deps

    @staticmethod
    def _mark(tok, ins, outs):
        for t in ins:
            s, v = tok
            if t.r.get(s, 0) < v:
                t.r[s] = v
        for t in outs:
            t.w = tok
            t.r = {}

    def op(self, eng, fn, ins=(), outs=(), extra=()):
        E = self.E[eng]
        deps = self._deps(ins, outs, extra)
        E.cnt += 1
        tok = (E.sem, E.cnt)
        E.add(fn, deps, (E.sem, 1))
        self._mark(tok, ins, outs)
        return tok

    def dma(self, q, out, in_, ins=(), outs=(), extra=(), slow=False):
        E = self.E[q]
        Q = self.dq[q]
        slot = Q["n"] % self.NDMA
        Q["n"] += 1
        deps = self._deps(ins, outs, extra)
        deps.append(Q["tok"][slot])
        sem = Q["sems"][slot]
        prev = Q["tok"][slot][1] if Q["tok"][slot] else 0
        tok = (sem, prev + 16)
        Q["tok"][slot] = tok
        E.add(lambda e: e.dma_start(out=out, in_=in_, allow_slow_non_contiguous=slow), deps, (sem, 16))
        self._mark(tok, ins, outs)
        return tok

    def all_tokens(self):
        toks = [(E.sem, E.cnt) for E in self.E.values() if E.cnt > 0]
        for Q in self.dq.values():
            toks.extend(t for t in Q["tok"] if t is not None)
        return toks

    def barrier(self):
        toks = self.all_tokens()
        for E in self.E.values():
            E.add(None, toks, None)

    def flush(self, name=None):
        nc = self.nc

        def replay(E):
            def f(e):
                for waits, fn, inc in E.ops:
                    for sem, v in waits:
                        e.wait_ge(sem, v)
                    if fn is not None:
                        ins = fn(e)
                        if inc is not None:
                            ins.then_inc(inc[0], inc[1])
                E.ops = []
            return f

        with nc.Block() as block:
            block.sync(replay(self.E["sp"]))
            block.scalar(replay(self.E["act"]))
            block.vector(replay(self.E["dve"]))
            block.gpsimd(replay(self.E["pool"]))
            block.tensor(replay(self.E["pe"]))


class Rot:
    def __init__(self, aps):
        self.items = [(a, Tile()) for a in aps]
        self.i = 0

    def next(self):
        it = self.items[self.i % len(self.items)]
        self.i += 1
        return it


class Prog:
    def __init__(self, dbg=None):
        self.dbg = dbg or {}
        self.nc = bass.Bass("TRN2", target_bir_lowering=False)
        self.st = ExitStack()
        self.R = Rec(self.nc, self.st)
        self.declare_io()

    def din(self, name, shape, dt=F32):
        if name in self.dbg.get("shrink", ()):
            return None
        return self.nc.dram_tensor(name, list(shape), dt, kind="ExternalInput").ap()

    def dout(self, name, shape, dt=F32):
        if name in self.dbg.get("shrink", ()):
            return self.nc.dram_tensor(name, list(shape), dt, kind="Internal").ap()
        return self.nc.dram_tensor(name, list(shape), dt, kind="ExternalOutput").ap()

    def dscr(self, name, shape, dt=F32):
        kind = "ExternalOutput" if name in self.dbg.get("dump", ()) else "Internal"
        return self.nc.dram_tensor(name, list(shape), dt, kind=kind).ap()

    def sb(self, st, name, shape, dt=F32):
        self.uid = getattr(self, "uid", 0) + 1
        return st.enter_context(self.nc.sbuf_tensor("%s_%d" % (name, self.uid), list(shape), dt))

    def declare_io(self):
        L = DEPTH
        self.x_p = self.din("x_prompt", [T, D])
        self.x_s = self.din("x_sample", [TS, D])
        self.c_sb_k = self.din("cache_sb_k", [L, T, MW])
        self.c_sb_v = self.din("cache_sb_v", [L, T, MW])
        self.c_bd_k = self.din("cache_band_k", [L, BAND, MW])
        self.c_bd_v = self.din("cache_band_v", [L, BAND, MW])
        self.st_re = self.din("state_ssm_re", [L, G, SN])
        self.st_im = self.din("state_ssm_im", [L, G, SN])
        self.norm_g = self.din("norm_g", [L, D])
        self.w_in = self.din("w_in", [L, D, NIN])
        self.a_re = self.din("ssm_a_re", [L, G, SN])
        self.a_im = self.din("ssm_a_im", [L, G, SN])
        self.log_dt = self.din("ssm_log_dt", [L, G])
        self.b_re = self.din("ssm_b_re", [L, G, SN, SP_])
        self.b_im = self.din("ssm_b_im", [L, G, SN, SP_])
        self.c_re = self.din("ssm_c_re", [L, G, SP_, SN])
        self.c_im = self.din("ssm_c_im", [L, G, SP_, SN])
        self.ssm_d = self.din("ssm_d", [L, MW])
        self.w_glu = self.din("w_glu", [L, MW, MW])
        self.b_glu = self.din("b_glu", [L, MW])
        self.qn_g = self.din("q_norm_g", [L, DH])
        self.kn_g = self.din("k_norm_g", [L, DH])
        self.rel_bias = self.din("rel_bias", [L, NH, 257])
        self.w_br = [self.din("w_br_a", [L, MW, D]), self.din("w_br_b", [L, MW, D]), self.din("w_br_c", [L, MW, D])]
        self.gate_b = self.din("gate_b", [L, 3 * D])
        self.w_out = self.din("w_out", [L, D, D])
        self.y_p = self.dout("y_prompt", [T, D])
        self.y_s = self.dout("y_sample", [TS, D])
        self.p_sb_k = self.dout("p_sb_k", [L, T, MW])
        self.p_sb_v = self.dout("p_sb_v", [L, T, MW])
        self.p_bd_k = self.dout("p_band_k", [L, BAND, MW])
        self.p_bd_v = self.dout("p_band_v", [L, BAND, MW])
        self.p_ss_re = self.dout("p_ssm_re", [L, G, SN])
        self.p_ss_im = self.dout("p_ssm_im", [L, G, SN])
        self.s_sb_k = self.dout("s_sb_k", [L, TS, MW])
        self.s_sb_v = self.dout("s_sb_v", [L, TS, MW])
        self.s_bd_k = self.dout("s_band_k", [L, TS, MW])
        self.s_bd_v = self.dout("s_band_v", [L, TS, MW])
        self.s_ss_re = self.dout("s_ssm_re", [L, G, SN])
        self.s_ss_im = self.dout("s_ssm_im", [L, G, SN])
        self.y1 = self.dscr("y1", [NT, D])
        self.uT = self.dscr("uT", [MW, NT])
        self.szT = [self.dscr("sz%dT" % i, [MW, NT]) for i in range(3)]
        self.qcT32 = self.dscr("qcT32", [MW, NT])
        self.gT = self.dscr("gT", [3 * D, NT])
        self.qb32 = self.dscr("qb32", [NT, MW])
        self.kb = self.dscr("kb", [NT, MW])
        self.vb = self.dscr("vb", [NT, MW])
        self.yT = [self.dscr("y%dT" % i, [MW, NT], BF16) for i in range(3)]
        self.mixT = self.dscr("mixT", [D, NT], BF16)
        self.yaT32 = self.dscr("yaT32", [MW, NT])

    def x_rows(self, l, tt):
        if l == 0:
            return self.x_p[tt * 128:(tt + 1) * 128, :] if tt < 16 else self.x_s[:, :]
        return self.y1[tt * 128:tt * 128 + trows(tt), :]

    def y_rows(self, l, tt):
        if l == DEPTH - 1:
            return self.y_p[tt * 128:(tt + 1) * 128, :] if tt < 16 else self.y_s[:, :]
        return self.y1[tt * 128:tt * 128 + trows(tt), :]

    def setup_consts(self):
        nc, R, st = self.nc, self.R, self.st
        self.ident = self.sb(st, "ident", [128, 128])
        self.identb = self.sb(st, "identb", [128, 128], BF16)
        self.t_ident = Tile()
        self.ps = [st.enter_context(nc.psum_tensor("ps%d" % i, [128, 512], F32)) for i in range(8)]
        self.psr = Rot([p for p in self.ps])
        ident, identb = self.ident, self.identb
        R.op("pool", lambda e: e.memset(ident[:], 0.0), outs=[self.t_ident])
        R.op("pool", lambda e: e.affine_select(out=ident[:], in_=ident[:], pattern=[[-1, 128]],
                                                compare_op=ALU.not_equal, fill=1.0, base=0,
                                                channel_multiplier=1), outs=[self.t_ident])
        R.op("pool", lambda e: e.tensor_copy(out=identb[:], in_=ident[:]), ins=[self.t_ident], outs=[self.t_ident])
        self.cst = self.sb(st, "cst", [128, 4])
        cst = self.cst
        self.eps_col = cst[:, 0:1]
        R.op("pool", lambda e: e.memset(cst[:, 0:1], EPS), outs=[self.t_ident])
        R.op("pool", lambda e: e.memset(cst[:, 1:2], 1.0), outs=[self.t_ident])
        R.op("pool", lambda e: e.memset(cst[:, 2:3], 0.0), outs=[self.t_ident])
        R.barrier()
        R.flush()

    def load_cols(self, st, name, src2d, n):
        R = self.R
        tmp = self.sb(st, name + "_rows", [128, 128])
        dst = self.sb(st, name, [128, n])
        t_tmp, t_dst = Tile(), Tile()
        R.dma("sp", tmp[:n, :], src2d, outs=[t_tmp])
        ps, t_ps = self.psr.next()
        ident = self.ident
        R.op("pe", lambda e: e.matmul(ps[:, :n], tmp[:n, :], ident[:n, :n], start=True, stop=True),
             ins=[t_tmp], outs=[t_ps])
        R.op("dve", lambda e: e.tensor_copy(out=dst[:, :n], in_=ps[:, :n]), ins=[t_ps], outs=[t_dst])
        return dst, t_dst

    def phase_norm(self, st, l, actT):
        R = self.R
        ready = []
        xts = Rot([self.sb(st, "n_xt%d" % i, [128, D]) for i in range(2)])
        junk = self.sb(st, "n_junk", [128, D], BF16)
        t_junk = Tile()
        gcol, t_g = self.load_cols(st, "n_gcol", self.norm_g[l].rearrange("(k p) -> k p", p=128), KT)
        small = Rot([self.sb(st, "n_sm%d" % i, [128, 4]) for i in range(2)])
        dg = Rot([self.sb(st, "n_dg%d" % i, [128, 128]) for i in range(2)])
        ident = self.ident
        for tt in range(NTT):
            r = trows(tt)
            xt, t_x = xts.next()
            R.dma("sp", xt[:r, :], self.x_rows(l, tt), outs=[t_x])
            sm, t_sm = small.next()
            R.op("act", lambda e, xt=xt, sm=sm, r=r: e.activation(out=junk[:r, :], in_=xt[:r, :], func=AF.Square,
                                                                 accum_out=sm[:r, 0:1]),
                 ins=[t_x], outs=[t_junk, t_sm])
            R.op("act", lambda e, sm=sm, r=r: e.activation(out=sm[:r, 1:2], in_=sm[:r, 0:1], func=AF.Sqrt,
                                                           scale=1.0 / D, bias=self.eps_col[:r, :]),
                 ins=[t_sm], outs=[t_sm])
            R.op("dve", lambda e, sm=sm, r=r: e.reciprocal(out=sm[:r, 2:3], in_=sm[:r, 1:2]), ins=[t_sm], outs=[t_sm])
            d, t_d = dg.next()
            R.op("dve", lambda e, d=d, sm=sm, r=r: e.tensor_scalar(out=d[:r, :r], in0=ident[:r, :r], scalar1=sm[:r, 2:3],
                                                                  scalar2=None, op0=ALU.mult),
                 ins=[t_sm, self.t_ident], outs=[t_d])
            for kg in range(KT // 4):
                ps, t_ps = self.psr.next()

                def mm(e, xt=xt, d=d, ps=ps, kg=kg, r=r):
                    for j in range(4):
                        k = kg * 4 + j
                        ins = e.matmul(ps[:, j * 128:j * 128 + r], xt[:r, k * 128:(k + 1) * 128], d[:r, :r],
                                       start=True, stop=True)
                    return ins
                R.op("pe", mm, ins=[t_x, t_d], outs=[t_ps])
                for j in range(4):
                    k = kg * 4 + j
                    dst = actT[:, k, tt * 128:tt * 128 + r]
                    if True:
                        ready.append(R.op("dve", lambda e, ps=ps, j=j, k=k, dst=dst, r=r: e.tensor_scalar(
                            out=dst, in0=ps[:, j * 128:j * 128 + r], scalar1=gcol[:, k:k + 1], scalar2=None,
                            op0=ALU.mult), ins=[t_ps, t_g]))
        return ready

    def load_slab(self, slabs, W, kt, c0, ncol):
        slab, t_slab = slabs.next()
        self.R.dma("pool", slab[:, :kt, :ncol], W[:, c0:c0 + ncol].rearrange("(k p) n -> p k n", p=128),
                   outs=[t_slab])
        return slab, t_slab

    def mm_F(self, slab, t_slab, cbl, actT, kt, ch, ready):
        ps, t_ps = self.psr.next()

        def mm(e):
            for k in range(kt):
                ins = e.matmul(ps[:, :CH], slab[:, k, cbl * 128:(cbl + 1) * 128], actT[:, k, ch * CH:(ch + 1) * CH],
                               start=(k == 0), stop=(k == kt - 1))
            return ins
        self.R.op("pe", mm, ins=[t_slab], outs=[t_ps], extra=ready)
        return ps, t_ps

    def mm_T(self, slab, t_slab, ncol, actT, kt, tt, ready):
        ps, t_ps = self.psr.next()
        r = trows(tt)

        def mm(e):
            for k in range(kt):
                ins = e.matmul(ps[:r, :ncol], actT[:, k, tt * 128:tt * 128 + r], slab[:, k, :ncol],
                               start=(k == 0), stop=(k == kt - 1))
            return ins
        self.R.op("pe", mm, ins=[t_slab], outs=[t_ps], extra=ready)
        return ps, t_ps

    def phase_gemm_in(self, st, l, actT, ready, slab_range=None):
        R = self.R
        W = self.w_in[l]
        slabs = Rot([self.sb(st, "g_slab%d" % i, [128, KT, 256], BF16) for i in range(2)])
        stF = Rot([self.sb(st, "g_stF%d" % i, [128, NT]) for i in range(2)])
        stT = Rot([self.sb(st, "g_stT%d" % i, [128, 256]) for i in range(4)])
        gb, t_c = self.load_cols(st, "g_gateb", self.gate_b[l].rearrange("(j p) -> j p", p=128), 96)
        qg = self.sb(st, "g_qg", [128, DH])
        kg = self.sb(st, "g_kg", [128, DH])
        R.dma("sp", qg[:], self.qn_g[l].partition_broadcast(128), outs=[t_c])
        R.dma("sp", kg[:], self.kn_g[l].partition_broadcast(128), outs=[t_c])
        R.op("dve", lambda e: e.tensor_scalar(out=qg[:], in0=qg[:], scalar1=SCALE, scalar2=None, op0=ALU.mult),
             ins=[t_c], outs=[t_c])
        small = Rot([self.sb(st, "g_sm%d" % i, [128, 8]) for i in range(4)])
        junk = self.sb(st, "g_junk", [128, 128], BF16)
        t_junk = Tile()

        regions = ["u", "z0", "qb", "kb", "vb", "z1", "qc", "kc", "vc", "z2"]
        nsl = NIN // 256
        for s in (slab_range if slab_range is not None else range(nsl)):
            c0 = s * 256
            reg = regions[c0 // MW] if c0 < 10 * MW else "g"
            rc = c0 % MW if reg != "g" else c0 - 10 * MW
            slab, t_slab = self.load_slab(slabs, W, KT, c0, 256)
            if reg in ("u", "z0", "z1", "z2", "qc", "g"):
                for cbl in range(2):
                    row0 = rc + cbl * 128
                    stg, t_stg = stF.next()
                    for ch in range(NCH):
                        ps, t_ps = self.mm_F(slab, t_slab, cbl, actT, KT, ch, ready)
                        dst = stg[:, ch * CH:(ch + 1) * CH]
                        if reg == "u":
                            R.op("dve", lambda e, dst=dst, ps=ps: e.tensor_copy(out=dst, in_=ps[:, :CH]),
                                 ins=[t_ps], outs=[t_stg])
                        elif reg == "qc":
                            R.op("dve", lambda e, dst=dst, ps=ps: e.tensor_copy(out=dst, in_=ps[:, :CH]),
                                 ins=[t_ps], outs=[t_stg])
                        elif reg == "g":
                            j = row0 // 128
                            R.op("act", lambda e, dst=dst, ps=ps, j=j: e.activation(
                                out=dst, in_=ps[:, :CH], func=AF.Sigmoid, bias=gb[:, j:j + 1]),
                                ins=[t_ps, t_c], outs=[t_stg])
                        else:
                            R.op("act", lambda e, dst=dst, ps=ps: e.activation(out=dst, in_=ps[:, :CH], func=AF.Silu),
                                 ins=[t_ps], outs=[t_stg])
                    if reg == "u":
                        dram = self.uT[row0:row0 + 128, :]
                    elif reg == "qc":
                        dram = self.qcT32[row0:row0 + 128, :]
                    elif reg == "g":
                        dram = self.gT[row0:row0 + 128, :]
                    else:
                        dram = self.szT[int(reg[1])][row0:row0 + 128, :]
                    R.dma("sp", dram, stg[:, :], ins=[t_stg])
            else:
                for tt in range(NTT):
                    r = trows(tt)
                    ps, t_ps = self.mm_T(slab, t_slab, 256, actT, KT, tt, ready)
                    stg, t_stg = stT.next()
                    if reg in ("qb", "kb"):
                        sm, t_sm = small.next()
                        for h in range(2):
                            R.op("act", lambda e, ps=ps, sm=sm, h=h, r=r: e.activation(
                                out=junk[:r, :], in_=ps[:r, h * 128:(h + 1) * 128], func=AF.Square,
                                accum_out=sm[:r, h:h + 1]), ins=[t_ps], outs=[t_junk, t_sm])
                        R.op("act", lambda e, sm=sm, r=r: e.activation(out=sm[:r, 2:4], in_=sm[:r, 0:2], func=AF.Sqrt,
                                                                       scale=1.0 / DH, bias=self.eps_col[:r, :]),
                             ins=[t_sm], outs=[t_sm])
                        R.op("dve", lambda e, sm=sm, r=r: e.reciprocal(out=sm[:r, 4:6], in_=sm[:r, 2:4]),
                             ins=[t_sm], outs=[t_sm])
                        gvec = qg if reg == "qb" else kg
                        for h in range(2):
                            R.op("dve", lambda e, ps=ps, sm=sm, h=h, r=r, stg=stg, gvec=gvec: e.scalar_tensor_tensor(
                                out=stg[:r, h * 128:(h + 1) * 128], in0=ps[:r, h * 128:(h + 1) * 128],
                                scalar=sm[:r, 4 + h:5 + h], in1=gvec[:r, :], op0=ALU.mult, op1=ALU.mult),
                                ins=[t_ps, t_sm, t_c], outs=[t_stg])
                    else:
                        R.op("dve", lambda e, ps=ps, stg=stg, r=r: e.tensor_copy(out=stg[:r, :], in_=ps[:r, :256]),
                             ins=[t_ps], outs=[t_stg])
                    cs = slice(rc, rc + 256)
                    tok0 = tt * 128
                    if reg == "qb":
                        R.dma("sp", self.qb32[tok0:tok0 + r, cs], stg[:r, :], ins=[t_stg])
                    elif reg in ("kb", "vb"):
                        scr = self.kb if reg == "kb" else self.vb
                        R.dma("sp", scr[tok0:tok0 + r, cs], stg[:r, :], ins=[t_stg])
                        pout = self.p_bd_k if reg == "kb" else self.p_bd_v
                        sout = self.s_bd_k if reg == "kb" else self.s_bd_v
                        if tt == 16:
                            R.dma("sp", sout[l, :, cs], stg[:r, :], ins=[t_stg])
                        elif tok0 >= T - BAND:
                            o0 = tok0 - (T - BAND)
                            R.dma("sp", pout[l, o0:o0 + 128, cs], stg[:r, :], ins=[t_stg])
                    else:
                        pout = self.p_sb_k if reg == "kc" else self.p_sb_v
                        sout = self.s_sb_k if reg == "kc" else self.s_sb_v
                        if tt == 16:
                            R.dma("sp", sout[l, :, cs], stg[:r, :], ins=[t_stg])
                        else:
                            R.dma("sp", pout[l, tok0:tok0 + 128, cs], stg[:r, :], ins=[t_stg])

    def run_phase(self, fn):
        with ExitStack() as st:
            fn(st)
            self.R.barrier()
            self.R.flush()

    def setup_masks(self):
        R, st = self.R, self.st
        self.m01 = self.sb(st, "m01", [128, 128])
        self.mneg = self.sb(st, "mneg", [128, 128])
        self.ones_col = self.cst[:, 1:2]
        m01, mneg = self.m01, self.mneg
        t = Tile()
        R.op("pool", lambda e: e.memset(m01[:], 1.0), outs=[t])
        R.op("pool", lambda e: e.affine_select(out=m01[:], in_=m01[:], pattern=[[-1, 128]], compare_op=ALU.is_gt,
                                                fill=0.0, base=0, channel_multiplier=1), outs=[t])
        R.op("pool", lambda e: e.memset(mneg[:], 0.0), outs=[t])
        R.op("pool", lambda e: e.affine_select(out=mneg[:], in_=mneg[:], pattern=[[-1, 128]], compare_op=ALU.is_gt,
                                                fill=-1e30, base=0, channel_multiplier=1), outs=[t])
        R.barrier()
        R.flush()

    def transpose_bf16(self, src_blocks, dst_fn, ins, outs_tile, evac_eng="act"):
        R = self.R
        identb = self.identb
        i = 0
        toks = []
        while i < len(src_blocks):
            grp = src_blocks[i:i + 4]
            ps, t_ps = self.psr.next()
            psb = ps[:].bitcast(BF16)

            def tr(e, grp=grp, psb=psb):
                for j, (ap, nk, nq) in enumerate(grp):
                    ins_ = e.transpose(psb[:nq, j * 128:j * 128 + nk], ap, identb[:nk, :nk])
                return ins_
            R.op("pe", tr, ins=ins, outs=[t_ps])
            for j, (ap, nk, nq) in enumerate(grp):
                dst = dst_fn(i + j)
                if evac_eng == "act":
                    toks.append(R.op("act", lambda e, dst=dst, psb=psb, j=j, nk=nk, nq=nq: e.activation(
                        out=dst, in_=psb[:nq, j * 128:j * 128 + nk], func=AF.Copy), ins=[t_ps], outs=[outs_tile]))
                else:
                    toks.append(R.op("dve", lambda e, dst=dst, psb=psb, j=j, nk=nk, nq=nq: e.tensor_copy(
                        out=dst, in_=psb[:nq, j * 128:j * 128 + nk]), ins=[t_ps], outs=[outs_tile]))
            i += 4
        return toks

    def phase_sb(self, st, l, heads=range(NH), qblocks=range(16), do_sample=True):
        R = self.R
        SM = T + TS
        heads = list(heads)

        class HD:
            pass
        hds = []
        for i in range(2):
            hd = HD()
            hd.qT = self.sb(st, "c_qT", [128, NT], BF16)
            hd.kld = self.sb(st, "c_kld", [128, 16, 128], BF16)
            hd.kT = self.sb(st, "c_kT", [128, T], BF16)
            hd.kld2 = self.sb(st, "c_kld2", [128, 16, 128], BF16)
            hd.kTs = self.sb(st, "c_kTs", [128, SM], BF16)
            hd.ksn = self.sb(st, "c_ksn", [TS, 128], BF16)
            hd.V = self.sb(st, "c_V", [128, 16, 128], BF16)
            hd.Vc = self.sb(st, "c_Vc", [128, 16, 128], BF16)
            hd.Vn = self.sb(st, "c_Vn", [TS, 128], BF16)
            hd.sz = self.sb(st, "c_sz", [128, NT])
            hd.Y = self.sb(st, "c_Y", [128, NT], BF16)
            (hd.t_q, hd.t_kld, hd.t_kld2, hd.t_kT, hd.t_kTs, hd.t_ksn, hd.t_V, hd.t_Vc, hd.t_Vn, hd.t_sz, hd.t_Y) = \
                [Tile() for _ in range(11)]
            hds.append(hd)
        Eb = Rot([self.sb(st, "c_E%d" % i, [128, SM]) for i in range(3)])
        NLb = Rot([self.sb(st, "c_NL%d" % i, [128, SM]) for i in range(4)])
        Gb = Rot([self.sb(st, "c_G%d" % i, [128, SM]) for i in range(2)])
        Wb = Rot([self.sb(st, "c_W%d" % i, [128, SM], BF16) for i in range(2)])
        WTb = Rot([self.sb(st, "c_WT%d" % i, [128, 17, 128], BF16) for i in range(3)])
        ng = Rot([self.sb(st, "c_ng%d" % i, [128, 1]) for i in range(3)])
        identb = self.identb
        m01, mneg = self.m01, self.mneg
        ones_col = self.ones_col

        class Ctx:
            pass

        def st_z_el(desc):
            (hd, qlo, r, kTa, S, vblocks, d0) = desc
            c = Ctx()
            c.hd, c.qlo, c.r, c.S, c.vblocks, c.d0 = hd, qlo, r, S, vblocks, d0
            c.E, c.t_E = Eb.next()
            c.pss = []
            nchunk = (S + 511) // 512
            qT = hd.qT
            for ci in range(nchunk):
                n = min(512, S - ci * 512)
                ps, t_ps = self.psr.next()
                R.op("pe", lambda e, ps=ps, ci=ci, n=n: e.matmul(ps[:r, :n], qT[:, qlo:qlo + r], kTa[:, ci * 512:ci * 512 + n],
                                                                 start=True, stop=True),
                     ins=[hd.t_q, hd.t_kT, hd.t_kTs], outs=[t_ps])
                cs = slice(ci * 512, ci * 512 + n)
                E = c.E
                R.op("act", lambda e, ps=ps, n=n, cs=cs, E=E: e.activation(out=E[:r, cs], in_=ps[:r, :n], func=AF.Exp, scale=-SCALE),
                     ins=[t_ps], outs=[c.t_E])
                R.op("act", lambda e, cs=cs, E=E: e.activation(out=E[:r, cs], in_=E[:r, cs], func=AF.Ln, bias=ones_col[:r, :]),
                     ins=[c.t_E], outs=[c.t_E])
                c.pss.append((ps, t_ps, n, cs))
            return c

        def st_nlf(c):
            r, d0, E = c.r, c.d0, c.E
            c.NL, c.t_NL = NLb.next()
            NL = c.NL
            for (ps, t_ps, n, cs) in c.pss:
                R.op("dve", lambda e, ps=ps, n=n, cs=cs: e.scalar_tensor_tensor(
                    out=NL[:r, cs], in0=ps[:r, :n], scalar=SCALE, in1=E[:r, cs], op0=ALU.mult, op1=ALU.add),
                    ins=[t_ps, c.t_E], outs=[c.t_NL])
            R.op("dve", lambda e: e.tensor_tensor(out=NL[:r, d0:d0 + r], in0=NL[:r, d0:d0 + r], in1=m01[:r, :r], op=ALU.mult),
                 ins=[c.t_NL], outs=[c.t_NL])

        def st_scan_arg(c):
            r, S, d0, E, NL = c.r, c.S, c.d0, c.E, c.NL
            Gt, t_G = Gb.next()
            c.ngc, c.t_ng = ng.next()
            ngc = c.ngc
            R.op("dve", lambda e: e.tensor_tensor_scan(out=Gt[:r, :S], data0=ones_col[:r, :].to_broadcast([r, S]),
                                                       data1=NL[:r, :S], initial=0.0, op0=ALU.mult, op1=ALU.add),
                 ins=[c.t_NL], outs=[t_G])
            R.op("dve", lambda e: e.tensor_scalar(out=ngc[:r, :], in0=Gt[:r, S - 1:S], scalar1=-1.0, scalar2=None, op0=ALU.mult),
                 ins=[t_G], outs=[c.t_ng])
            R.op("pool", lambda e: e.tensor_tensor(out=NL[:r, :S], in0=Gt[:r, :S], in1=E[:r, :S], op=ALU.subtract),
                 ins=[t_G, c.t_E], outs=[c.t_NL])
            R.op("pool", lambda e: e.tensor_tensor(out=NL[:r, d0:d0 + r], in0=NL[:r, d0:d0 + r], in1=mneg[:r, :r], op=ALU.add),
                 ins=[c.t_NL], outs=[c.t_NL])

        def st_exp2(c):
            r, S, NL, ngc = c.r, c.S, c.NL, c.ngc
            c.W, c.t_W = Wb.next()
            W = c.W
            R.op("act", lambda e: e.activation(out=W[:r, :S], in_=NL[:r, :S], func=AF.Exp, bias=ngc[:r, :]),
                 ins=[c.t_NL, c.t_ng], outs=[c.t_W])

        def st_t_cp(c):
            r, S, W = c.r, c.S, c.W
            c.WT, c.t_WT = WTb.next()
            WT = c.WT
            nb = (S + 127) // 128
            c.nb = nb
            b0 = 0
            while b0 < nb:
                grp = list(range(b0, min(b0 + 4, nb)))
                ps, t_ps = self.psr.next()
                psb = ps[:].bitcast(BF16)

                def tr(e, grp=grp, psb=psb):
                    for j, b in enumerate(grp):
                        nk = min(128, S - b * 128)
                        ins_ = e.transpose(psb[:nk, j * 128:j * 128 + r], W[:r, b * 128:b * 128 + nk], identb[:r, :r])
                    return ins_
                R.op("pe", tr, ins=[c.t_W], outs=[t_ps])
                full = [b for b in grp if S - b * 128 >= 128]
                if full:
                    nf = len(full)
                    if r == 128:
                        R.op("act", lambda e, psb=psb, f0=full[0], nf=nf: e.activation(
                            out=WT[:, f0:f0 + nf, :], in_=psb[:, 0:nf * 128].rearrange("p (b q) -> p b q", q=128), func=AF.Copy),
                            ins=[t_ps], outs=[c.t_WT])
                    else:
                        for j, b in enumerate(full):
                            R.op("act", lambda e, psb=psb, j=j, b=b: e.activation(
                                out=WT[:, b, :r], in_=psb[:, j * 128:j * 128 + r], func=AF.Copy), ins=[t_ps], outs=[c.t_WT])
                for j, b in enumerate(grp):
                    nk = min(128, S - b * 128)
                    if nk < 128:
                        R.op("act", lambda e, psb=psb, j=j, b=b, nk=nk: e.activation(
                            out=WT[:nk, b, :r], in_=psb[:nk, j * 128:j * 128 + r], func=AF.Copy), ins=[t_ps], outs=[c.t_WT])
                b0 += 4

        def st_pv_y(c):
            hd, r, qlo, vblocks, WT, nb = c.hd, c.r, c.qlo, c.vblocks, c.WT, c.nb
            ps, t_ps = self.psr.next()
            Y, sz = hd.Y, hd.sz

            def pv(e):
                for b in range(nb):
                    vap, nk = vblocks[b]
                    ins_ = e.matmul(ps[:, :r], vap, WT[:nk, b, :r], start=(b == 0), stop=(b == nb - 1))
                return ins_
            R.op("pe", pv, ins=[c.t_WT, hd.t_V, hd.t_Vc, hd.t_Vn], outs=[t_ps])
            R.op("dve", lambda e: e.tensor_tensor(out=Y[:, qlo:qlo + r], in0=ps[:, :r], in1=sz[:, qlo:qlo + r], op=ALU.mult),
                 ins=[t_ps, hd.t_sz], outs=[hd.t_Y])

        def load_head(hd, h):
            hs = slice(h * 128, (h + 1) * 128)
            if len(qblocks) < 16:
                R.op("pool", lambda e: e.memset(hd.Y[:, :], 0.0), outs=[hd.t_Y])
            R.dma("pool", hd.qT[:, :], self.qcT32[hs, :], outs=[hd.t_q])
            R.dma("sp", hd.sz[:, :], self.szT[2][hs, :], outs=[hd.t_sz])
            R.dma("pool", hd.kld[:, :, :], self.p_sb_k[l][:, hs].rearrange("(b p) d -> p b d", p=128), outs=[hd.t_kld])
            R.dma("pool", hd.V[:, :, :], self.p_sb_v[l][:, hs].rearrange("(b p) d -> p b d", p=128), outs=[hd.t_V])
            if do_sample:
                R.dma("pool", hd.kld2[:, :, :], self.c_sb_k[l][:, hs].rearrange("(b p) d -> p b d", p=128), outs=[hd.t_kld2])
                R.dma("pool", hd.Vc[:, :, :], self.c_sb_v[l][:, hs].rearrange("(b p) d -> p b d", p=128), outs=[hd.t_Vc])
                R.dma("pool", hd.ksn[:, :], self.s_sb_k[l][:, hs], outs=[hd.t_ksn])
                R.dma("pool", hd.Vn[:, :], self.s_sb_v[l][:, hs], outs=[hd.t_Vn])

        def prep_head(hd):
            kT, kTs, kld, kld2, ksn = hd.kT, hd.kTs, hd.kld, hd.kld2, hd.ksn
            self.transpose_bf16([(kld[:, b, :], 128, 128) for b in range(16)],
                                lambda i: kT[:, i * 128:(i + 1) * 128], ins=[hd.t_kld], outs_tile=hd.t_kT, evac_eng="dve")
            if do_sample:
                self.transpose_bf16([(kld2[:, b, :], 128, 128) for b in range(16)],
                                    lambda i: kTs[:, i * 128:(i + 1) * 128], ins=[hd.t_kld2], outs_tile=hd.t_kTs, evac_eng="dve")
                self.transpose_bf16([(ksn[:, :], TS, 128)], lambda i: kTs[:, T:T + TS], ins=[hd.t_ksn], outs_tile=hd.t_kTs,
                                    evac_eng="dve")

        def head_descs(hd):
            d = []
            for qb in qblocks:
                S = 128 * (qb + 1)
                d.append((hd, qb * 128, 128, hd.kT, S, [(hd.V[:, b, :], 128) for b in range(qb + 1)], qb * 128))
            if do_sample:
                d.append((hd, T, TS, hd.kTs, SM, [(hd.Vc[:, b, :], 128) for b in range(16)] + [(hd.Vn[:, :], TS)], T))
            return d

        descs, pre_hooks, post_hooks = [], {}, {}
        for i, h in enumerate(heads):
            hd = hds[i % 2]
            base = len(descs)
            dl = head_descs(hd)
            descs.extend(dl)
            nbk = len(dl)
            if i == 0:
                load_head(hd, h)
                prep_head(hd)
            if i + 1 < len(heads):
                nhd, nh = hds[(i + 1) % 2], heads[i + 1]
                pre_hooks.setdefault(base + min(3, nbk - 1), []).append(lambda nhd=nhd, nh=nh: load_head(nhd, nh))
                pre_hooks.setdefault(base + max(min(3, nbk - 1), nbk - 4), []).append(lambda nhd=nhd: prep_head(nhd))
            hs = slice(h * 128, (h + 1) * 128)
            post_hooks.setdefault(base + nbk - 1, []).append(
                lambda hd=hd, hs=hs: R.dma("sp", self.yT[2][hs, :], hd.Y[:, :], ins=[hd.t_Y]))
        n = len(descs)
        ctx = {}
        for j in range(-2, n + 1):
            if 0 <= j - 1 < n:
                st_pv_y(ctx[j - 1])
                del ctx[j - 1]
                for f in post_hooks.get(j - 1, []):
                    f()
            if 0 <= j < n:
                st_exp2(ctx[j])
            if 0 <= j + 2 < n:
                for f in pre_hooks.get(j + 2, []):
                    f()
                ctx[j + 2] = st_z_el(descs[j + 2])
            if 0 <= j < n:
                st_t_cp(ctx[j])
            if 0 <= j + 1 < n:
                st_scan_arg(ctx[j + 1])
            if 0 <= j + 2 < n:
                st_nlf(ctx[j + 2])

    def setup_band_consts(self):
        R, st = self.R, self.st
        self.Jb = self.sb(st, "Jb", [128, 128], BF16)
        self.J32 = self.sb(st, "J32", [128, 128])
        self.ones32 = self.sb(st, "ones32", [128, 128])
        J32, Jb, ones32 = self.J32, self.Jb, self.ones32
        t = Tile()
        R.op("pool", lambda e: e.memset(J32[:], 0.0), outs=[t])
        R.op("pool", lambda e: e.affine_select(out=J32[:], in_=J32[:], pattern=[[1, 128]], compare_op=ALU.not_equal,
                                                fill=1.0, base=-127, channel_multiplier=1), outs=[t])
        R.op("pool", lambda e: e.tensor_copy(out=Jb[:], in_=J32[:]), ins=[t], outs=[t])
        R.op("pool", lambda e: e.memset(ones32[:], 1.0), outs=[t])
        self.ext = self.dscr("ext_bias", [NH, 768])
        R.barrier()
        R.flush()

    def phase_band(self, st, l, heads=range(NH), mblocks=range(16), do_sample=True):
        R = self.R
        ident, identb, Jb, ones32 = self.ident, self.identb, self.Jb, self.ones32
        rb = self.sb(st, "b_rb", [NH, 257])
        exs = self.sb(st, "b_exs", [NH, 768])
        t_rb, t_ext = Tile(), Tile()
        R.dma("sp", rb[:, :], self.rel_bias[l], outs=[t_rb])
        R.op("dve", lambda e: e.tensor_copy(out=exs[:, 0:256], in_=rb[:, 1:257]), ins=[t_rb], outs=[t_ext])
        R.op("dve", lambda e: e.tensor_copy(out=exs[:, 256:768], in_=rb[:, 256:257].to_broadcast([NH, 512])),
             ins=[t_rb], outs=[t_ext])
        t_extd = Tile()
        R.dma("sp", self.ext[:, :], exs[:, :], ins=[t_ext], outs=[t_extd])

        HK = self.sb(st, "b_HK", [128, 5, 128], BF16)
        qld = self.sb(st, "b_qld", [128, 16, 128], BF16)
        qls = self.sb(st, "b_qls", [TS, 128], BF16)
        kld = self.sb(st, "b_kld", [128, 16, 128], BF16)
        kls = self.sb(st, "b_kls", [TS, 128], BF16)
        kcl = self.sb(st, "b_kcl", [128, 4, 128], BF16)
        qT = self.sb(st, "b_qT", [128, NT], BF16)
        kT = self.sb(st, "b_kT", [128, NT], BF16)
        kcT = self.sb(st, "b_kcT", [128, BAND], BF16)
        V = self.sb(st, "b_V", [128, 16, 128], BF16)
        Vn = self.sb(st, "b_Vn", [TS, 128], BF16)
        Vc = self.sb(st, "b_Vc", [128, 4, 128], BF16)
        sz = self.sb(st, "b_sz", [128, NT])
        Y = self.sb(st, "b_Y", [128, NT], BF16)
        t_HK, t_qld, t_kld, t_kcl, t_qT, t_kT, t_kcT, t_V, t_sz, t_Y = [Tile() for _ in range(10)]
        Pb = Rot([self.sb(st, "b_P%d" % i, [128, 640], BF16) for i in range(3)])
        PTb = Rot([self.sb(st, "b_PT%d" % i, [128, 5, 128], BF16) for i in range(3)])
        smb = Rot([self.sb(st, "b_sm%d" % i, [128, 8]) for i in range(3)])
        D32b = Rot([self.sb(st, "b_D%d" % i, [128, 128]) for i in range(3)])
        tmpb = Rot([self.sb(st, "b_tmp%d" % i, [128, 128]) for i in range(2)])
        for (P_, tP) in Pb.items:
            R.op("pool", lambda e, P_=P_: e.memset(P_[:, :], 0.0), outs=[tP])

        class Ctx:
            pass

        def b_qk(desc):
            (qlo, r, segs, vblocks) = desc
            c = Ctx()
            c.qlo, c.r, c.segs, c.vblocks = qlo, r, segs, vblocks
            c.psA, c.t_A = self.psr.next()
            c.psB, c.t_B = self.psr.next()
            psA, psB, t_A, t_B = c.psA, c.psB, c.t_A, c.t_B
            pss = [(psA, t_A), (psB, t_B)]
            for (pi, c0, n, kap, kc0) in segs:
                ps, t_ps = pss[pi]
                cc = kc0 // 128

                def mm(e, ps=ps, c0=c0, n=n, kap=kap, cc=cc):
                    e.matmul(ps[:r, c0:c0 + n], qT[:, qlo:qlo + r], kap, start=True, stop=False)
                    return e.matmul(ps[:r, c0:c0 + n], HK[:, 4 - cc, 0:r], Jb[:, 0:n], start=False, stop=True)
                R.op("pe", mm, ins=[t_qT, t_kT, t_kcT, t_HK], outs=[t_ps])
            c.sm, c.t_sm = smb.next()
            sm, t_sm = c.sm, c.t_sm
            nA = sum(n for (pi, c0, n, kap, kc0) in segs if pi == 0)
            a0 = min([c0 for (pi, c0, n, kap, kc0) in segs if pi == 0] + [512])
            nB = sum(n for (pi, c0, n, kap, kc0) in segs if pi == 1)
            R.op("dve", lambda e: e.tensor_reduce(out=sm[:r, 1:2], in_=psB[:r, 0:nB], axis=AX.X, op=ALU.max),
                 ins=[t_B], outs=[t_sm])
            if nA > 0:
                R.op("dve", lambda e: e.tensor_reduce(out=sm[:r, 0:1], in_=psA[:r, a0:a0 + nA], axis=AX.X, op=ALU.max),
                     ins=[t_A], outs=[t_sm])
                R.op("dve", lambda e: e.tensor_scalar(out=sm[:r, 2:3], in0=sm[:r, 0:1], scalar1=sm[:r, 1:2], scalar2=-1.0,
                                                      op0=ALU.max, op1=ALU.mult), ins=[t_sm], outs=[t_sm])
            else:
                R.op("dve", lambda e: e.tensor_scalar(out=sm[:r, 2:3], in0=sm[:r, 1:2], scalar1=-1.0, scalar2=None,
                                                      op0=ALU.mult), ins=[t_sm], outs=[t_sm])
            return c

        def b_exp(c):
            r, segs, sm, t_sm, psA, psB, t_A, t_B = c.r, c.segs, c.sm, c.t_sm, c.psA, c.psB, c.t_A, c.t_B
            lo = segs[0][4]
            c.P, c.t_P = Pb.next()
            P_, t_P = c.P, c.t_P
            if r == 128:
                halves = [(0, 64, lo, 576), (64, 128, max(lo, 64), 640)]
            else:
                halves = [(0, r, 0, BAND + TS)]
            ncol = 3
            for (p0, p1, v0, v1) in halves:
                if v0 < 512:
                    e1 = min(v1, 512)
                    R.op("act", lambda e, p0=p0, p1=p1, v0=v0, e1=e1, ncol=ncol: e.activation(
                        out=P_[p0:p1, v0:e1], in_=psA[p0:p1, v0:e1], func=AF.Exp, bias=sm[p0:p1, 2:3],
                        accum_out=sm[p0:p1, ncol:ncol + 1]), ins=[t_A, t_sm], outs=[t_P, t_sm])
                else:
                    R.op("pool", lambda e, p0=p0, p1=p1, ncol=ncol: e.memset(sm[p0:p1, ncol:ncol + 1], 0.0), outs=[t_sm])
                R.op("act", lambda e, p0=p0, p1=p1, v1=v1, ncol=ncol: e.activation(
                    out=P_[p0:p1, 512:v1], in_=psB[p0:p1, 0:v1 - 512], func=AF.Exp, bias=sm[p0:p1, 2:3],
                    accum_out=sm[p0:p1, ncol + 1:ncol + 2]), ins=[t_B, t_sm], outs=[t_P, t_sm])
            R.op("dve", lambda e: e.tensor_tensor(out=sm[:r, 5:6], in0=sm[:r, 3:4], in1=sm[:r, 4:5], op=ALU.add),
                 ins=[t_sm], outs=[t_sm])
            R.op("dve", lambda e: e.reciprocal(out=sm[:r, 6:7], in_=sm[:r, 5:6]), ins=[t_sm], outs=[t_sm])
            c.D32, c.t_D = D32b.next()
            D32 = c.D32
            R.op("dve", lambda e: e.tensor_scalar(out=D32[:r, :r], in0=ident[:r, :r], scalar1=sm[:r, 6:7], scalar2=None,
                                                  op0=ALU.mult), ins=[t_sm], outs=[c.t_D])

        def b_t_cp(c):
            r, vblocks, P_ = c.r, c.vblocks, c.P
            c.PT, c.t_PT = PTb.next()
            PT = c.PT
            nb = len(vblocks)
            b0 = 0
            while b0 < nb:
                grp = list(range(b0, min(b0 + 4, nb)))
                ps, t_ps = self.psr.next()
                psb = ps[:].bitcast(BF16)

                def tr(e, grp=grp, psb=psb):
                    for j, b in enumerate(grp):
                        (vap, nk, kc0) = vblocks[b]
                        ins_ = e.transpose(psb[:nk, j * 128:j * 128 + r], P_[:r, kc0:kc0 + nk], identb[:r, :r])
                    return ins_
                R.op("pe", tr, ins=[c.t_P], outs=[t_ps])
                full = [b for b in grp if vblocks[b][1] == 128]
                if full and r == 128:
                    nf = len(full)
                    R.op("act", lambda e, psb=psb, f0=full[0], nf=nf: e.activation(
                        out=PT[:, f0:f0 + nf, :], in_=psb[:, 0:nf * 128].rearrange("p (b q) -> p b q", q=128), func=AF.Copy),
                        ins=[t_ps], outs=[c.t_PT])
                else:
                    for j, b in enumerate(grp):
                        if vblocks[b][1] == 128:
                            R.op("act", lambda e, psb=psb, j=j, b=b: e.activation(
                                out=PT[:, b, :r], in_=psb[:, j * 128:j * 128 + r], func=AF.Copy), ins=[t_ps], outs=[c.t_PT])
                for j, b in enumerate(grp):
                    nk = vblocks[b][1]
                    if nk < 128:
                        R.op("act", lambda e, psb=psb, j=j, b=b, nk=nk: e.activation(
                            out=PT[:nk, b, :r], in_=psb[:nk, j * 128:j * 128 + r], func=AF.Copy), ins=[t_ps], outs=[c.t_PT])
                b0 += 4

        def b_pv_y(c):
            r, qlo, vblocks, PT, D32 = c.r, c.qlo, c.vblocks, c.PT, c.D32
            pso, t_o = self.psr.next()
            nb = len(vblocks)

            def pv(e):
                for b, (vap, nk, kc0) in enumerate(vblocks):
                    ins_ = e.matmul(pso[:, :r], vap, PT[:nk, b, :r], start=(b == 0), stop=(b == nb - 1))
                return ins_
            R.op("pe", pv, ins=[c.t_PT, t_V], outs=[t_o])
            psr_, t_r = self.psr.next()
            R.op("pe", lambda e: e.matmul(psr_[:, :r], ones32[:r, :], D32[:r, :r], start=True, stop=True),
                 ins=[c.t_D], outs=[t_r])
            tmp, t_tmp = tmpb.next()
            R.op("dve", lambda e: e.tensor_tensor(out=tmp[:, :r], in0=pso[:, :r], in1=sz[:, qlo:qlo + r], op=ALU.mult),
                 ins=[t_o, t_sz], outs=[t_tmp])
            R.op("dve", lambda e: e.tensor_tensor(out=Y[:, qlo:qlo + r], in0=psr_[:, :r], in1=tmp[:, :r], op=ALU.mult),
                 ins=[t_r, t_tmp], outs=[t_Y])

        def run_pipeline(descs):
            n = len(descs)
            ctx = {}
            for j in range(-2, n + 1):
                if 0 <= j - 1 < n:
                    b_pv_y(ctx[j - 1])
                    del ctx[j - 1]
                if 0 <= j + 2 < n:
                    ctx[j + 2] = b_qk(descs[j + 2])
                if 0 <= j + 1 < n:
                    b_exp(ctx[j + 1])
                if 0 <= j < n:
                    b_t_cp(ctx[j])

        for h in heads:
            hs = slice(h * 128, (h + 1) * 128)
            descs = []
            if len(mblocks) < 16:
                R.op("pool", lambda e: e.memset(Y[:, :], 0.0), outs=[t_Y])
            R.dma("pool", HK[:, :, :], bass.AP(self.ext.tensor, self.ext[h, 0:1].offset, [[1, 128], [128, 5], [1, 128]]),
                  ins=[t_extd], outs=[t_HK])
            R.dma("pool", qld[:, :, :], self.qb32[0:T, hs].rearrange("(b p) d -> p b d", p=128), outs=[t_qld])
            R.dma("pool", qls[:, :], self.qb32[T:NT, hs], outs=[t_qld])
            R.dma("pool", kld[:, :, :], self.kb[0:T, hs].rearrange("(b p) d -> p b d", p=128), outs=[t_kld])
            R.dma("pool", kls[:, :], self.kb[T:NT, hs], outs=[t_kld])
            R.dma("pool", kcl[:, :, :], self.c_bd_k[l][:, hs].rearrange("(b p) d -> p b d", p=128), outs=[t_kcl])
            R.dma("pool", V[:, :, :], self.vb[0:T, hs].rearrange("(b p) d -> p b d", p=128), outs=[t_V])
            R.dma("pool", Vn[:, :], self.vb[T:NT, hs], outs=[t_V])
            R.dma("pool", Vc[:, :, :], self.c_bd_v[l][:, hs].rearrange("(b p) d -> p b d", p=128), outs=[t_V])
            R.dma("sp", sz[:, :], self.szT[1][hs, :], outs=[t_sz])
            self.transpose_bf16([(qld[:, b, :], 128, 128) for b in range(16)] + [(qls[:, :], TS, 128)],
                                lambda i: qT[:, i * 128:i * 128 + (128 if i < 16 else TS)], ins=[t_qld], outs_tile=t_qT,
                                evac_eng="dve")
            self.transpose_bf16([(kld[:, b, :], 128, 128) for b in range(16)] + [(kls[:, :], TS, 128)],
                                lambda i: kT[:, i * 128:i * 128 + (128 if i < 16 else TS)], ins=[t_kld], outs_tile=t_kT,
                                evac_eng="dve")
            self.transpose_bf16([(kcl[:, b, :], 128, 128) for b in range(4)],
                                lambda i: kcT[:, i * 128:(i + 1) * 128], ins=[t_kcl], outs_tile=t_kcT, evac_eng="dve")
            for m in mblocks:
                lo = max(0, 512 - 128 * m)
                w0 = 128 * m - 512
                segs, vbl = [], []
                for c in range(lo // 128, 5):
                    kc0 = c * 128
                    segs.append((0 if c < 4 else 1, kc0 if c < 4 else 0, 128, kT[:, w0 + kc0:w0 + kc0 + 128], kc0))
                    vbl.append((V[:, (w0 + kc0) // 128, :], 128, kc0))
                descs.append((m * 128, 128, segs, vbl))
            if do_sample:
                segs = [(0, c * 128, 128, kcT[:, c * 128:(c + 1) * 128], c * 128) for c in range(4)]
                segs.append((1, 0, TS, kT[:, T:T + TS], 512))
                vbl = [(Vc[:, c, :], 128, c * 128) for c in range(4)] + [(Vn[:, :], TS, 512)]
                descs.append((T, TS, segs, vbl))
            run_pipeline(descs)
            R.dma("sp", self.yT[1][hs, :], Y[:, :], ins=[t_Y])

    def phase_merge(self, st, l, slab_range=None):
        R = self.R
        yts = []
        t_y = Tile()
        for i in range(3):
            yt = self.sb(st, "f_y%d" % i, [128, 8, NT], BF16)
            R.dma("sp", yt[:, :, :], self.yT[i].rearrange("(k p) n -> p k n", p=128), outs=[t_y])
            yts.append(yt)
        slabs = [Rot([self.sb(st, "f_slab%d_%d" % (i, j), [128, 8, 256], BF16) for j in range(2)]) for i in range(3)]
        gts = [Rot([self.sb(st, "f_g%d_%d" % (i, j), [128, NT]) for j in range(2)]) for i in range(3)]
        accs = Rot([self.sb(st, "f_acc%d" % j, [128, CH]) for j in range(3)])
        tmps = Rot([self.sb(st, "f_tmp%d" % j, [128, CH]) for j in range(3)])
        stg = Rot([self.sb(st, "f_stg%d" % j, [128, NT], BF16) for j in range(2)])
        for s in (slab_range if slab_range is not None else range(D // 256)):
            c0 = s * 256
            sl = [self.load_slab(slabs[i], self.w_br[i][l], 8, c0, 256) for i in range(3)]
            for cbl in range(2):
                fb = c0 // 128 + cbl
                gs = []
                for i in range(3):
                    g, t_g = gts[i].next()
                    R.dma("sp", g[:, :], self.gT[i * D + fb * 128:i * D + (fb + 1) * 128, :], outs=[t_g])
                    gs.append((g, t_g))
                so, t_so = stg.next()
                for ch in range(NCH):
                    cs = slice(ch * CH, (ch + 1) * CH)
                    pss = []
                    for i in range(3):
                        slab, t_slab = sl[i]
                        pss.append(self.mm_F(slab, t_slab, cbl, yts[i], 8, ch, [t_y.w]))
                    acc, t_acc = accs.next()
                    tmp, t_tmp = tmps.next()
                    R.op("dve", lambda e, acc=acc, ps=pss[0][0], g=gs[0][0], cs=cs: e.tensor_tensor(
                        out=acc[:, :], in0=ps[:, :CH], in1=g[:, cs], op=ALU.mult), ins=[pss[0][1], gs[0][1]], outs=[t_acc])
                    R.op("dve", lambda e, tmp=tmp, ps=pss[1][0], g=gs[1][0], cs=cs: e.tensor_tensor(
                        out=tmp[:, :], in0=ps[:, :CH], in1=g[:, cs], op=ALU.mult), ins=[pss[1][1], gs[1][1]], outs=[t_tmp])
                    R.op("pool", lambda e, acc=acc, tmp=tmp: e.tensor_tensor(out=acc[:, :], in0=acc[:, :], in1=tmp[:, :], op=ALU.add),
                         ins=[t_tmp], outs=[t_acc])
                    R.op("dve", lambda e, tmp=tmp, ps=pss[2][0], g=gs[2][0], cs=cs: e.tensor_tensor(
                        out=tmp[:, :], in0=ps[:, :CH], in1=g[:, cs], op=ALU.mult), ins=[pss[2][1], gs[2][1]], outs=[t_tmp])
                    R.op("pool", lambda e, acc=acc, tmp=tmp, so=so, cs=cs: e.tensor_tensor(out=so[:, cs], in0=acc[:, :], in1=tmp[:, :],
                                                                                         op=ALU.add),
                         ins=[t_tmp, t_acc], outs=[t_so])
                R.dma("sp", self.mixT[fb * 128:(fb + 1) * 128, :], so[:, :], ins=[t_so])

    def phase_out(self, st, l, actT, slab_range=None):
        R = self.R
        t_a = Tile()
        R.dma("sp", actT[:, :, :], self.mixT.rearrange("(k p) n -> p k n", p=128), outs=[t_a])
        OC = 512
        slabs = Rot([self.sb(st, "o_slab%d" % i, [128, KT, OC], BF16) for i in range(2)])
        xts = Rot([self.sb(st, "o_x%d" % i, [128, OC]) for i in range(3)])
        for s in (slab_range if slab_range is not None else range(D // OC)):
            c0 = s * OC
            slab, t_slab = self.load_slab(slabs, self.w_out[l], KT, c0, OC)
            for tt in range(NTT):
                r = trows(tt)
                xt, t_x = xts.next()
                R.dma("sp", xt[:r, :], self.x_rows(l, tt)[:, c0:c0 + OC], outs=[t_x])
                ps, t_ps = self.mm_T(slab, t_slab, OC, actT, KT, tt, [t_a.w])
                R.op("dve", lambda e, xt=xt, ps=ps, r=r: e.tensor_tensor(out=xt[:r, :], in0=ps[:r, :OC], in1=xt[:r, :], op=ALU.add),
                     ins=[t_ps], outs=[t_x])
                R.dma("sp", self.y_rows(l, tt)[:, c0:c0 + OC], xt[:r, :], ins=[t_x])

    def setup_s5_consts(self, st=None):
        R = self.R
        st = st if st is not None else self.st
        t = Tile()
        self.I2 = self.sb(st, "s5_I2", [64, 32])
        self.SelAll = self.sb(st, "s5_Sel", [128, 64, 128], BF16)
        rm = self.sb(st, "s5_rm", [128, 8])
        ident, I2, SelAll = self.ident, self.I2, self.SelAll
        cst = self.cst
        R.op("pool", lambda e: e.memset(cst[:, 3:4], math.pi / 2), outs=[t])
        self.halfpi = cst[:, 3:4]
        R.barrier()
        R.flush()
        R.op("pool", lambda e: e.tensor_copy(out=I2[0:32, :], in_=ident[0:32, 0:32]), outs=[t])
        R.op("pool", lambda e: e.tensor_copy(out=I2[32:64, :], in_=ident[32:64, 32:64]), outs=[t])
        R.op("pool", lambda e: e.memset(rm[:, :], 1.0), outs=[t])
        R.op("pool", lambda e: e.affine_select(out=rm[:, :], in_=rm[:, :], pattern=[[-16, 8]], compare_op=ALU.is_ge,
                                                fill=0.0, base=0, channel_multiplier=1), outs=[t])
        R.op("pool", lambda e: e.affine_select(out=rm[:, :], in_=rm[:, :], pattern=[[16, 8]], compare_op=ALU.is_ge,
                                                fill=0.0, base=15, channel_multiplier=-1), outs=[t])
        with ExitStack() as tmp:
            SI = self.sb(tmp, "s5_SI", [128, 15, 128])
            R.op("pool", lambda e: e.memset(SI[:, :, :], 0.0), outs=[t])
            for s in range(-7, 8):
                R.op("pool", lambda e, s=s: e.affine_select(out=SI[:, s + 7, :], in_=SI[:, s + 7, :], pattern=[[1, 128]],
                                                             compare_op=ALU.not_equal, fill=1.0, base=-16 * s,
                                                             channel_multiplier=-1), outs=[t])
            for a in range(8):
                for b in range(8):
                    R.op("dve", lambda e, a=a, b=b: e.tensor_scalar(out=SelAll[:, a * 8 + b, :], in0=SI[:, b - a + 7, :],
                                                                    scalar1=rm[:, a:a + 1], scalar2=None, op0=ALU.mult),
                         ins=[t], outs=[t])
            R.barrier()
            R.flush()

    def phase_s5(self, st, l, pairs=range(32)):
        R = self.R
        self.setup_s5_consts(st)
        ident, identb, I2, SelAll = self.ident, self.identb, self.I2, self.SelAll
        NC_ = NT // 8
        PADL = 128
        t = Tile()

        def tt(eng, out, a, b, op):
            R.op(eng, lambda e: e.tensor_tensor(out=out, in0=a, in1=b, op=op), ins=[t], outs=[t])

        def ts(eng, out, a, s1, op0, s2=None, op1=None):
            if op1 is None:
                R.op(eng, lambda e: e.tensor_scalar(out=out, in0=a, scalar1=s1, scalar2=None, op0=op0), ins=[t], outs=[t])
            else:
                R.op(eng, lambda e: e.tensor_scalar(out=out, in0=a, scalar1=s1, scalar2=s2, op0=op0, op1=op1),
                     ins=[t], outs=[t])

        LPre = self.sb(st, "s5_LPre", [128, 9, 32])
        LPim = self.sb(st, "s5_LPim", [128, 9, 32])
        AKre = self.sb(st, "s5_AKre", [128, 8, 32])
        AKim = self.sb(st, "s5_AKim", [128, 8, 32])
        AKni = self.sb(st, "s5_AKni", [128, 8, 32])
        Vre = self.sb(st, "s5_Vre", [128, 32, 8, 16], BF16)
        Vim = self.sb(st, "s5_Vim", [128, 32, 8, 16], BF16)
        KTp = self.sb(st, "s5_KT", [128, 64, 128], BF16)
        WPre = self.sb(st, "s5_WPre", [128, 64, 128], BF16)
        WPim = self.sb(st, "s5_WPim", [128, 64, 128], BF16)
        H0re = self.sb(st, "s5_H0re", [128, 32])
        H0im = self.sb(st, "s5_H0im", [128, 32])
        dcol, t_dcol = self.load_cols(st, "s5_dcol", self.ssm_d[l].rearrange("(k p) -> k p", p=128), 8)

        with ExitStack() as tb:
            gl = self.sb(tb, "s5_gl", [64, 64])
            Lh = self.sb(tb, "s5_Lh", [64, 128])
            are = self.sb(tb, "s5_are", [128, 32])
            aim = self.sb(tb, "s5_aim", [128, 32])
            dtp = self.sb(tb, "s5_dtp", [128, 32])
            adt = self.sb(tb, "s5_adt", [128, 32])
            th = self.sb(tb, "s5_th", [128, 32])
            w = [self.sb(tb, "s5_w%d" % i, [128, 32]) for i in range(8)]
            ldc = self.sb(tb, "s5_ldc", [64, 1])
            R.op("pool", lambda e: e.memset(Lh[:, :], 0.0), ins=[t], outs=[t])

            def to_pair(dst, fill_gl):
                fill_gl()
                R.op("dve", lambda e: e.tensor_copy(out=Lh[0:32, 0:64], in_=gl[0:32, :]), ins=[t], outs=[t])
                R.op("dve", lambda e: e.tensor_copy(out=Lh[32:64, 64:128], in_=gl[32:64, :]), ins=[t], outs=[t])
                ps, t_ps = self.psr.next()
                R.op("pe", lambda e: e.matmul(ps[:, 0:32], Lh[:, :], I2[:, :], start=True, stop=True), ins=[t], outs=[t_ps])
                R.op("dve", lambda e: e.tensor_copy(out=dst, in_=ps[:, 0:32]), ins=[t_ps, t], outs=[t])

            to_pair(are[:, :], lambda: R.dma("sp", gl[:, :], self.a_re[l], ins=[t], outs=[t]))
            to_pair(aim[:, :], lambda: R.dma("sp", gl[:, :], self.a_im[l], ins=[t], outs=[t]))
            to_pair(H0re[:, :], lambda: R.dma("sp", gl[:, :], self.st_re[l], ins=[t], outs=[t]))
            to_pair(H0im[:, :], lambda: R.dma("sp", gl[:, :], self.st_im[l], ins=[t], outs=[t]))

            def fill_dt():
                R.dma("sp", ldc[:, :], self.log_dt[l].unsqueeze(1), ins=[t], outs=[t])
                R.op("act", lambda e: e.activation(out=ldc[:, :], in_=ldc[:, :], func=AF.Exp), ins=[t], outs=[t])
                R.op("dve", lambda e: e.tensor_copy(out=gl[:, :], in_=ldc[:, 0:1].to_broadcast([64, 64])), ins=[t], outs=[t])
            to_pair(dtp[:, :], fill_dt)
            tt("dve", adt[:, :], are[:, :], dtp[:, :], ALU.mult)
            tt("dve", th[:, :], aim[:, :], dtp[:, :], ALU.mult)
            R.op("act", lambda e: e.activation(out=w[0][:, :], in_=adt[:, :], func=AF.Exp, scale=1.0 / 32), ins=[t], outs=[t])
            R.op("act", lambda e: e.activation(out=w[1][:, :], in_=th[:, :], func=AF.Sin, scale=1.0 / 32), ins=[t], outs=[t])
            R.op("act", lambda e: e.activation(out=w[2][:, :], in_=th[:, :], func=AF.Sin, scale=1.0 / 32,
                                               bias=self.halfpi), ins=[t], outs=[t])
            tt("dve", w[3][:, :], w[0][:, :], w[2][:, :], ALU.mult)
            tt("dve", w[4][:, :], w[0][:, :], w[1][:, :], ALU.mult)

            def csq(o_re, o_im, a_re_, a_im_):
                tt("dve", w[6][:, :], a_re_, a_re_, ALU.mult)
                tt("dve", w[7][:, :], a_im_, a_im_, ALU.mult)
                tt("dve", w[5][:, :], a_re_, a_im_, ALU.mult)
                tt("dve", o_re, w[6][:, :], w[7][:, :], ALU.subtract)
                ts("dve", o_im, w[5][:, :], 2.0, ALU.mult)

            def cmul(o_re, o_im, a_re_, a_im_, b_re_, b_im_, sh=None):
                F = [128, 32] if sh is None else sh
                t1 = w[6][:, :] if sh is None else big[0]
                t2 = w[7][:, :] if sh is None else big[1]
                tt("dve", t1, a_re_, b_re_, ALU.mult)
                tt("dve", t2, a_im_, b_im_, ALU.mult)
                tt("dve", o_re, t1, t2, ALU.subtract)
                tt("dve", t1, a_re_, b_im_, ALU.mult)
                tt("dve", t2, a_im_, b_re_, ALU.mult)
                tt("dve", o_im, t1, t2, ALU.add)

            cur = (w[3], w[4])
            alt = (w[0], w[1])
            for i in range(5):
                dst = (LPre[:, 1, :], LPim[:, 1, :]) if i == 4 else (alt[0][:, :], alt[1][:, :])
                csq(dst[0], dst[1], cur[0][:, :], cur[1][:, :])
                cur, alt = alt, cur
            R.op("pool", lambda e: e.memset(LPre[:, 0, :], 1.0), ins=[t], outs=[t])
            R.op("pool", lambda e: e.memset(LPim[:, 0, :], 0.0), ins=[t], outs=[t])
            L_ = lambda k: (LPre[:, k, :], LPim[:, k, :])
            csq(*L_(2), *L_(1))
            cmul(*L_(3), *L_(2), *L_(1))
            csq(*L_(4), *L_(2))
            cmul(*L_(5), *L_(4), *L_(1))
            csq(*L_(6), *L_(3))
            cmul(*L_(7), *L_(4), *L_(3))
            csq(*L_(8), *L_(4))
            R.op("dve", lambda e: e.tensor_copy(out=AKre[:, 0, :], in_=LPre[:, 8, :]), ins=[t], outs=[t])
            R.op("dve", lambda e: e.tensor_copy(out=AKim[:, 0, :], in_=LPim[:, 8, :]), ins=[t], outs=[t])
            for k in range(1, 8):
                csq(AKre[:, k, :], AKim[:, k, :], AKre[:, k - 1, :], AKim[:, k - 1, :])
            ts("dve", AKni[:, :, :], AKim[:, :, :], -1.0, ALU.mult)
            kre = self.sb(tb, "s5_kre", [128, 32])
            kim = self.sb(tb, "s5_kim", [128, 32])
            tt("dve", w[0][:, :], are[:, :], are[:, :], ALU.mult)
            tt("dve", w[1][:, :], aim[:, :], aim[:, :], ALU.mult)
            tt("dve", w[0][:, :], w[0][:, :], w[1][:, :], ALU.add)
            R.op("dve", lambda e: e.reciprocal(out=w[0][:, :], in_=w[0][:, :]), ins=[t], outs=[t])
            ts("dve", w[1][:, :], LPre[:, 1, :], -1.0, ALU.add)
            tt("dve", w[2][:, :], w[1][:, :], are[:, :], ALU.mult)
            tt("dve", w[3][:, :], LPim[:, 1, :], aim[:, :], ALU.mult)
            tt("dve", w[2][:, :], w[2][:, :], w[3][:, :], ALU.add)
            tt("dve", kre[:, :], w[2][:, :], w[0][:, :], ALU.mult)
            tt("dve", w[2][:, :], LPim[:, 1, :], are[:, :], ALU.mult)
            tt("dve", w[3][:, :], w[1][:, :], aim[:, :], ALU.mult)
            tt("dve", w[2][:, :], w[2][:, :], w[3][:, :], ALU.subtract)
            tt("dve", kim[:, :], w[2][:, :], w[0][:, :], ALU.mult)
            Bre = self.sb(tb, "s5_Bre", [128, 32, 16])
            Bim = self.sb(tb, "s5_Bim", [128, 32, 16])
            BBre = self.sb(tb, "s5_BBre", [128, 32, 16])
            BBim = self.sb(tb, "s5_BBim", [128, 32, 16])
            big = [self.sb(tb, "s5_big%d" % i, [128, 32, 16])[:, :, :] for i in range(2)]
            for h in range(2):
                R.dma("sp", Bre[h * 64:(h + 1) * 64, :, :], self.b_re[l][32 * h:32 * h + 32].rearrange("g n p -> n g p"),
                      ins=[t], outs=[t])
                R.dma("sp", Bim[h * 64:(h + 1) * 64, :, :], self.b_im[l][32 * h:32 * h + 32].rearrange("g n p -> n g p"),
                      ins=[t], outs=[t])
            bc = lambda ap: ap.unsqueeze(2).to_broadcast([128, 32, 16])
            cmul(BBre[:, :, :], BBim[:, :, :], bc(kre[:, :]), bc(kim[:, :]), Bre[:, :, :], Bim[:, :, :], sh=1)
            MXre = self.sb(tb, "s5_MXre", [128, 32, 15, 16], BF16)
            MXim = self.sb(tb, "s5_MXim", [128, 32, 15, 16], BF16)
            R.op("pool", lambda e: e.memset(MXre[:, :, :, :], 0.0), ins=[t], outs=[t])
            R.op("pool", lambda e: e.memset(MXim[:, :, :, :], 0.0), ins=[t], outs=[t])
            for d in range(8):
                e_ = 7 - d
                if d == 0:
                    R.op("dve", lambda e: e.tensor_copy(out=MXre[:, :, 7, :], in_=BBre[:, :, :]), ins=[t], outs=[t])
                    R.op("dve", lambda e: e.tensor_copy(out=MXim[:, :, 7, :], in_=BBim[:, :, :]), ins=[t], outs=[t])
                else:
                    cmul(MXre[:, :, e_, :], MXim[:, :, e_, :], bc(LPre[:, d, :]), bc(LPim[:, d, :]), BBre[:, :, :], BBim[:, :, :],
                         sh=1)
            CTre = self.sb(tb, "s5_CTre", [128, 32, 16])
            CTim = self.sb(tb, "s5_CTim", [128, 32, 16])
            CTni = self.sb(tb, "s5_CTni", [128, 32, 16])
            Cpad = self.sb(tb, "s5_Cpad", [128, 8, 128])
            for (src, dstC) in ((self.c_re, CTre), (self.c_im, CTim)):
                R.op("pool", lambda e: e.memset(Cpad[:, :, :], 0.0), ins=[t], outs=[t])
                for rt in range(8):
                    h = rt // 4
                    R.dma("sp", Cpad[:, rt, h * 64:(h + 1) * 64],
                          src[l][rt * 8:(rt + 1) * 8].rearrange("g q n -> (g q) n"), ins=[t], outs=[t])
                ps, t_ps = self.psr.next()

                def mmc(e, ps=ps):
                    for rt4 in range(4):
                        e.matmul(ps[:, rt4 * 128:(rt4 + 1) * 128], Cpad[:, rt4, :], ident[:, :], start=True, stop=False)
                        ins_ = e.matmul(ps[:, rt4 * 128:(rt4 + 1) * 128], Cpad[:, rt4 + 4, :], ident[:, :], start=False, stop=True)
                    return ins_
                R.op("pe", mmc, ins=[t], outs=[t_ps])
                R.op("dve", lambda e, ps=ps, dstC=dstC: e.tensor_copy(out=dstC[:, :, :].rearrange("p g q -> p (g q)"), in_=ps[:, :]),
                     ins=[t_ps, t], outs=[t])
            ts("dve", CTni[:, :, :], CTim[:, :, :], -1.0, ALU.mult)
            CTre_b = self.sb(tb, "s5_CTre_b", [128, 32, 16], BF16)
            CTni_b = self.sb(tb, "s5_CTni_b", [128, 32, 16], BF16)
            R.op("dve", lambda e: e.tensor_copy(out=CTre_b[:, :, :], in_=CTre[:, :, :]), ins=[t], outs=[t])
            R.op("dve", lambda e: e.tensor_copy(out=CTni_b[:, :, :], in_=CTni[:, :, :]), ins=[t], outs=[t])
            for j in range(8):
                lr, li = bc(LPre[:, j + 1, :]), bc(LPim[:, j + 1, :])
                tt("dve", big[0], CTre[:, :, :], lr, ALU.mult)
                tt("dve", big[1], CTim[:, :, :], li, ALU.mult)
                tt("dve", Vre[:, :, j, :], big[0], big[1], ALU.subtract)
                tt("dve", big[0], CTre[:, :, :], li, ALU.mult)
                tt("dve", big[1], CTni[:, :, :], lr, ALU.mult)
                tt("dve", Vim[:, :, j, :], big[1], big[0], ALU.subtract)
            R.op("pool", lambda e: e.memset(WPre[:, :, :], 0.0), ins=[t], outs=[t])
            R.op("pool", lambda e: e.memset(WPim[:, :, :], 0.0), ins=[t], outs=[t])
            t_tabs_done = t.w
            for g in range(G):
                h, gg = g // 32, g % 32
                hp = slice(h * 64, (h + 1) * 64)
                ps, t_ps = self.psr.next()

                def mmk(e, ps=ps, hp=hp, gg=gg):
                    for j in range(8):
                        lo_ = (7 - j)
                        mre = MXre[hp, gg, lo_:lo_ + 8, :].rearrange("p e q -> p (e q)")
                        mim = MXim[hp, gg, lo_:lo_ + 8, :].rearrange("p e q -> p (e q)")
                        e.matmul(ps[:, j * 16:(j + 1) * 16], mre, CTre_b[hp, gg, :], start=True, stop=False)
                        e.matmul(ps[:, j * 16:(j + 1) * 16], mim, CTni_b[hp, gg, :], start=False, stop=True)
                    wre = MXre[hp, gg, 0:8, :].rearrange("p e q -> p (e q)")
                    wim = MXim[hp, gg, 0:8, :].rearrange("p e q -> p (e q)")
                    e.matmul(ps[:, 128:192], wre, identb[hp, hp], start=True, stop=True)
                    return e.matmul(ps[:, 192:256], wim, identb[hp, hp], start=True, stop=True)
                R.op("pe", mmk, outs=[t_ps], extra=[t_tabs_done])
                R.op("act", lambda e, ps=ps, g=g: e.activation(out=KTp[:, g, :], in_=ps[:, 0:128], func=AF.Copy),
                     ins=[t_ps], extra=[t_tabs_done])
                R.op("dve", lambda e, ps=ps, g=g, hp=hp: e.tensor_copy(out=WPre[:, g, hp], in_=ps[:, 128:192]),
                     ins=[t_ps], extra=[t_tabs_done])
                R.op("act", lambda e, ps=ps, g=g, hp=hp: e.activation(out=WPim[:, g, hp], in_=ps[:, 192:256], func=AF.Copy),
                     ins=[t_ps], extra=[t_tabs_done])
            R.barrier()
            R.flush()

        ub = self.sb(st, "s5_ub", [128, 8, NT], BF16)
        t_ub = Tile()
        R.dma("pool", ub[:, :, :], self.uT.rearrange("(k p) n -> p k n", p=128), outs=[t_ub])
        Ub = Rot([self.sb(st, "s5_U%d" % i, [128, NC_], BF16) for i in range(4)])
        Zre = Rot([self.sb(st, "s5_Zre%d" % i, [128, PADL + 256]) for i in range(2)])
        Zim = Rot([self.sb(st, "s5_Zim%d" % i, [128, PADL + 256]) for i in range(2)])
        Xre = Rot([self.sb(st, "s5_Xre%d" % i, [128, PADL + 256]) for i in range(2)])
        Xim = Rot([self.sb(st, "s5_Xim%d" % i, [128, PADL + 256]) for i in range(2)])
        for rot in (Zre, Zim, Xre, Xim):
            for (a, ta) in rot.items:
                R.op("pool", lambda e, a=a: e.memset(a[:, 0:PADL], 0.0), outs=[ta])
        Zs = Rot([self.sb(st, "s5_Zs%d" % i, [128, 8]) for i in range(2)])
        Spb = Rot([self.sb(st, "s5_Sp%d" % i, [128, 2, NC_], BF16) for i in range(2)])
        Ysb = self.sb(st, "s5_Ysb", [128, 16, NC_], BF16)
        t_Ysb = Tile()
        Sfin = self.sb(st, "s5_Sfin", [128, 4, 32])
        t_Sfin = Tile()
        u32 = Rot([self.sb(st, "s5_u32_%d" % i, [128, NT]) for i in range(2)])
        ya32 = Rot([self.sb(st, "s5_ya32_%d" % i, [128, NT]) for i in range(2)])

        def p1(gg):
            zre, t_zre = Zre.next()
            zim, t_zim = Zim.next()
            us = []
            for h in range(2):
                g = 32 * h + gg
                b, g8 = g // 8, g % 8
                U, t_U = Ub.next()
                ps, t_ps = self.psr.next()

                def mmu(e, ps=ps, b=b, g8=g8):
                    for i in range(8):
                        ins_ = e.matmul(ps[:, :NC_], SelAll[:, g8 * 8 + i, :],
                                        ub[:, b, :].rearrange("p (c i) -> p i c", i=8)[:, i, :], start=(i == 0), stop=(i == 7))
                    return ins_
                R.op("pe", mmu, ins=[t_ub], outs=[t_ps])
                R.op("act", lambda e, U=U, ps=ps: e.activation(out=U[:, :], in_=ps[:, :NC_], func=AF.Copy), ins=[t_ps], outs=[t_U])
                us.append((U, t_U, g))
            psr_, t_pr = self.psr.next()
            psi_, t_pi = self.psr.next()

            def mmz(e):
                for h in range(2):
                    U, t_U, g = us[h]
                    e.matmul(psr_[:, :NC_], WPre[:, g, :], U[:, :], start=(h == 0), stop=(h == 1))
                for h in range(2):
                    U, t_U, g = us[h]
                    ins_ = e.matmul(psi_[:, :NC_], WPim[:, g, :], U[:, :], start=(h == 0), stop=(h == 1))
                return ins_
            R.op("pe", mmz, ins=[us[0][1], us[1][1]], outs=[t_pr, t_pi])
            zs, t_zs = Zs.next()
            R.op("dve", lambda e: e.tensor_copy(out=zre[:, PADL:PADL + 256], in_=psr_[:, 0:256]), ins=[t_pr], outs=[t_zre])
            R.op("dve", lambda e: e.tensor_copy(out=zim[:, PADL:PADL + 256], in_=psi_[:, 0:256]), ins=[t_pi], outs=[t_zim])
            R.op("dve", lambda e: e.tensor_copy(out=zs[:, 0:2], in_=psr_[:, 256:258]), ins=[t_pr], outs=[t_zs])
            R.op("dve", lambda e: e.tensor_copy(out=zs[:, 2:4], in_=psi_[:, 256:258]), ins=[t_pi], outs=[t_zs])
            xre, t_xre = Xre.next()
            xim, t_xim = Xim.next()
            c = {"gg": gg, "us": us, "zs": zs, "t_zs": t_zs,
                 "cur": (zre, t_zre, zim, t_zim), "nxt": (xre, t_xre, xim, t_xim)}
            return c

        def scan_step(c, k):
            gg = c["gg"]
            if True:
                d = 1 << k
                cr, t_cr, ci, t_ci = c["cur"]
                nr, t_nr, ni, t_ni = c["nxt"]
                a_r, a_i, a_n = AKre[:, k, gg:gg + 1], AKim[:, k, gg:gg + 1], AKni[:, k, gg:gg + 1]
                lo_, hi_ = PADL, PADL + 256
                R.op("dve", lambda e, nr=nr, cr=cr, a_r=a_r, d=d: e.scalar_tensor_tensor(
                    out=nr[:, lo_:hi_], in0=cr[:, lo_ - d:hi_ - d], scalar=a_r, in1=cr[:, lo_:hi_], op0=ALU.mult, op1=ALU.add),
                    ins=[t_cr], outs=[t_nr])
                R.op("dve", lambda e, nr=nr, ci=ci, a_n=a_n, d=d: e.scalar_tensor_tensor(
                    out=nr[:, lo_:hi_], in0=ci[:, lo_ - d:hi_ - d], scalar=a_n, in1=nr[:, lo_:hi_], op0=ALU.mult, op1=ALU.add),
                    ins=[t_ci], outs=[t_nr])
                R.op("dve", lambda e, ni=ni, ci=ci, a_r=a_r, d=d: e.scalar_tensor_tensor(
                    out=ni[:, lo_:hi_], in0=ci[:, lo_ - d:hi_ - d], scalar=a_r, in1=ci[:, lo_:hi_], op0=ALU.mult, op1=ALU.add),
                    ins=[t_ci], outs=[t_ni])
                R.op("dve", lambda e, ni=ni, cr=cr, a_i=a_i, d=d: e.scalar_tensor_tensor(
                    out=ni[:, lo_:hi_], in0=cr[:, lo_ - d:hi_ - d], scalar=a_i, in1=ni[:, lo_:hi_], op0=ALU.mult, op1=ALU.add),
                    ins=[t_cr], outs=[t_ni])
                c["cur"], c["nxt"] = c["nxt"], c["cur"]

        def p3(c):
            gg, us, zs, t_zs = c["gg"], c["us"], c["zs"], c["t_zs"]
            sre, t_sre, sim, t_sim = c["cur"]
            a_r, a_i, a_n = AKre[:, 0, gg:gg + 1], AKim[:, 0, gg:gg + 1], AKni[:, 0, gg:gg + 1]
            prev_re, prev_im = H0re[:, gg:gg + 1], H0im[:, gg:gg + 1]
            for s_ in range(2):
                o_re, o_im = zs[:, 4 + s_:5 + s_], zs[:, 6 + s_:7 + s_]
                z_re, z_im = zs[:, s_:s_ + 1], zs[:, 2 + s_:3 + s_]
                for (o, p1, c1, p2, c2, z) in ((o_re, prev_re, a_r, prev_im, a_n, z_re), (o_im, prev_im, a_r, prev_re, a_i, z_im)):
                    R.op("dve", lambda e, o=o, p1=p1, c1=c1, z=z: e.scalar_tensor_tensor(
                        out=o, in0=p1, scalar=c1, in1=z, op0=ALU.mult, op1=ALU.add), ins=[t_zs], outs=[t_zs])
                    R.op("dve", lambda e, o=o, p2=p2, c2=c2: e.scalar_tensor_tensor(
                        out=o, in0=p2, scalar=c2, in1=o, op0=ALU.mult, op1=ALU.add), ins=[t_zs], outs=[t_zs])
                prev_re, prev_im = o_re, o_im
            sp, t_sp = Spb.next()
            R.op("act", lambda e: e.activation(out=sp[:, 0, 0:256], in_=sre[:, PADL - 1:PADL + 255], func=AF.Copy),
                 ins=[t_sre], outs=[t_sp])
            R.op("act", lambda e: e.activation(out=sp[:, 1, 0:256], in_=sim[:, PADL - 1:PADL + 255], func=AF.Copy),
                 ins=[t_sim], outs=[t_sp])
            R.op("pool", lambda e: e.tensor_copy(out=sp[:, 0, 256:257], in_=H0re[:, gg:gg + 1]), outs=[t_sp])
            R.op("pool", lambda e: e.tensor_copy(out=sp[:, 1, 256:257], in_=H0im[:, gg:gg + 1]), outs=[t_sp])
            R.op("pool", lambda e: e.tensor_copy(out=sp[:, 0, 257:258], in_=zs[:, 4:5]), ins=[t_zs], outs=[t_sp])
            R.op("pool", lambda e: e.tensor_copy(out=sp[:, 1, 257:258], in_=zs[:, 6:7]), ins=[t_zs], outs=[t_sp])
            R.op("pool", lambda e: e.tensor_copy(out=Sfin[:, 0, gg:gg + 1], in_=sre[:, PADL + 255:PADL + 256]), ins=[t_sre], outs=[t_Sfin])
            R.op("pool", lambda e: e.tensor_copy(out=Sfin[:, 1, gg:gg + 1], in_=sim[:, PADL + 255:PADL + 256]), ins=[t_sim], outs=[t_Sfin])
            R.op("pool", lambda e: e.tensor_copy(out=Sfin[:, 2, gg:gg + 1], in_=zs[:, 5:6]), ins=[t_zs], outs=[t_Sfin])
            R.op("pool", lambda e: e.tensor_copy(out=Sfin[:, 3, gg:gg + 1], in_=zs[:, 7:8]), ins=[t_zs], outs=[t_Sfin])
            for h in range(2):
                U, t_U, g = us[h]
                hp = slice(h * 64, (h + 1) * 64)
                ps, t_ps = self.psr.next()

                def mmy(e, ps=ps, U=U, g=g, hp=hp):
                    e.matmul(ps[:, :NC_], KTp[:, g, :], U[:, :], start=True, stop=False)
                    e.matmul(ps[:, :NC_], Vre[hp, gg, :, :].rearrange("p j q -> p (j q)"), sp[hp, 0, :], start=False, stop=False)
                    return e.matmul(ps[:, :NC_], Vim[hp, gg, :, :].rearrange("p j q -> p (j q)"), sp[hp, 1, :], start=False, stop=True)
                R.op("pe", mmy, ins=[t_U, t_sp], outs=[t_ps])
                slot = (gg % 8) + 8 * h
                R.op("act", lambda e, ps=ps, slot=slot: e.activation(out=Ysb[:, slot, :], in_=ps[:, :NC_], func=AF.Copy),
                     ins=[t_ps], outs=[t_Ysb])

        def finish_blocks(b0):
            for h in range(2):
                b = b0 + 4 * h
                u, t_u = u32.next()
                ya, t_ya = ya32.next()
                R.dma("sp", u[:, :], self.uT[b * 128:(b + 1) * 128, :], outs=[t_u])
                for j in range(8):
                    ps, t_ps = self.psr.next()

                    def mmi(e, ps=ps, j=j, h=h):
                        for g8 in range(8):
                            ins_ = e.matmul(ps[:, :NC_], SelAll[:, j * 8 + g8, :], Ysb[:, 8 * h + g8, :], start=(g8 == 0), stop=(g8 == 7))
                        return ins_
                    R.op("pe", mmi, ins=[t_Ysb], outs=[t_ps])
                    uv = u[:, :].rearrange("p (c i) -> p i c", i=8)[:, j, :]
                    yv = ya[:, :].rearrange("p (c i) -> p i c", i=8)[:, j, :]
                    R.op("dve", lambda e, uv=uv, yv=yv, ps=ps, b=b: e.scalar_tensor_tensor(
                        out=yv, in0=uv, scalar=dcol[:, b:b + 1], in1=ps[:, :NC_], op0=ALU.mult, op1=ALU.add),
                        ins=[t_ps, t_u, t_dcol], outs=[t_ya])
                R.op("act", lambda e, ya=ya: e.activation(out=ya[:, :], in_=ya[:, :], func=AF.Gelu), ins=[t_ya], outs=[t_ya])
                R.dma("sp", self.yaT32[b * 128:(b + 1) * 128, :], ya[:, :], ins=[t_ya])

        pl = list(pairs)
        for i0 in range(0, len(pl), 2):
            grp = pl[i0:i0 + 2]
            cs_ = [p1(gg) for gg in grp]
            for k in range(8):
                for c in cs_:
                    scan_step(c, k)
            for c in cs_:
                p3(c)
            if grp[-1] % 8 == 7:
                finish_blocks(grp[-1] // 8)
        stg = self.sb(st, "s5_stg", [32, 4, 128])
        t_stg = Tile()
        for i, dst in enumerate((self.p_ss_re, self.p_ss_im, self.s_ss_re, self.s_ss_im)):
            ps, t_ps = self.psr.next()
            R.op("pe", lambda e, ps=ps, i=i: e.matmul(ps[:32, 0:128], Sfin[:, i, :], ident[:, :], start=True, stop=True),
                 ins=[t_Sfin], outs=[t_ps])
            R.op("dve", lambda e, ps=ps, i=i: e.tensor_copy(out=stg[:, i, :], in_=ps[:32, 0:128]), ins=[t_ps], outs=[t_stg])
            for h in range(2):
                R.dma("sp", dst[l, 32 * h:32 * h + 32, :], stg[:, i, h * 64:(h + 1) * 64], ins=[t_stg])

    def phase_glu(self, st, l):
        R = self.R
        bgcol, t_bg = self.load_cols(st, "s5_bgcol", self.b_glu[l].rearrange("(k p) -> k p", p=128), 8)
        yab = self.sb(st, "s5_yab", [128, 8, NT], BF16)
        t_yab = Tile()
        R.dma("pool", yab[:, :, :], self.yaT32.rearrange("(k p) n -> p k n", p=128), outs=[t_yab])
        slabs = Rot([self.sb(st, "s5_slab%d" % i, [128, 8, 256], BF16) for i in range(2)])
        yrow = Rot([self.sb(st, "s5_yrow%d" % i, [128, NT]) for i in range(2)])
        zrow = Rot([self.sb(st, "s5_zrow%d" % i, [128, NT]) for i in range(2)])
        gate = Rot([self.sb(st, "s5_gate%d" % i, [128, CH]) for i in range(3)])
        outs_ = Rot([self.sb(st, "s5_out%d" % i, [128, NT], BF16) for i in range(2)])
        for s in range(MW // 256):
            slab, t_slab = self.load_slab(slabs, self.w_glu[l], 8, s * 256, 256)
            for cbl in range(2):
                cb = s * 2 + cbl
                yr, t_yr = yrow.next()
                zr, t_zr = zrow.next()
                R.dma("sp", yr[:, :], self.yaT32[cb * 128:(cb + 1) * 128, :], outs=[t_yr])
                R.dma("sp", zr[:, :], self.szT[0][cb * 128:(cb + 1) * 128, :], outs=[t_zr])
                R.op("pool", lambda e, yr=yr, zr=zr: e.tensor_tensor(out=yr[:, :], in0=yr[:, :], in1=zr[:, :], op=ALU.mult),
                     ins=[t_zr], outs=[t_yr])
                o, t_o = outs_.next()
                for ch in range(NCH):
                    cs = slice(ch * CH, (ch + 1) * CH)
                    ps, t_ps = self.mm_F(slab, t_slab, cbl, yab, 8, ch, [t_yab.w])
                    gt, t_gt = gate.next()
                    R.op("act", lambda e, gt=gt, ps=ps, cb=cb: e.activation(out=gt[:, :], in_=ps[:, :CH], func=AF.Sigmoid,
                                                                          bias=bgcol[:, cb:cb + 1]), ins=[t_ps, t_bg], outs=[t_gt])
                    R.op("dve", lambda e, gt=gt, yr=yr, o=o, cs=cs: e.tensor_tensor(out=o[:, cs], in0=gt[:, :], in1=yr[:, cs], op=ALU.mult),
                         ins=[t_gt, t_yr], outs=[t_o])
                R.dma("sp", self.yT[0][cb * 128:(cb + 1) * 128, :], o[:, :], ins=[t_o])


def build_program():
    P = Prog()
    P.setup_consts()
    P.setup_masks()
    P.setup_band_consts()
    for l in range(DEPTH):
        with ExitStack() as sa:
            actT = P.sb(sa, "actT", [128, KT, NT], BF16)
            P.run_phase(lambda st: P.phase_norm(st, l, actT))
            P.run_phase(lambda st: P.phase_gemm_in(st, l, actT, []))
        P.run_phase(lambda st: P.phase_s5(st, l))
        P.run_phase(lambda st: P.phase_glu(st, l))
        P.run_phase(lambda st: P.phase_band(st, l))
        P.run_phase(lambda st: P.phase_sb(st, l))
        P.run_phase(lambda st: P.phase_merge(st, l))
        with ExitStack() as sa:
            actT = P.sb(sa, "actT", [128, KT, NT], BF16)
            P.run_phase(lambda st: P.phase_out(st, l, actT))
    return P


_PER_CORE_5D = ("cache_sb_k", "cache_sb_v", "cache_band_k", "cache_band_v")
_WEIGHTS = ("norm_g", "w_in", "ssm_a_re", "ssm_a_im", "ssm_log_dt", "ssm_b_re", "ssm_b_im", "ssm_c_re", "ssm_c_im",
            "ssm_d", "w_glu", "b_glu", "q_norm_g", "k_norm_g", "rel_bias", "w_br_a", "w_br_b", "w_br_c", "gate_b", "w_out")


def kernel(**inputs):
    NB = 8
    P = build_program()
    f32 = lambda a: np.ascontiguousarray(np.asarray(a, dtype=np.float32))
    w = {k: f32(inputs[k]) for k in _WEIGHTS}
    in_maps = []
    for b in range(NB):
        m = dict(w)
        m["x_prompt"] = f32(inputs["x_prompt"][b])
        m["x_sample"] = f32(inputs["x_sample"][b])
        for k in _PER_CORE_5D:
            a = np.asarray(inputs[k])
            m[k] = f32(a[:, b].reshape(a.shape[0], a.shape[2], MW))
        m["state_ssm_re"] = f32(np.asarray(inputs["state_ssm_re"])[:, b])
        m["state_ssm_im"] = f32(np.asarray(inputs["state_ssm_im"])[:, b])
        in_maps.append(m)
    res = run_bass_kernel_spmd(P.nc, in_maps, core_ids=list(range(NB)))
    r = res.results
    L = DEPTH
    st = lambda name: np.stack([np.asarray(r[b][name]) for b in range(NB)], axis=0)
    y_p = st("y_prompt")
    y_s = st("y_sample")

    def kv(name, rows):
        a = st(name)
        return np.ascontiguousarray(a.transpose(1, 0, 2, 3)).reshape(L, NB, rows, NH, DH)

    def ss(name):
        return np.ascontiguousarray(st(name).transpose(1, 0, 2, 3))

    return (y_p, y_s,
            kv("p_sb_k", T), kv("p_sb_v", T), kv("p_band_k", BAND), kv("p_band_v", BAND), ss("p_ssm_re"), ss("p_ssm_im"),
            kv("s_sb_k", TS), kv("s_sb_v", TS), kv("s_band_k", TS), kv("s_band_v", TS), ss("s_ssm_re"), ss("s_ssm_im"))
```
